# Optimizing a Trainium2 kernel written in Bass

```python
import math
import jax
import jax.numpy as jnp
from jax import lax
import numpy as np


D_MODEL = 1024
BATCH = 8
SEQ = 4096
DEPTH = 1

D_MIX = D_MODEL
D_FF = 2816
RMS_EPS = 1e-6
GN_EPS = 1e-5

RET_HEADS = 4
RET_DK = D_MODEL // 16
RET_DV = 2 * RET_DK
RET_CHUNK = 128
RET_THETA = 10000.0

NSA_HEADS = 8
NSA_KV_HEADS = 2
NSA_GROUP = NSA_HEADS // NSA_KV_HEADS
NSA_DH = D_MODEL // 16
CMP_LEN = 32
CMP_STRIDE = 16
CMP_HIDDEN = 256
SEL_LEN = 64
SEL_TOP = 16
WINDOW = 512
Q_BLOCK = 64
N_GATES = 3

ROPE_THETA = 500000.0
ROPE_DIM = NSA_DH // 4

NEG_INF = -1e30
FORCE_BONUS = 1e4

IN_WIDTHS = (
    RET_HEADS * RET_DK,
    RET_HEADS * RET_DK,
    RET_HEADS * RET_DV,
    RET_HEADS * RET_DV,
    NSA_HEADS * NSA_DH,
    NSA_KV_HEADS * NSA_DH,
    NSA_KV_HEADS * NSA_DH,
    NSA_KV_HEADS * NSA_DH,
    NSA_KV_HEADS * NSA_DH,
    NSA_KV_HEADS * NSA_DH,
    NSA_KV_HEADS * NSA_DH,
    NSA_HEADS * N_GATES,
)
D_IN = sum(IN_WIDTHS)

kernel_name = 'hybrid_retention_nsa_macaron'


def rms_norm(x, w):
    xf = x.astype(jnp.float32)
    y = xf * lax.rsqrt(jnp.mean(xf * xf, axis=-1, keepdims=True) + RMS_EPS)
    return (y * w.astype(jnp.float32)).astype(x.dtype)


def swiglu(h, w_gate, w_up, w_down):
    return (jax.nn.silu(h @ w_gate) * (h @ w_up)) @ w_down


def partial_rope(x, rot_dim, theta):
    seq = x.shape[1]
    half = rot_dim // 2
    inv_freq = theta ** (-2.0 * jnp.arange(half, dtype=jnp.float32) / rot_dim)
    ang = jnp.arange(seq, dtype=jnp.float32)[:, None] * inv_freq[None, :]
    cos = jnp.cos(ang)[None, :, None, :].astype(x.dtype)
    sin = jnp.sin(ang)[None, :, None, :].astype(x.dtype)
    x1 = x[..., :half]
    x2 = x[..., half:rot_dim]
    return jnp.concatenate([x1 * cos - x2 * sin, x2 * cos + x1 * sin, x[..., rot_dim:]], axis=-1)


def retention_chunkwise(q, k, v):
    bsz, seq, heads, dk = q.shape
    dv = v.shape[-1]
    c = RET_CHUNK
    n_chunks = seq // c
    gamma = 1.0 - 2.0 ** (-5.0 - jnp.arange(heads, dtype=jnp.float32))
    log_g = jnp.log(gamma)
    j = jnp.arange(c, dtype=jnp.float32)
    diff = j[:, None] - j[None, :]
    dmat = jnp.where(diff >= 0, jnp.exp(log_g[:, None, None] * jnp.maximum(diff, 0.0)), 0.0)
    xi = jnp.exp(log_g[:, None] * (j[None, :] + 1.0))
    zeta = jnp.exp(log_g[:, None] * (c - 1.0 - j[None, :]))
    g_chunk = jnp.exp(log_g * c)

    def chunks(a):
        return a.reshape(bsz, n_chunks, c, heads, a.shape[-1]).transpose(1, 0, 3, 2, 4)

    def step(state, inp):
        qi, ki, vi = inp
        inner = jnp.einsum('bhnd,bhmd->bhnm', qi, ki) * dmat
        out = (jnp.einsum('bhnm,bhme->bhne', inner, vi)
               + jnp.einsum('bhnd,bhde->bhne', qi * xi[None, :, :, None], state))
        state = (g_chunk[None, :, None, None] * state
                 + jnp.einsum('bhmd,bhme->bhde', ki * zeta[None, :, :, None], vi))
        return state, out

    state0 = jnp.zeros((bsz, heads, dk, dv), jnp.float32)
    _, out = lax.scan(step, state0, (chunks(q), chunks(k), chunks(v)))
    return out.transpose(1, 0, 3, 2, 4).reshape(bsz, seq, heads, dv)


def group_norm_heads(o, w):
    mu = jnp.mean(o, axis=-1, keepdims=True)
    var = jnp.mean(jnp.square(o - mu), axis=-1, keepdims=True)
    return (o - mu) * lax.rsqrt(var + GN_EPS) * w.astype(jnp.float32).reshape(RET_HEADS, RET_DV)


def compress_blocks(x, pe, w1, b1, w2):
    bsz, seq, hkv, dh = x.shape
    n_cmp = (seq - CMP_LEN) // CMP_STRIDE + 1
    idx = np.arange(n_cmp)[:, None] * CMP_STRIDE + np.arange(CMP_LEN)[None, :]
    blk = x[:, idx] + pe
    blk = blk.transpose(0, 1, 3, 2, 4).reshape(bsz, n_cmp, hkv, CMP_LEN * dh)
    return jax.nn.gelu(blk @ w1 + b1) @ w2


def cmp_to_sel_matrix(n_cmp, n_sel):
    c_start = np.arange(n_cmp) * CMP_STRIDE
    s_start = np.arange(n_sel) * SEL_LEN
    overlap = (np.minimum(c_start[:, None] + CMP_LEN, s_start[None, :] + SEL_LEN)
               - np.maximum(c_start[:, None], s_start[None, :]))
    return jnp.asarray(np.clip(overlap, 0, None) / CMP_LEN, dtype=jnp.float32)


def native_sparse_attention(q, k_c, v_c, k_s, v_s, k_w, v_w, gates,
                            pe_k, w1_k, b1_k, w2_k, pe_v, w1_v, b1_v, w2_v):
    bsz, seq = q.shape[:2]
    dtype = q.dtype
    n_sel = seq // SEL_LEN
    n_top = min(SEL_TOP, n_sel)
    n_qb = seq // Q_BLOCK
    scale = NSA_DH ** -0.5

    k_cmp = compress_blocks(k_c, pe_k, w1_k, b1_k, w2_k)
    v_cmp = compress_blocks(v_c, pe_v, w1_v, b1_v, w2_v)
    n_cmp = k_cmp.shape[1]
    m_cs = cmp_to_sel_matrix(n_cmp, n_sel)
    c_end = jnp.arange(n_cmp) * CMP_STRIDE + CMP_LEN - 1

    q_rot = partial_rope(q, ROPE_DIM, ROPE_THETA)
    k_s = partial_rope(k_s, ROPE_DIM, ROPE_THETA)
    k_w = partial_rope(k_w, ROPE_DIM, ROPE_THETA)

    k_sel = k_s.reshape(bsz, n_sel, SEL_LEN, NSA_KV_HEADS, NSA_DH).transpose(0, 3, 1, 2, 4)
    v_sel = v_s.reshape(bsz, n_sel, SEL_LEN, NSA_KV_HEADS, NSA_DH).transpose(0, 3, 1, 2, 4)
    pad = ((0, 0), (WINDOW, 0), (0, 0), (0, 0))
    k_win = jnp.pad(k_w, pad)
    v_win = jnp.pad(v_w, pad)

    def to_blocks(a):
        return a.reshape((bsz, n_qb, Q_BLOCK) + a.shape[2:]).swapaxes(0, 1)

    q_raw_g = q.reshape(bsz, seq, NSA_KV_HEADS, NSA_GROUP, NSA_DH)
    q_rot_g = q_rot.reshape(bsz, seq, NSA_KV_HEADS, NSA_GROUP, NSA_DH)
    b_idx = jnp.arange(bsz)[:, None, None, None]
    g_idx = jnp.arange(NSA_KV_HEADS)[None, :, None, None]
    blk_ids = jnp.arange(n_sel)
    sel_offsets = jnp.arange(SEL_LEN)

    def block(inp):
        i, qr, qp, g = inp
        t = i * Q_BLOCK + jnp.arange(Q_BLOCK)

        s_c = jnp.einsum('bqgkd,bcgd->bgkqc', qr, k_cmp).astype(jnp.float32) * scale
        mask_c = c_end[None, :] <= t[:, None]
        p_c = jax.nn.softmax(jnp.where(mask_c, s_c, NEG_INF), axis=-1) * mask_c
        o_c = jnp.einsum('bgkqc,bcgd->bqgkd', p_c.astype(dtype), v_cmp)

        imp = jnp.einsum('bgkqc,cs->bgqs', p_c, m_cs)
        cur = t // SEL_LEN
        valid = blk_ids[None, :] <= cur[:, None]
        forced = ((blk_ids[None, :] == 0) | (blk_ids[None, :] == cur[:, None])
                  | (blk_ids[None, :] == cur[:, None] - 1))
        imp = jnp.where(valid, imp + jnp.where(forced, FORCE_BONUS, 0.0), NEG_INF)
        _, sel = lax.top_k(imp, n_top)

        k_g = k_sel[b_idx, g_idx, sel]
        v_g = v_sel[b_idx, g_idx, sel]
        s_s = jnp.einsum('bqgkd,bgqnld->bgkqnl', qp, k_g).astype(jnp.float32) * scale
        kpos = sel[..., None] * SEL_LEN + sel_offsets
        mask_s = (kpos <= t[None, None, :, None, None])[:, :, None]
        s_s = jnp.where(mask_s, s_s, NEG_INF).reshape(bsz, NSA_KV_HEADS, NSA_GROUP, Q_BLOCK, n_top * SEL_LEN)
        p_s = jax.nn.softmax(s_s, axis=-1).reshape(bsz, NSA_KV_HEADS, NSA_GROUP, Q_BLOCK, n_top, SEL_LEN)
        o_s = jnp.einsum('bgkqnl,bgqnld->bqgkd', p_s.astype(dtype), v_g)

        kw_blk = lax.dynamic_slice_in_dim(k_win, i * Q_BLOCK, Q_BLOCK + WINDOW, axis=1)
        vw_blk = lax.dynamic_slice_in_dim(v_win, i * Q_BLOCK, Q_BLOCK + WINDOW, axis=1)
        wpos = i * Q_BLOCK - WINDOW + jnp.arange(Q_BLOCK + WINDOW)
        mask_w = ((wpos[None, :] <= t[:, None]) & (wpos[None, :] > t[:, None] - WINDOW)
                  & (wpos[None, :] >= 0))
        s_w = jnp.einsum('bqgkd,bjgd->bgkqj', qp, kw_blk).astype(jnp.float32) * scale
        p_w = jax.nn.softmax(jnp.where(mask_w, s_w, NEG_INF), axis=-1)
        o_w = jnp.einsum('bgkqj,bjgd->bqgkd', p_w.astype(dtype), vw_blk)

        return g[..., 0:1] * o_c + g[..., 1:2] * o_s + g[..., 2:3] * o_w

    out = lax.map(block, (jnp.arange(n_qb), to_blocks(q_raw_g), to_blocks(q_rot_g), to_blocks(gates)))
    return out.swapaxes(0, 1).reshape(bsz, seq, NSA_HEADS * NSA_DH)


def setup_inputs(seed: int = 0) -> dict:
    key = jax.random.key(seed)
    ks = jax.random.split(key, 24)
    f32 = jnp.float32

    def nrm(k, shape, scale):
        return jax.random.normal(k, shape, f32) * scale

    def gain(k, n):
        return 1.0 + 0.1 * jax.random.normal(k, (DEPTH, n), f32)

    flat = CMP_LEN * NSA_DH
    return {
        'x': nrm(ks[0], (BATCH, SEQ, D_MODEL), 1.0),
        'ffn1_norm_w': gain(ks[1], D_MODEL),
        'ffn1_w_gate': nrm(ks[2], (DEPTH, D_MODEL, D_FF), D_MODEL ** -0.5),
        'ffn1_w_up': nrm(ks[3], (DEPTH, D_MODEL, D_FF), D_MODEL ** -0.5),
        'ffn1_w_down': nrm(ks[4], (DEPTH, D_FF, D_MODEL), D_FF ** -0.5),
        'mix_norm_w': gain(ks[5], D_MODEL),
        'w_in': nrm(ks[6], (DEPTH, D_MODEL, D_IN), D_MODEL ** -0.5),
        'ret_norm_w': gain(ks[7], RET_HEADS * RET_DV),
        'cmp_pe_k': nrm(ks[8], (DEPTH, CMP_LEN, NSA_KV_HEADS, NSA_DH), 0.02),
        'cmp_k_w1': nrm(ks[9], (DEPTH, flat, CMP_HIDDEN), flat ** -0.5),
        'cmp_k_b1': nrm(ks[10], (DEPTH, CMP_HIDDEN), 0.01),
        'cmp_k_w2': nrm(ks[11], (DEPTH, CMP_HIDDEN, NSA_DH), CMP_HIDDEN ** -0.5),
        'cmp_pe_v': nrm(ks[12], (DEPTH, CMP_LEN, NSA_KV_HEADS, NSA_DH), 0.02),
        'cmp_v_w1': nrm(ks[13], (DEPTH, flat, CMP_HIDDEN), flat ** -0.5),
        'cmp_v_b1': nrm(ks[14], (DEPTH, CMP_HIDDEN), 0.01),
        'cmp_v_w2': nrm(ks[15], (DEPTH, CMP_HIDDEN, NSA_DH), CMP_HIDDEN ** -0.5),
        'w_out': nrm(ks[16], (DEPTH, D_MIX, D_MODEL), D_MIX ** -0.5),
        'ffn2_norm_w': gain(ks[17], D_MODEL),
        'ffn2_w_gate': nrm(ks[18], (DEPTH, D_MODEL, D_FF), D_MODEL ** -0.5),
        'ffn2_w_up': nrm(ks[19], (DEPTH, D_MODEL, D_FF), D_MODEL ** -0.5),
        'ffn2_w_down': nrm(ks[20], (DEPTH, D_FF, D_MODEL), D_FF ** -0.5),
        'final_norm_w': 1.0 + 0.1 * jax.random.normal(ks[21], (D_MODEL,), f32),
    }


def reference(x, ffn1_norm_w, ffn1_w_gate, ffn1_w_up, ffn1_w_down, mix_norm_w, w_in,
              ret_norm_w, cmp_pe_k, cmp_k_w1, cmp_k_b1, cmp_k_w2, cmp_pe_v, cmp_v_w1,
              cmp_v_b1, cmp_v_w2, w_out, ffn2_norm_w, ffn2_w_gate, ffn2_w_up, ffn2_w_down,
              final_norm_w):
    bsz, seq, _ = x.shape
    dtype = x.dtype
    offsets = tuple(int(o) for o in np.cumsum(IN_WIDTHS)[:-1])
    h = x
    for layer in range(DEPTH):
        h = h + 0.5 * swiglu(rms_norm(h, ffn1_norm_w[layer]), ffn1_w_gate[layer],
                             ffn1_w_up[layer], ffn1_w_down[layer])

        u = rms_norm(h, mix_norm_w[layer])
        proj = u @ w_in[layer]
        (rq, rk, rv, rg, nq, kc, vc, ksel, vsel, kwin, vwin, ng) = jnp.split(proj, offsets, axis=-1)

        rq = partial_rope(rq.reshape(bsz, seq, RET_HEADS, RET_DK), RET_DK, RET_THETA)
        rk = partial_rope(rk.reshape(bsz, seq, RET_HEADS, RET_DK), RET_DK, RET_THETA) * (RET_DK ** -0.5)
        rv = rv.reshape(bsz, seq, RET_HEADS, RET_DV)
        ret = retention_chunkwise(rq.astype(jnp.float32), rk.astype(jnp.float32), rv.astype(jnp.float32))
        ret = group_norm_heads(ret, ret_norm_w[layer]).astype(dtype).reshape(bsz, seq, RET_HEADS * RET_DV)
        ret = ret * jax.nn.silu(rg)

        kv_shape = (bsz, seq, NSA_KV_HEADS, NSA_DH)
        gates = jax.nn.sigmoid(ng).reshape(bsz, seq, NSA_KV_HEADS, NSA_GROUP, N_GATES)
        nsa = native_sparse_attention(
            nq.reshape(bsz, seq, NSA_HEADS, NSA_DH),
            kc.reshape(kv_shape), vc.reshape(kv_shape),
            ksel.reshape(kv_shape), vsel.reshape(kv_shape),
            kwin.reshape(kv_shape), vwin.reshape(kv_shape), gates,
            cmp_pe_k[layer], cmp_k_w1[layer], cmp_k_b1[layer], cmp_k_w2[layer],
            cmp_pe_v[layer], cmp_v_w1[layer], cmp_v_b1[layer], cmp_v_w2[layer])

        h = h + jnp.concatenate([ret, nsa], axis=-1) @ w_out[layer]

        h = h + 0.5 * swiglu(rms_norm(h, ffn2_norm_w[layer]), ffn2_w_gate[layer],
                             ffn2_w_up[layer], ffn2_w_down[layer])
    return rms_norm(h, final_norm_w)
```

```python
import os
import numpy as np
import ml_dtypes
import concourse.bass as bass
import concourse.mybir as mybir
from concourse.bass_utils import run_bass_kernel_spmd

F32 = mybir.dt.float32
BF16 = mybir.dt.bfloat16
AF = mybir.ActivationFunctionType
ALU = mybir.AluOpType
AX = mybir.AxisListType

S = 4096
D = 1024
DFF = 2816
T = 512
NT = S // T
NSEM = 12
RMS_EPS = 1e-6
GN_EPS = 1e-5


class Dep:
    __slots__ = ("name", "writer", "readers")

    def __init__(self, name):
        self.name = name
        self.writer = None
        self.readers = []


class Prog:
    ENG = ["pe", "act", "dve", "pool", "sp"]

    def __init__(self, nc):
        self.nc = nc
        self.ops = {e: [] for e in self.ENG}
        self.seen = {e: {f: -1 for f in self.ENG} for e in self.ENG}
        self.seen_dma = {e: set() for e in self.ENG}
        self.dma_ring = {q: [None] * NSEM for q in ("sp", "pool", "act")}
        self.dma_count = {q: 0 for q in ("sp", "pool", "act")}
        self.final = []

    def _need(self, rec, eng, ev):
        if ev[0] == "c":
            _, f, i = ev
            if f == "pe" and eng == "pe":
                return
            if self.seen[eng][f] >= i:
                return
            self.seen[eng][f] = i
            rec["waits"].append((f, i))
            self.ops[f][i]["signal"] = True
        else:
            key = ev[1:]
            if key in self.seen_dma[eng]:
                return
            self.seen_dma[eng].add(key)
            rec["dma_waits"].append(key)

    def _deps(self, rec, eng, me, reads, writes):
        for r in reads:
            if r.writer is not None:
                self._need(rec, eng, r.writer)
        for w in writes:
            if w.writer is not None:
                self._need(rec, eng, w.writer)
            for ev in w.readers:
                self._need(rec, eng, ev)
        for r in reads:
            if me[0] == "c":
                r.readers = [ev for ev in r.readers if not (ev[0] == "c" and ev[1] == me[1])]
            r.readers.append(me)
        for w in writes:
            w.writer = me
            w.readers = []

    def op(self, eng, fn, reads=(), writes=()):
        idx = len(self.ops[eng])
        rec = dict(fn=fn, waits=[], dma_waits=[], signal=False, dma=None)
        self.ops[eng].append(rec)
        self._deps(rec, eng, ("c", eng, idx), reads, writes)

    def dma(self, q, out, in_, reads=(), writes=(), final=False):
        k = self.dma_count[q]
        self.dma_count[q] += 1
        slot = k % NSEM
        val = 16 * (k // NSEM + 1)
        rec = dict(fn=(lambda e, o=out, i=in_: e.dma_start(out=o, in_=i)),
                   waits=[], dma_waits=[], signal=False, dma=(q, slot, val))
        prev = self.dma_ring[q][slot]
        if prev is not None:
            self._need(rec, q, ("d",) + prev)
        self.dma_ring[q][slot] = (q, slot, val)
        self.ops[q].append(rec)
        self._deps(rec, q, ("d", q, slot, val), reads, writes)
        if final:
            self.final.append((q, slot, val))

    def emit(self):
        nc = self.nc
        sem = {e: nc.alloc_semaphore("s_" + e) for e in self.ENG}
        dsem = {q: [nc.alloc_semaphore("d_%s%d" % (q, i)) for i in range(NSEM)]
                for q in ("sp", "pool")}
        for e in self.ENG:
            c = 0
            for rec in self.ops[e]:
                if rec["signal"]:
                    c += 1
                rec["semval"] = c
        ops = self.ops
        final = self.final

        def run(name, e):
            for rec in ops[name]:
                best = {}
                for (f, i) in rec["waits"]:
                    v = ops[f][i]["semval"]
                    best[f] = max(best.get(f, 0), v)
                for f, v in best.items():
                    e.wait_ge(sem[f], v)
                for (q, slot, val) in rec["dma_waits"]:
                    e.wait_ge(dsem[q][slot], val)
                ins = rec["fn"](e)
                if rec["signal"]:
                    ins.then_inc(sem[name], 1)
                if rec["dma"] is not None:
                    q, slot, val = rec["dma"]
                    ins.then_inc(dsem[q][slot], 16)
            if name == "sp":
                for (q, slot, val) in final:
                    e.wait_ge(dsem[q][slot], val)
                for q in ("sp", "pool"):
                    for slot in range(NSEM):
                        last = self.dma_ring[q][slot]
                        if last is not None:
                            e.wait_ge(dsem[q][slot], last[2])

        with nc.Block() as block:
            @block.tensor
            def _(e):
                run("pe", e)

            @block.scalar
            def _(e):
                run("act", e)

            @block.vector
            def _(e):
                run("dve", e)

            @block.gpsimd
            def _(e):
                run("pool", e)

            @block.sync
            def _(e):
                run("sp", e)


class Tn:
    def __init__(self, h, F, name):
        self.h = h
        self.F = F
        self.dep = Dep(name)

    def ap(self, dims, off=0, p0=0, np_=128):
        return bass.AP(self.h, p0 * self.F + off, [[self.F, np_]] + [[s, n] for (s, n) in dims])


class Ctx:
    def __init__(self, nc):
        self.nc = nc
        self.P = Prog(nc)
        self.dram = {}
        self.n = 0

    def sb(self, name, F, dtype):
        h = self.nc.alloc_sbuf_tensor(name, [128, F], dtype)
        return Tn(h, F, name)

    def din(self, name, shape, dtype=F32):
        h = self.nc.dram_tensor(name, list(shape), dtype, kind="ExternalInput")
        self.dram[name] = h
        return h


def lay_gateup(w):
    return np.ascontiguousarray(w.reshape(8, 128, 22, 128).transpose(2, 1, 0, 3)).reshape(22, 128, 1024)


def lay_down(w):
    a = w.reshape(2, 11, 128, 2, 512).transpose(0, 3, 2, 1, 4)
    return np.ascontiguousarray(a).reshape(4, 128, 11 * 512)


def lay_cols(w, ncols_piece=512):
    C = w.shape[1]
    npc = C // ncols_piece
    a = w.reshape(8, 128, npc, ncols_piece).transpose(2, 1, 0, 3)
    return np.ascontiguousarray(a).reshape(npc, 128, 8 * ncols_piece)


def lay_normw(w):
    return np.ascontiguousarray(w.reshape(8, 128).T)


def build(ntiles=NT, stage="full"):
    nc = bass.Bass("TRN2", target_bir_lowering=False)
    C = Ctx(nc)
    P = C.P

    x_d = C.din("x", [S, D])
    out_d = nc.dram_tensor("out", [S, D], F32, kind="ExternalOutput")
    nw_d = C.din("nw", [128, 32])
    ident_d = C.din("ident", [128, 128], BF16)
    wg_d = [C.din("wg%d" % i, [22, 128, 1024]) for i in (1, 2)]
    wu_d = [C.din("wu%d" % i, [22, 128, 1024]) for i in (1, 2)]
    wd_d = [C.din("wd%d" % i, [4, 128, 11 * 512]) for i in (1, 2)]
    wfin_d = C.din("wfin", [128, D])
    dbg_d = nc.dram_tensor("dbg", [NT, 128, 8 * T], F32, kind="ExternalOutput") if stage in ("ret", "mix") else None

    xt = C.sb("xt", 4 * D, F32)
    xt.d = [Dep("xt%d" % j) for j in range(4)]
    hnT = C.sb("hnT", 8 * T, BF16)
    xs = [C.sb("xs0", D, BF16)] * 2
    actT = C.sb("actT", 11 * T, BF16)
    wgr = [C.sb("wgr%d" % i, 1024, BF16) for i in range(2)]
    wur = [C.sb("wur%d" % i, 1024, BF16) for i in range(2)]
    wdr = [C.sb("wdr%d" % i, 11 * 512, BF16) for i in range(2)]
    sg = [C.sb("sg%d" % i, T, BF16) for i in range(2)]
    nw = C.sb("nw_sb", 32, F32)
    ident = C.sb("ident_sb", 128, BF16)
    wfin = C.sb("wfin_sb", D, F32)
    st = C.sb("stats", 64, F32)

    ps = []
    for i in range(8):
        h = nc.alloc_psum_tensor("ps%d" % i, [128, 512], F32)
        t_ = Tn(h, 512, "ps%d" % i)
        t_.hb = h.bitcast(BF16)
        ps.append(t_)

    def psb(i, dims, off=0, p0=0, np_=128):
        return bass.AP(ps[i].hb, p0 * 1024 + off, [[1024, np_]] + [[s, n] for (s, n) in dims])

    P.dma("sp", nw.ap([(1, 32)]), nw_d.ap(), writes=[nw.dep])
    P.dma("sp", ident.ap([(1, 128)]), ident_d.ap(), writes=[ident.dep])
    P.dma("sp", wfin.ap([(1, D)]), wfin_d.ap(), writes=[wfin.dep])

    cnt = {"w": 0, "wd": 0, "sg": 0, "psT": 0, "wq": 0}
    scr = {}

    def wload(ring_t, n_el, src_ap, key, i):
        if key not in scr:
            h = nc.dram_tensor("scr_" + key, [128, n_el], BF16, kind="Internal")
            scr[key] = (h, Dep("scr_" + key))
        h, dep = scr[key]
        if i == 0:
            P.dma("pool", ring_t.ap([(1, n_el)]), src_ap, writes=[ring_t.dep])
            if ntiles > 1:
                P.dma("sp", h.ap(), ring_t.ap([(1, n_el)]), reads=[ring_t.dep], writes=[dep])
        else:
            q_ = "sp"
            cnt["wq"] += 1
            P.dma(q_, ring_t.ap([(1, n_el)]), h.ap(), reads=[dep], writes=[ring_t.dep])

    def rms_stats():
        for j in range(4):
            P.op("act", lambda e, j=j: e.activation(out=junk.ap([(1, D)]), in_=xt.ap([(1, D)], off=j * D),
                                                    func=AF.Square, accum_out=st.ap([(1, 1)], off=j)),
                 reads=[xt.d[j]], writes=[junk.dep, st.dep])
            P.op("act", lambda e, j=j: e.activation(out=st.ap([(1, 1)], off=8 + j), in_=st.ap([(1, 1)], off=j),
                                                    func=AF.Sqrt, scale=1.0 / D, bias=RMS_EPS),
                 reads=[st.dep], writes=[st.dep])
            P.op("dve", lambda e, j=j: e.reciprocal(out=st.ap([(1, 1)], off=16 + j), in_=st.ap([(1, 1)], off=8 + j)),
                 reads=[st.dep], writes=[st.dep])

    def norm_T(norm_idx):
        rms_stats()
        for j in range(4):
            x2 = xs[cnt["psT"] % 2]
            P.op("dve", lambda e, j=j, x2=x2: e.tensor_scalar(out=x2.ap([(1, D)]), in0=xt.ap([(1, D)], off=j * D),
                                                       scalar1=st.ap([(1, 1)], off=16 + j), scalar2=None,
                                                       op0=ALU.mult),
                 reads=[xt.d[j], st.dep], writes=[x2.dep])
            b = 6 + (cnt["psT"] % 2)
            cnt["psT"] += 1
            for kc in range(8):
                P.op("pe", lambda e, kc=kc, b=b, x2=x2: e.transpose(out=psb(b, [(1, 128)], off=kc * 128),
                                                             in_=x2.ap([(1, 128)], off=kc * 128),
                                                             identity=ident.ap([(1, 128)])),
                     reads=[x2.dep, ident.dep], writes=[ps[b].dep])
            P.op("dve", lambda e, j=j, b=b: e.tensor_tensor(
                out=hnT.ap([(T, 8), (1, 128)], off=j * 128),
                in0=psb(b, [(128, 8), (1, 128)]),
                in1=nw.ap([(1, 8), (0, 128)], off=norm_idx * 8),
                op=ALU.mult),
                reads=[ps[b].dep, nw.dep], writes=[hnT.dep])

    def ffn(fi, i):
        for dh in range(2):
            for cc in range(11):
                c = dh * 11 + cc
                r = cnt["w"] % 2
                cnt["w"] += 1
                wload(wgr[r], 1024, wg_d[fi].ap()[c], "wg%d_%d" % (fi, c), i)
                wload(wur[r], 1024, wu_d[fi].ap()[c], "wu%d_%d" % (fi, c), i)
                bg, bu = (0, 2) if c % 2 == 0 else (1, 3)
                for kc in range(8):
                    P.op("pe", lambda e, kc=kc, r=r, bg=bg: e.matmul(
                        out=ps[bg].ap([(1, 512)]), lhsT=wgr[r].ap([(1, 128)], off=kc * 128),
                        rhs=hnT.ap([(1, T)], off=kc * T), start=(kc == 0), stop=(kc == 7)),
                        reads=[wgr[r].dep, hnT.dep], writes=[ps[bg].dep])
                for kc in range(8):
                    P.op("pe", lambda e, kc=kc, r=r, bu=bu: e.matmul(
                        out=ps[bu].ap([(1, 512)]), lhsT=wur[r].ap([(1, 128)], off=kc * 128),
                        rhs=hnT.ap([(1, T)], off=kc * T), start=(kc == 0), stop=(kc == 7)),
                        reads=[wur[r].dep, hnT.dep], writes=[ps[bu].dep])
                s = cnt["sg"] % 2
                cnt["sg"] += 1
                P.op("act", lambda e, s=s, bg=bg: e.activation(out=sg[s].ap([(1, T)]), in_=ps[bg].ap([(1, 512)]),
                                                               func=AF.Silu),
                     reads=[ps[bg].dep], writes=[sg[s].dep])
                P.op("dve", lambda e, s=s, bu=bu, cc=cc: e.tensor_tensor(
                    out=actT.ap([(1, T)], off=cc * T), in0=ps[bu].ap([(1, 512)]), in1=sg[s].ap([(1, T)]),
                    op=ALU.mult),
                    reads=[ps[bu].dep, sg[s].dep], writes=[actT.dep])
            rs = []
            for ch in range(2):
                r = cnt["wd"] % 2
                cnt["wd"] += 1
                wload(wdr[r], 11 * 512, wd_d[fi].ap()[dh * 2 + ch], "wd%d_%d" % (fi, dh * 2 + ch), i)
                rs.append(r)
            order = ([(ch, j) for ch in range(2) for j in range(4)] if dh == 0
                     else [(ch, j) for j in range(4) for ch in range(2)])
            for n_, (ch, j) in enumerate(order):
                r = rs[ch]
                b = 4 + (n_ % 2)
                for f in range(11):
                    P.op("pe", lambda e, f=f, j=j, r=r, b=b: e.matmul(
                        out=ps[b].ap([(1, 512)]), lhsT=actT.ap([(1, 128)], off=f * T + j * 128),
                        rhs=wdr[r].ap([(1, 512)], off=f * 512), start=(f == 0), stop=(f == 10)),
                        reads=[actT.dep, wdr[r].dep], writes=[ps[b].dep])
                P.op("dve", lambda e, j=j, ch=ch, b=b: e.scalar_tensor_tensor(
                    out=xt.ap([(1, 512)], off=j * D + ch * 512), in0=ps[b].ap([(1, 512)]), scalar=0.5,
                    in1=xt.ap([(1, 512)], off=j * D + ch * 512), op0=ALU.mult, op1=ALU.add),
                    reads=[ps[b].dep, xt.d[j]], writes=[xt.d[j]])

    def final_norm_store(i):
        rms_stats()
        for j in range(4):
            P.op("dve", lambda e, j=j: e.scalar_tensor_tensor(
                out=xt.ap([(1, D)], off=j * D), in0=xt.ap([(1, D)], off=j * D),
                scalar=st.ap([(1, 1)], off=16 + j), in1=wfin.ap([(1, D)]), op0=ALU.mult, op1=ALU.mult),
                reads=[xt.d[j], st.dep, wfin.dep], writes=[xt.d[j]])
            P.dma("sp", out_d.ap()[i * T + j * 128:i * T + (j + 1) * 128, :], xt.ap([(1, D)], off=j * D),
                  reads=[xt.d[j]], final=True)

    win_d = C.din("win", [17, 128, 8 * 256])
    wng_d = C.din("wng", [128, 8 * 24])
    wout_d = C.din("wout", [4, 128, 8 * 256])
    w1s_d = C.din("w1s", [4, 128, 8 * 256])
    w2k_d = C.din("w2k", [128, 256])
    w2v_d = C.din("w2v", [128, 128])
    b1_d = C.din("b1", [128, 4])
    pe_d = C.din("pe", [128, 64])
    retw_d = C.din("retw", [128, 512])
    ropeR_d = C.din("ropeR", [NT, 128, 2 * T], BF16)
    ropeN_d = C.din("ropeN", [NT, 128, 2 * T], BF16)
    zt_d = C.din("zt", [NT, 128, 1024])
    dmat_d = C.din("dmat", [128, 512])
    xi_d = C.din("xi", [128, 256])
    eall_d = C.din("eall", [128, S], BF16)
    masks_d = C.din("masks", [128, 384], BF16)
    cmask_d = C.din("cmask", [NT, 128, 512], BF16)
    addc_d = C.din("addc", [NT, 128, 256])
    mcs_d = C.din("mcs", [128, 128], BF16)

    wp = [C.sb("wp%d" % i, 8 * 256, BF16) for i in range(3)]
    wng = C.sb("wng_sb", 8 * 24, BF16)
    w2k = C.sb("w2k_sb", 256, BF16)
    w2v = C.sb("w2v_sb", 128, BF16)
    b1 = C.sb("b1_sb", 4, F32)
    pe = C.sb("pe_sb", 64, F32)
    retw = C.sb("retw_sb", 512, F32)
    ropeR = C.sb("ropeR_sb", 2 * T, BF16)
    ropeN = C.sb("ropeN_sb", 2 * T, BF16)
    zt = C.sb("zt_sb", 1024, F32)
    dmat = C.sb("dmat_sb", 512, F32)
    xi = C.sb("xi_sb", 256, F32)
    masks = C.sb("masks_sb", 384, BF16)
    cmask = C.sb("cmask_sb", 512, BF16)
    addc = C.sb("addc_sb", 256, F32)
    mcs = C.sb("mcs_sb", 128, BF16)
    rqT = C.sb("rqT", 2 * T, BF16)
    rkTz = C.sb("rkTz", 4 * T, BF16)
    rqx = C.sb("rqx", 2 * T, BF16)
    qraw = C.sb("qraw", 4 * T, BF16)
    qrope = C.sb("qrope", 4 * T, BF16)
    kvc = C.sb("kvc", 2 * 528, BF16)
    ksTz = C.sb("ksTz", 2 * S, BF16)
    kwTz = C.sb("kwTz", 4 * T, BF16)
    vs_all = C.sb("vs_all", 32 * 130, BF16)
    vw_r = C.sb("vw_r", 8 * 130, BF16)
    rv = C.sb("rv", 4 * 512, BF16)
    g2 = C.sb("g2", 4 * 512, BF16)
    rkzp = C.sb("rkzp", 4 * 512, BF16)
    gates = C.sb("gates", 4 * 24, F32)
    catT = C.sb("catT", 8 * T, BF16)
    tA = C.sb("tA", 512, F32)
    tB = C.sb("tB", 512, F32)
    tC = C.sb("tC", 512, F32)
    junk = Tn(tC.h.bitcast(BF16), 1024, "junk")
    junk.dep = tC.dep
    state = C.sb("state", 512, F32)
    state_bf = C.sb("state_bf", 512, BF16)
    inner_bf = C.sb("inner_bf", 512, BF16)
    retg = C.sb("retg", 512, BF16)
    sm = C.sb("sm", 128, F32)
    blk = C.sb("blk", 4 * 1024, BF16)
    xh = C.sb("xh", 256, F32)
    gt = C.sb("gt", 256, F32)
    hT = C.sb("hT", 256, BF16)
    kcmpT = C.sb("kcmpT", 512, BF16)
    vcmp = C.sb("vcmp", 2 * 130, BF16)
    vstage = C.sb("vstage", 128, BF16)
    pT = [C.sb("pT%d" % i, 512, BF16) for i in range(2)]
    pT.append(xs[0])
    imp = C.sb("imp", 256, F32)
    selb = C.sb("selb", 128, BF16)
    qaug = [C.sb("qaug%d" % i_, 512, BF16) for i_ in range(2)]
    nsa_tok = Tn(gt.h.bitcast(BF16), 512, "nsa_tok")
    nsa_tok.dep = gt.dep
    acc = xh
    sm2 = C.sb("sm2", 64, F32)

    for (t_, d_, n_) in ((wng, wng_d, 192), (w2k, w2k_d, 256), (w2v, w2v_d, 128)):
        P.dma("pool", t_.ap([(1, n_)]), d_.ap(), writes=[t_.dep])
    for (t_, d_, n_) in ((b1, b1_d, 4), (pe, pe_d, 64), (retw, retw_d, 512), (dmat, dmat_d, 512),
                         (xi, xi_d, 256), (masks, masks_d, 384), (mcs, mcs_d, 128)):
        P.dma("sp", t_.ap([(1, n_)]), d_.ap(), writes=[t_.dep])
    P.op("dve", lambda e: e.memset(kcmpT.ap([(1, 512)]), 0.0), writes=[kcmpT.dep])
    for t_, n_ in ((rkTz, 4 * T), (kwTz, 4 * T), (rkzp, 2048), (blk, 4096), (selb, 128)):
        P.op("dve", lambda e, t_=t_, n_=n_: e.memset(t_.ap([(1, n_)]), 0.0), writes=[t_.dep])
    P.op("dve", lambda e: e.memset(vcmp.ap([(1, 260)]), 0.0), writes=[vcmp.dep])
    P.op("dve", lambda e: e.memset(vcmp.ap([(65, 4), (1, 1)], off=64), 1.0), writes=[vcmp.dep])
    P.op("dve", lambda e: e.memset(vs_all.ap([(65, 64), (1, 1)], off=64), 1.0), writes=[vs_all.dep])
    P.op("dve", lambda e: e.memset(vw_r.ap([(65, 16), (1, 1)], off=64), 1.0), writes=[vw_r.dep])
    P.op("dve", lambda e: e.memset(kvc.ap([(1, 2 * 528)]), 0.0), writes=[kvc.dep])
    P.op("dve", lambda e: e.memset(state.ap([(1, 512)]), 0.0), writes=[state.dep])
    P.op("dve", lambda e: e.memset(state_bf.ap([(1, 512)]), 0.0), writes=[state_bf.dep])
    if stage in ("ret", "mix"):
        P.op("dve", lambda e: e.memset(catT.ap([(1, 8 * T)]), 0.0), writes=[catT.dep])
    GAM = [1.0 - 2.0 ** (-5.0 - h) for h in range(4)]
    P.dma("sp", ksTz.ap([(1, S)], off=0, p0=64, np_=64), eall_d.ap()[0:64, :], writes=[ksTz.dep])
    P.dma("sp", ksTz.ap([(1, S)], off=S, p0=0, np_=64), eall_d.ap()[0:64, :], writes=[ksTz.dep])
    cnt["wp"] = 0
    cnt["pT"] = 0
    cnt["sc"] = 0
    cnt["qa"] = 0
    cnt["wq"] = 0

    def mm(out, lhsT, rhs, start, stop, reads, writes):
        P.op("pe", lambda e: e.matmul(out=out, lhsT=lhsT, rhs=rhs, start=start, stop=stop), reads, writes)

    def dve(name, reads, writes, **kw):
        P.op("dve", lambda e: getattr(e, name)(**kw), reads, writes)

    def act(reads, writes, **kw):
        P.op("act", lambda e: e.activation(**kw), reads, writes)

    def load_wp(src_ap, key, i):
        r = wp[cnt["wp"] % 3]
        cnt["wp"] += 1
        wload(r, 2048, src_ap, key, i)
        return r

    def transposes_to(dst_fn, src, nblk, dst_dep):
        b = 6 + (cnt["psT"] % 2)
        cnt["psT"] += 1
        for k in range(nblk):
            P.op("pe", lambda e, k=k: e.transpose(out=psb(b, [(1, 128)], off=k * 128),
                                                  in_=src.ap([(1, 128)], off=k * 128),
                                                  identity=ident.ap([(1, 128)])),
                 reads=[src.dep, ident.dep], writes=[ps[b].dep])
        dst_fn(psb(b, [(128, nblk), (1, 128)]), ps[b].dep)

    def actcopy(reads, writes, out, in_):
        P.op("dve", lambda e: e.tensor_copy(out=out, in_=in_), reads, writes)

    def mixer(i):
        P.dma("sp", ropeR.ap([(1, 2 * T)]), ropeR_d.ap()[i], writes=[ropeR.dep])
        P.dma("sp", ropeN.ap([(1, 2 * T)]), ropeN_d.ap()[i], writes=[ropeN.dep])
        P.dma("sp", zt.ap([(1, 1024)]), zt_d.ap()[i], writes=[zt.dep])
        P.dma("sp", cmask.ap([(1, 512)]), cmask_d.ap()[i], writes=[cmask.dep])
        P.dma("sp", addc.ap([(1, 256)]), addc_d.ap()[i], writes=[addc.dep])
        norm_T(1)
        CUT = int(os.environ.get('K_CUT', '99'))
        if CUT <= 1:
            return
        if i > 0:
            dve("tensor_copy", [kvc.dep], [kvc.dep], out=kvc.ap([(528, 2), (1, 16)]),
                in_=kvc.ap([(528, 2), (1, 16)], off=512))

        def cmp_gen():
            for g in range(2):
                for kv in range(2):
                    dve("tensor_tensor", [kvc.dep, pe.dep], [blk.dep],
                        out=blk.ap([(32, 32), (1, 32)], off=(kv * 2 + g) * 1024, p0=kv * 64, np_=64),
                        in0=kvc.ap([(1, 32), (16, 32)], off=g * 528, p0=kv * 64, np_=64),
                        in1=pe.ap([(1, 32), (0, 32)], off=g * 32, p0=kv * 64, np_=64), op=ALU.add)
            yield
            first = True
            for pcw in range(4):
                r = load_wp(w1s_d.ap()[pcw], "w1s%d" % pcw, i)
                for kv in range(2):
                    for g in range(2):
                        for hc in range(2):
                            idx = (kv * 2 + g) * 2 + hc
                            for l8 in range(8):
                                l = pcw * 8 + l8
                                mm(ps[6].ap([(1, 32)], off=idx * 32),
                                   r.ap([(1, 128)], off=l8 * 256 + hc * 128),
                                   blk.ap([(1, 32)], off=(kv * 2 + g) * 1024 + l * 32),
                                   first, (pcw == 3 and idx == 7 and l8 == 7), [r.dep, blk.dep], [ps[6].dep])
                                first = False
                yield
            for kv in range(2):
                for hc in range(2):
                    o_ = kv * 128 + hc * 32
                    dve("tensor_scalar", [ps[6].dep, b1.dep], [xh.dep], out=xh.ap([(64, 2), (1, 32)], off=o_),
                        in0=ps[6].ap([(64, 2), (1, 32)], off=o_), scalar1=b1.ap([(1, 1)], off=kv * 2 + hc),
                        scalar2=None, op0=ALU.add)
            dve("tensor_tensor", [xh.dep], [gt.dep], out=gt.ap([(1, 256)]), in0=xh.ap([(1, 256)]),
                in1=xh.ap([(1, 256)]), op=ALU.mult)
            dve("tensor_scalar", [gt.dep], [gt.dep], out=gt.ap([(1, 256)]), in0=gt.ap([(1, 256)]),
                scalar1=0.044715, scalar2=1.0, op0=ALU.mult, op1=ALU.add)
            dve("tensor_tensor", [gt.dep, xh.dep], [gt.dep], out=gt.ap([(1, 256)]), in0=gt.ap([(1, 256)]),
                in1=xh.ap([(1, 256)]), op=ALU.mult)
            act([gt.dep], [gt.dep], out=gt.ap([(1, 256)]), in_=gt.ap([(1, 256)]), func=AF.Sigmoid,
                scale=1.5957691216057308)
            dve("tensor_tensor", [gt.dep, xh.dep], [hT.dep], out=hT.ap([(1, 256)]), in0=gt.ap([(1, 256)]),
                in1=xh.ap([(1, 256)]), op=ALU.mult)
            yield
            for g in range(2):
                for hc in range(2):
                    mm(ps[7].ap([(1, 32)], off=g * 32), w2k.ap([(1, 128)], off=hc * 128),
                       hT.ap([(1, 32)], off=g * 64 + hc * 32), (g == 0 and hc == 0), (g == 1 and hc == 1),
                       [w2k.dep, hT.dep], [ps[7].dep])
            for g in range(2):
                for hc in range(2):
                    mm(ps[7].ap([(1, 64)], off=64 + g * 64, np_=32), hT.ap([(1, 32)], off=128 + g * 64 + hc * 32),
                       w2v.ap([(1, 64)], off=hc * 64), (g == 0 and hc == 0), (g == 1 and hc == 1),
                       [w2v.dep, hT.dep], [ps[7].dep])
            for g in range(2):
                actcopy([ps[7].dep], [kcmpT.dep], kcmpT.ap([(1, 32)], off=g * 256 + 32 * i, p0=g * 64, np_=64),
                        ps[7].ap([(1, 32)], off=g * 32, p0=g * 64, np_=64))
            actcopy([ps[7].dep], [vstage.dep], vstage.ap([(1, 128)], np_=32), ps[7].ap([(1, 128)], off=64, np_=32))
            P.dma("sp", vcmp.ap([(65, 2), (1, 64)], off=(i // 4) * 130, p0=32 * (i % 4), np_=32),
                  vstage.ap([(64, 2), (1, 64)], np_=32), reads=[vstage.dep], writes=[vcmp.dep])


        cg = [None]

        def cadv():
            if cg[0] is not None:
                next(cg[0], None)

        def rope_evac(bx, by, tab, dst_ap, dst_dep, split=None):
            dve("tensor_tensor", [ps[bx].dep, tab.dep], [tA.dep], out=tA.ap([(1, 512)]),
                in0=ps[bx].ap([(1, 512)]), in1=tab.ap([(1, 512)]), op=ALU.mult)
            dve("tensor_tensor", [ps[by].dep, tab.dep], [tB.dep], out=tB.ap([(1, 512)]),
                in0=ps[by].ap([(1, 512)]), in1=tab.ap([(1, 512)], off=512), op=ALU.mult)
            if split is None:
                dve("tensor_tensor", [tA.dep, tB.dep], [dst_dep], out=dst_ap,
                    in0=tA.ap([(1, 512)]), in1=tB.ap([(1, 512)]), op=ALU.add)
            else:
                tz, o_lo, o_hi = split
                for p0_, o__ in ((0, o_lo), (64, o_hi)):
                    dve("tensor_tensor", [tA.dep, tB.dep], [tz.dep], out=tz.ap([(1, 512)], off=o__, p0=p0_, np_=64),
                        in0=tA.ap([(1, 512)], p0=p0_, np_=64), in1=tB.ap([(1, 512)], p0=p0_, np_=64), op=ALU.add)

        for pc in (8, 0, 1, 2, 3, 4, 5, 6, 7, 9, 10):
            r = load_wp(win_d.ap()[pc], "win%d" % pc, i)
            bx, by = (0, 1) if pc % 2 == 0 else (2, 3)
            for gi, b in ((0, bx), (1, by)):
                for kc in range(8):
                    mm(ps[b].ap([(1, 512)]), r.ap([(1, 128)], off=kc * 256 + gi * 128),
                       hnT.ap([(1, T)], off=kc * T), kc == 0, kc == 7, [r.dep, hnT.dep], [ps[b].dep])
            if pc < 2:
                rope_evac(bx, by, ropeR, rqT.ap([(1, 512)], off=pc * 512), rqT.dep)
            elif pc < 4:
                rope_evac(bx, by, ropeR, None, None, split=(rkTz, (2 * (pc - 2)) * 512, (2 * (pc - 2) + 1) * 512))
            elif pc < 8:
                k = pc - 4
                actcopy([ps[bx].dep], [qraw.dep], qraw.ap([(1, 512)], off=k * 512), ps[bx].ap([(1, 512)]))
                rope_evac(bx, by, ropeN, qrope.ap([(1, 512)], off=k * 512), qrope.dep)
            elif pc == 8:
                for gi, b in ((0, bx), (1, by)):
                    actcopy([ps[b].dep], [kvc.dep], kvc.ap([(1, 512)], off=gi * 528 + 16), ps[b].ap([(1, 512)]))
                cg[0] = cmp_gen()
            elif pc == 9:
                rope_evac(bx, by, ropeN, None, None, split=(ksTz, i * 512, S + i * 512))
            else:
                rope_evac(bx, by, ropeN, None, None, split=(kwTz, (i % 2) * 512, 1024 + (i % 2) * 512))
            if pc != 8:
                cadv()

        if CUT <= 2:
            return
        for tp in range(6):
            cadv()
            r = load_wp(win_d.ap()[11 + tp], "win%d" % (11 + tp), i)
            for j in range(4):
                b = 4 + (j % 2)
                for kc in range(8):
                    mm(ps[b].ap([(1, 256)]), hnT.ap([(1, 128)], off=kc * T + j * 128),
                       r.ap([(1, 256)], off=kc * 256), kc == 0, kc == 7, [hnT.dep, r.dep], [ps[b].dep])
                if tp < 2:
                    actcopy([ps[b].dep], [rv.dep], rv.ap([(1, 256)], off=j * 512 + tp * 256), ps[b].ap([(1, 256)]))
                elif tp < 4:
                    act([ps[b].dep], [tC.dep], out=tC.ap([(1, 256)]), in_=ps[b].ap([(1, 256)]), func=AF.Silu)
                    dve("tensor_tensor", [tC.dep, retw.dep], [g2.dep],
                        out=g2.ap([(1, 256)], off=j * 512 + (tp - 2) * 256), in0=tC.ap([(1, 256)]),
                        in1=retw.ap([(1, 256)], off=(tp - 2) * 256), op=ALU.mult)
                elif tp == 4:
                    actcopy([ps[b].dep], [vs_all.dep], vs_all.ap([(65, 2), (1, 64)], off=(4 * i + j) * 130),
                            ps[b].ap([(64, 2), (1, 64)]))
                    actcopy([ps[b].dep], [vw_r.dep], vw_r.ap([(65, 2), (1, 64)], off=((i % 2) * 4 + j) * 130),
                            ps[b].ap([(64, 2), (1, 64)], off=128))
                else:
                    x1 = ps[b].ap([(128, 2), (64, 2), (1, 32)])
                    x2 = ps[b].ap([(128, 2), (64, 2), (1, 32)], off=32)
                    cz = zt.ap([(64, 2), (32, 2), (1, 32)], off=j * 128)
                    sz = zt.ap([(64, 2), (32, 2), (1, 32)], off=512 + j * 128)
                    t1 = tA.ap([(64, 2), (32, 2), (1, 32)])
                    t2 = tB.ap([(64, 2), (32, 2), (1, 32)])
                    t3 = tA.ap([(64, 2), (32, 2), (1, 32)], off=128)
                    t4 = tB.ap([(64, 2), (32, 2), (1, 32)], off=128)
                    dve("tensor_tensor", [ps[b].dep, zt.dep], [tA.dep], out=t1, in0=x1, in1=cz, op=ALU.mult)
                    dve("tensor_tensor", [ps[b].dep, zt.dep], [tB.dep], out=t2, in0=x2, in1=sz, op=ALU.mult)
                    dve("tensor_tensor", [ps[b].dep, zt.dep], [tA.dep], out=t3, in0=x2, in1=cz, op=ALU.mult)
                    dve("tensor_tensor", [ps[b].dep, zt.dep], [tB.dep], out=t4, in0=x1, in1=sz, op=ALU.mult)
                    dve("tensor_tensor", [tA.dep, tB.dep], [rkzp.dep],
                        out=rkzp.ap([(256, 2), (192, 2), (1, 32)], off=j * 512), in0=t1, in1=t2, op=ALU.subtract)
                    dve("tensor_tensor", [tA.dep, tB.dep], [rkzp.dep],
                        out=rkzp.ap([(256, 2), (192, 2), (1, 32)], off=j * 512 + 32), in0=t3, in1=t4, op=ALU.add)
        if cg[0] is not None:
            for _ in cg[0]:
                pass
        for j in range(4):
            b = 4 + (j % 2)
            for kc in range(8):
                mm(ps[b].ap([(1, 24)]), hnT.ap([(1, 128)], off=kc * T + j * 128),
                   wng.ap([(1, 24)], off=kc * 24), kc == 0, kc == 7, [hnT.dep, wng.dep], [ps[b].dep])
            act([ps[b].dep], [gates.dep], out=gates.ap([(1, 24)], off=j * 24), in_=ps[b].ap([(1, 24)]),
                func=AF.Sigmoid)

        if CUT <= 3:
            return
        def ret_gen():
            dve("tensor_tensor", [rqT.dep, xi.dep], [rqx.dep], out=rqx.ap([(512, 2), (128, 4), (1, 128)]),
                in0=rqT.ap([(512, 2), (128, 4), (1, 128)]), in1=xi.ap([(128, 2), (0, 4), (1, 128)]), op=ALU.mult)
            for j in range(4):
                for h in range(4):
                    o_ = (h // 2) * 512 + j * 128
                    mm(ps[6].ap([(1, 128)], off=h * 128), rkTz.ap([(1, 128)], off=h * 512 + j * 128),
                       rqT.ap([(1, 128)], off=o_), h == 0, h == 3, [rkTz.dep, rqT.dep], [ps[6].dep])
                dve("tensor_tensor", [ps[6].dep, dmat.dep], [inner_bf.dep], out=inner_bf.ap([(1, 512)]),
                    in0=ps[6].ap([(1, 512)]), in1=dmat.ap([(1, 512)]), op=ALU.mult)
                yield
                for h in range(4):
                    o_ = (h // 2) * 512 + j * 128
                    mm(ps[7].ap([(1, 128)], off=h * 128), inner_bf.ap([(1, 128)], off=h * 128),
                       rv.ap([(1, 128)], off=j * 512 + h * 128), h == 0, False, [inner_bf.dep, rv.dep], [ps[7].dep])
                    mm(ps[7].ap([(1, 128)], off=h * 128), rqx.ap([(1, 128)], off=o_),
                       state_bf.ap([(1, 128)], off=h * 128), False, h == 3,
                       [rqx.dep, state_bf.dep], [ps[7].dep])
                for h in range(4):
                    mm(ps[6].ap([(1, 128)], off=h * 128), rkzp.ap([(1, 128)], off=j * 512 + h * 128),
                       rv.ap([(1, 128)], off=j * 512 + h * 128), h == 0, h == 3, [rkzp.dep, rv.dep], [ps[6].dep])
                for h in range(4):
                    sa = state.ap([(1, 128)], off=h * 128)
                    dve("scalar_tensor_tensor", [state.dep, ps[6].dep], [state.dep], out=sa, in0=sa,
                        scalar=float(GAM[h] ** 128), in1=ps[6].ap([(1, 128)], off=h * 128),
                        op0=ALU.mult, op1=ALU.add)
                dve("tensor_copy", [state.dep], [state_bf.dep], out=state_bf.ap([(1, 512)]), in_=state.ap([(1, 512)]))
                yield
                for h in range(4):
                    dve("bn_stats", [ps[7].dep], [sm2.dep], out=sm2.ap([(1, 6)], off=h * 6),
                        in_=ps[7].ap([(1, 128)], off=h * 128))
                for h in range(4):
                    dve("bn_aggr", [sm2.dep], [sm2.dep], out=sm2.ap([(1, 2)], off=32 + h * 2), in_=sm2.ap([(1, 6)], off=h * 6))
                act([sm2.dep], [sm2.dep], out=sm2.ap([(1, 4)], off=48), in_=sm2.ap([(2, 4)], off=33), func=AF.Sqrt,
                    bias=GN_EPS, scale=1.0)
                dve("reciprocal", [sm2.dep], [sm2.dep], out=sm2.ap([(1, 4)], off=52), in_=sm2.ap([(1, 4)], off=48))
                for h in range(4):
                    dve("tensor_scalar", [ps[7].dep, sm2.dep], [tC.dep], out=tC.ap([(1, 128)], off=h * 128),
                        in0=ps[7].ap([(1, 128)], off=h * 128), scalar1=sm2.ap([(1, 1)], off=32 + 2 * h),
                        scalar2=sm2.ap([(1, 1)], off=52 + h), op0=ALU.subtract, op1=ALU.mult)
                dve("tensor_tensor", [tC.dep, g2.dep], [retg.dep], out=retg.ap([(1, 512)]), in0=tC.ap([(1, 512)]),
                    in1=g2.ap([(1, 512)], off=j * 512), op=ALU.mult)
                yield
                transposes_to(lambda pap, pdep, j=j: actcopy([pdep], [catT.dep],
                                                             catT.ap([(T, 4), (1, 128)], off=j * 128), pap),
                              retg, 4, catT.dep)


        rgen = ret_gen()

        def adv():
            next(rgen, None)

        if stage == "ret":
            for _ in rgen:
                pass
            return

        def group_gen(qs, g, accb, acc_off):
            Q = 4 * i + qs

            def acc_ap():
                return accb.ap([(64, 4), (1, 64)], off=acc_off)

            p0 = g * 64

            def score_tile(pre, kT_ap, kdep, qsrc):
                b = cnt["sc"] % 2
                cnt["sc"] += 1
                first = True
                for (l_ap, r_ap, deps) in pre:
                    mm(ps[b].ap([(1, 512)]), l_ap, r_ap, first, False, deps, [ps[b].dep])
                    first = False
                if qsrc is qraw or qsrc is qrope:
                    q_ap = qsrc.ap([(512, 4), (1, 128)], off=qs * 128)
                else:
                    q_ap = qsrc.ap([(1, 512)])
                mm(ps[b].ap([(1, 512)]), kT_ap, q_ap, first, True, [kdep, qsrc.dep], [ps[b].dep])
                pt = pT[cnt["pT"] % 3]
                cnt["pT"] += 1
                act([ps[b].dep], [pt.dep], out=pt.ap([(1, 512)]), in_=ps[b].ap([(1, 512)]), func=AF.Exp,
                    scale=0.125)
                return pt

            def finish_branch(bank, gi, dst_ap, dst_dep, add_to):
                dve("tensor_scalar", [ps[bank].dep], [sm.dep], out=sm.ap([(1, 4)], off=64),
                    in0=ps[bank].ap([(65, 4)], off=64), scalar1=1e-20, scalar2=None, op0=ALU.max)
                dve("reciprocal", [sm.dep], [sm.dep], out=sm.ap([(1, 4)], off=68), in_=sm.ap([(1, 4)], off=64))
                dve("tensor_tensor", [sm.dep, gates.dep], [sm.dep], out=sm.ap([(1, 4)], off=72),
                    in0=sm.ap([(1, 4)], off=68), in1=gates.ap([(3, 4)], off=qs * 24 + g * 12 + gi), op=ALU.mult)
                if not add_to:
                    dve("tensor_tensor", [ps[bank].dep, sm.dep], [dst_dep], out=dst_ap,
                        in0=ps[bank].ap([(65, 4), (1, 64)]), in1=sm.ap([(1, 4), (0, 64)], off=72), op=ALU.mult)
                else:
                    dve("tensor_tensor", [ps[bank].dep, sm.dep], [tB.dep], out=tB.ap([(64, 4), (1, 64)]),
                        in0=ps[bank].ap([(65, 4), (1, 64)]), in1=sm.ap([(1, 4), (0, 64)], off=72), op=ALU.mult)
                    dve("tensor_tensor", [tB.dep, accb.dep], [dst_dep], out=dst_ap,
                        in0=tB.ap([(64, 4), (1, 64)]), in1=acc_ap(), op=ALU.add)

            def attn_loop(tiles, pre_fn, kT_fn, kdep, qsrc, pv_fn):
                pend = []
                for jk in tiles:
                    pt = score_tile(pre_fn(jk), kT_fn(jk), kdep, qsrc)
                    pend.append((jk, pt))
                    if len(pend) > 2:
                        pv_fn(*pend.pop(0))
                    adv()
                while pend:
                    pv_fn(*pend.pop(0))

            ncts = i // 4 + 1

            def cmp_pre(ct):
                if ct == i // 4:
                    return [(ident.ap([(1, 128)]), cmask.ap([(0, 4), (1, 128)], off=qs * 128),
                             [ident.dep, cmask.dep])]
                return [(ident.ap([(1, 128)]), masks.ap([(0, 4), (1, 128)], off=256), [ident.dep, masks.dep])]

            def cmp_pv(ct, pt):
                for k in range(4):
                    mm(ps[2].ap([(1, 65)], off=k * 65), pt.ap([(1, 128)], off=k * 128),
                       vcmp.ap([(1, 65)], off=ct * 130 + g * 65), (ct == 0 and k == 0),
                       (ct == ncts - 1 and k == 3), [pt.dep, vcmp.dep], [ps[2].dep])
                for k in range(4):
                    mm(ps[3].ap([(1, 64)], off=k * 64), pt.ap([(1, 128)], off=k * 128),
                       mcs.ap([(1, 64)], off=ct * 64), (ct == 0 and k == 0),
                       (ct == ncts - 1 and k == 3), [pt.dep, mcs.dep], [ps[3].dep])

            attn_loop(range(ncts), cmp_pre, lambda ct: kcmpT.ap([(1, 128)], off=g * 256 + ct * 128),
                      kcmpT.dep, qraw, cmp_pv)
            finish_branch(2, 0, acc_ap(), accb.dep, False)
            dve("tensor_tensor", [ps[3].dep, sm.dep], [tA.dep], out=tA.ap([(64, 4), (1, 64)]),
                in0=ps[3].ap([(64, 4), (1, 64)]), in1=sm.ap([(1, 4), (0, 64)], off=68), op=ALU.mult)
            dve("tensor_reduce", [tA.dep], [imp.dep], out=imp.ap([(1, 64)]), in_=tA.ap([(1, 64), (64, 4)]),
                axis=AX.X, op=ALU.add)
            dve("tensor_tensor", [imp.dep, addc.dep], [imp.dep], out=imp.ap([(1, 64)], off=64),
                in0=imp.ap([(1, 64)]), in1=addc.ap([(1, 64)], off=qs * 64), op=ALU.add)
            dve("max", [imp.dep], [sm.dep], out=sm.ap([(1, 8)], off=80), in_=imp.ap([(1, 64)], off=64))
            dve("match_replace", [imp.dep, sm.dep], [imp.dep], out=imp.ap([(1, 64)], off=128),
                in_to_replace=sm.ap([(1, 8)], off=80), in_values=imp.ap([(1, 64)], off=64), imm_value=-3e38)
            dve("max", [imp.dep], [sm.dep], out=sm.ap([(1, 8)], off=88), in_=imp.ap([(1, 64)], off=128))
            dve("tensor_scalar", [imp.dep, sm.dep], [selb.dep], out=selb.ap([(1, 64)], off=64 * (1 - g)),
                in0=imp.ap([(1, 64)], off=64), scalar1=sm.ap([(1, 1)], off=95), scalar2=-30000.0,
                op0=ALU.is_lt, op1=ALU.mult)
            yield
            j0 = max(0, Q - 4)

            def win_pre(jk):
                if jk == Q:
                    return [(ident.ap([(1, 128)]), masks.ap([(0, 4), (1, 128)], off=0), [ident.dep, masks.dep])]
                if jk == Q - 4:
                    return [(ident.ap([(1, 128)]), masks.ap([(0, 4), (1, 128)], off=128),
                             [ident.dep, masks.dep])]
                return []

            def win_pv(jk, pt):
                slot = (jk // 4) % 2
                for k in range(4):
                    mm(ps[5].ap([(1, 65)], off=k * 65), pt.ap([(1, 128)], off=k * 128),
                       vw_r.ap([(1, 65)], off=(slot * 4 + jk % 4) * 130 + g * 65), (jk == j0 and k == 0),
                       (jk == Q and k == 3), [pt.dep, vw_r.dep], [ps[5].dep])

            attn_loop(range(j0, Q + 1), win_pre,
                      lambda jk: kwTz.ap([(1, 128)], off=g * 1024 + ((jk // 4) % 2) * 512 + (jk % 4) * 128),
                      kwTz.dep, qrope, win_pv)
            bT = 6 + (cnt["psT"] % 2)
            cnt["psT"] += 1
            P.op("pe", lambda e, bT=bT: e.transpose(out=psb(bT, [(1, 128)]), in_=selb.ap([(1, 128)]),
                                                    identity=ident.ap([(1, 128)])),
                 reads=[selb.dep, ident.dep], writes=[ps[bT].dep])
            qa = qaug[cnt["qa"] % 2]
            cnt["qa"] += 1
            r0 = 64 * (1 - g)
            dve("tensor_copy", [ps[bT].dep], [qa.dep], out=qa.ap([(128, 4), (1, 128)], p0=r0, np_=64),
                in_=psb(bT, [(0, 4), (1, 128)], p0=r0, np_=64))
            dve("tensor_copy", [qrope.dep], [qa.dep], out=qa.ap([(128, 4), (1, 128)], p0=p0, np_=64),
                in_=qrope.ap([(512, 4), (1, 128)], off=qs * 128, p0=p0, np_=64))

            yield
            def sel_pre(jk):
                pre = []
                if jk == Q:
                    pre.append((ident.ap([(1, 128)]), masks.ap([(0, 4), (1, 128)], off=0),
                                [ident.dep, masks.dep]))
                return pre

            def sel_pv(jk, pt):
                for k in range(4):
                    mm(ps[4].ap([(1, 65)], off=k * 65), pt.ap([(1, 128)], off=k * 128),
                       vs_all.ap([(1, 65)], off=jk * 130 + g * 65), (jk == 0 and k == 0),
                       (jk == Q and k == 3), [pt.dep, vs_all.dep], [ps[4].dep])

            attn_loop(range(Q + 1), sel_pre, lambda jk: ksTz.ap([(1, 128)], off=g * S + jk * 128),
                      ksTz.dep, qa, sel_pv)
            finish_branch(5, 2, acc_ap(), accb.dep, True)
            finish_branch(4, 1, nsa_tok.ap([(64, 4), (1, 64)], off=g * 256), nsa_tok.dep, True)


        groups = [(qs, g) for qs in range(4) for g in range(2)]
        gens = [group_gen(qs, g, (xh, tA)[n_ % 2], (0, 256)[n_ % 2]) for n_, (qs, g) in enumerate(groups)]
        next(gens[0])
        for n_, (qs, g) in enumerate(groups):
            next(gens[n_])
            if n_ + 1 < len(groups):
                next(gens[n_ + 1])
            for _ in gens[n_]:
                pass
            if g == 1:
                transposes_to(lambda pap, pdep, qs=qs: actcopy([pdep], [catT.dep],
                                                               catT.ap([(T, 4), (1, 128)], off=4 * T + qs * 128), pap),
                              nsa_tok, 4, catT.dep)
        for _ in rgen:
            pass

    def outproj(i):
        for pc in range(4):
            r = load_wp(wout_d.ap()[pc], "wout%d" % pc, i)
            for j in range(4):
                b = 4 + (j % 2)
                for kc in range(8):
                    mm(ps[b].ap([(1, 256)]), catT.ap([(1, 128)], off=kc * T + j * 128),
                       r.ap([(1, 256)], off=kc * 256), kc == 0, kc == 7, [catT.dep, r.dep], [ps[b].dep])
                xa = xt.ap([(1, 256)], off=j * D + pc * 256)
                dve("tensor_tensor", [ps[b].dep, xt.d[j]], [xt.d[j]], out=xa, in0=ps[b].ap([(1, 256)]), in1=xa,
                    op=ALU.add)

    for i in range(ntiles):
        for j in range(4):
            P.dma("sp", xt.ap([(1, D)], off=j * D), x_d.ap()[i * T + j * 128:i * T + (j + 1) * 128, :],
                  writes=[xt.d[j]])
        norm_T(0)
        ffn(0, i)
        if stage == "ffn1":
            P.dma("sp", out_d.ap()[i * T:(i + 1) * T, :].rearrange("(j p) d -> p j d", p=128),
                  xt.ap([(D, 4), (1, D)]), reads=xt.d, final=True)
            continue
        mixer(i)
        if stage in ("ret", "mix") and os.environ.get("K_DBG", "1") == "1":
            P.dma("pool", dbg_d.ap()[i], catT.ap([(1, 8 * T)]), reads=[catT.dep], final=True)
        if stage == "ret":
            continue
        outproj(i)
        if stage == "mix":
            P.dma("sp", out_d.ap()[i * T:(i + 1) * T, :].rearrange("(j p) d -> p j d", p=128),
                  xt.ap([(D, 4), (1, D)]), reads=xt.d, final=True)
            continue
        norm_T(2)
        ffn(1, i)
        final_norm_store(i)

    P.emit()
    return nc


IN_W = (256, 256, 512, 512, 512, 128, 128, 128, 128, 128, 128, 24)


def win_cols():
    off = np.concatenate([[0], np.cumsum(IN_W)])
    o_rq, o_rk, o_rv, o_rg, o_nq, o_kc, o_vc, o_ks, o_vs, o_kw, o_vw, o_ng = [int(v) for v in off[:12]]

    def pl(base, h):
        return list(range(base + h * 64, base + h * 64 + 64))

    def ret_sw(base, h):
        b = base + h * 64
        return list(range(b + 32, b + 64)) + list(range(b, b + 32))

    def nsa_sw(base, h):
        b = base + h * 64
        return list(range(b + 8, b + 16)) + list(range(b, b + 8)) + list(range(b + 16, b + 64))

    F = []
    for base in (o_rq, o_rk):
        for pr in ((0, 1), (2, 3)):
            F += pl(base, pr[0]) + pl(base, pr[1])
            F += ret_sw(base, pr[0]) + ret_sw(base, pr[1])
    for k in range(4):
        F += pl(o_nq, k) + pl(o_nq, 4 + k)
        F += nsa_sw(o_nq, k) + nsa_sw(o_nq, 4 + k)
    F += pl(o_kc, 0) + pl(o_vc, 0)
    F += pl(o_kc, 1) + pl(o_vc, 1)
    F += pl(o_ks, 0) + pl(o_ks, 1)
    F += nsa_sw(o_ks, 0) + nsa_sw(o_ks, 1)
    F += pl(o_kw, 0) + pl(o_kw, 1)
    F += nsa_sw(o_kw, 0) + nsa_sw(o_kw, 1)
    Tm = (list(range(o_rv, o_rv + 512)) + list(range(o_rg, o_rg + 512)) + list(range(o_vs, o_vs + 128))
          + list(range(o_vw, o_vw + 128)) + list(range(o_rk, o_rk + 256)))
    ng = list(range(o_ng, o_ng + 24))
    return np.array(F + Tm), np.array(ng)


_CONST_CACHE = {}


def const_tables():
    if _CONST_CACHE:
        return _CONST_CACHE
    bf = ml_dtypes.bfloat16
    c = {}
    t = np.arange(S, dtype=np.float32)
    invR = (np.float32(10000.0) ** (-2.0 * np.arange(32, dtype=np.float32) / 64)).astype(np.float32)
    angR = (t[:, None] * invR[None, :]).astype(np.float32)
    cosR, sinR = np.cos(angR), np.sin(angR)
    d = np.arange(128) % 64
    cR = cosR[:, d % 32].T
    sR = np.where((d < 32)[:, None], -sinR[:, d % 32].T, sinR[:, d % 32].T)
    invN = (np.float32(500000.0) ** (-2.0 * np.arange(8, dtype=np.float32) / 16)).astype(np.float32)
    angN = (t[:, None] * invN[None, :]).astype(np.float32)
    cosN, sinN = np.cos(angN), np.sin(angN)
    cN = np.where((d < 16)[:, None], cosN[:, d % 8].T, 1.0)
    sN = np.where((d < 8)[:, None], -sinN[:, d % 8].T, np.where((d < 16)[:, None], sinN[:, d % 8].T, 0.0))

    def tiles2(a, b):
        a = a.reshape(128, NT, T).transpose(1, 0, 2)
        b = b.reshape(128, NT, T).transpose(1, 0, 2)
        return np.ascontiguousarray(np.concatenate([a, b], axis=2))
    c["ropeR"] = tiles2(cR, sR).astype(bf)
    c["ropeN"] = tiles2(cN, sN).astype(bf)
    gam = 1.0 - 2.0 ** (-5.0 - np.arange(4, dtype=np.float64))
    m = np.arange(S) % 128
    zeta = gam[None, :] ** (127.0 - m[:, None]) / 8.0
    cz = cosR[:, None, :] * zeta[:, :, None]
    sz = sinR[:, None, :] * zeta[:, :, None]

    def ztile(a):
        return a.reshape(NT, 4, 128, 4, 32).transpose(0, 2, 1, 3, 4).reshape(NT, 128, 512)
    c["zt"] = np.ascontiguousarray(np.concatenate([ztile(cz), ztile(sz)], axis=2)).astype(np.float32)
    mm_, nn_ = np.arange(128)[:, None], np.arange(128)[None, :]
    dm = np.stack([np.where(nn_ >= mm_, gam[h] ** np.maximum(nn_ - mm_, 0), 0.0) / 8.0 for h in range(4)], axis=1)
    c["dmat"] = np.ascontiguousarray(dm.reshape(128, 512)).astype(np.float32)
    xi = np.zeros((128, 2, 128))
    for p in range(128):
        for c2 in range(2):
            xi[p, c2] = gam[2 * c2 + p // 64] ** (np.arange(128) + 1.0)
    c["xi"] = xi.reshape(128, 256).astype(np.float32)
    e = np.zeros((128, S), np.float32)
    e[np.arange(S) // 64, np.arange(S)] = 1.0
    c["eall"] = e.astype(bf)
    NEG = -30000.0
    kk, qq = np.arange(128)[:, None], np.arange(128)[None, :]
    causal = np.where(kk <= qq, 0.0, NEG)
    anti = np.where(kk > qq, 0.0, NEG)
    cc0 = np.where(kk == 0, NEG, 0.0) + 0.0 * qq
    c["masks"] = np.concatenate([causal, anti, cc0], axis=1).astype(bf)
    cm = np.zeros((NT, 128, 4, 128), np.float32)
    for i in range(NT):
        ct = i // 4
        cidx = ct * 128 + np.arange(128) - 1
        for qs in range(4):
            tq = 512 * i + 128 * qs + np.arange(128)
            valid = (cidx[:, None] >= 0) & (16 * cidx[:, None] + 31 <= tq[None, :])
            cm[i, :, qs, :] = np.where(valid, 0.0, NEG)
    c["cmask"] = cm.reshape(NT, 128, 512).astype(bf)
    ad = np.zeros((NT, 128, 4, 64), np.float32)
    sblk = np.arange(64)[None, :]
    for i in range(NT):
        for qs in range(4):
            tq = 512 * i + 128 * qs + np.arange(128)
            cur = (tq // 64)[:, None]
            forced = (sblk == 0) | (sblk == cur) | (sblk == cur - 1)
            ad[i, :, qs, :] = np.where(sblk > cur, -1e30, np.where(forced, 1e4, 0.0))
    c["addc"] = ad.reshape(NT, 128, 256)
    n_cmp, n_sel = 255, 64
    c_start = np.arange(n_cmp) * 16
    s_start = np.arange(n_sel) * 64
    ov = (np.minimum(c_start[:, None] + 32, s_start[None, :] + 64) - np.maximum(c_start[:, None], s_start[None, :]))
    mcs = np.clip(ov, 0, None) / 32.0
    mc = np.zeros((256, 64), np.float32)
    mc[1:] = mcs
    c["mcs"] = np.ascontiguousarray(mc.reshape(2, 128, 64).transpose(1, 0, 2).reshape(128, 128)).astype(bf)
    _CONST_CACHE.update(c)
    return c


def host_mixer_inputs(inp, m):
    g = lambda k: np.asarray(inp[k], dtype=np.float32)
    fcols, ngcols = win_cols()
    w_in = g("w_in")[0]
    m["win"] = lay_cols(np.ascontiguousarray(w_in[:, fcols]), 256)
    wn = np.ascontiguousarray(w_in[:, ngcols])
    m["wng"] = np.ascontiguousarray(wn.reshape(8, 128, 24).transpose(1, 0, 2)).reshape(128, 192)
    m["wout"] = lay_cols(g("w_out")[0], 256)
    w1k = g("cmp_k_w1")[0].reshape(32, 64, 256).transpose(1, 0, 2)
    w1v = g("cmp_v_w1")[0].reshape(32, 64, 256).transpose(1, 0, 2)
    w1 = np.concatenate([w1k, w1v], axis=0)
    m["w1s"] = np.ascontiguousarray(w1.reshape(128, 4, 8 * 256).transpose(1, 0, 2))
    w2k = g("cmp_k_w2")[0].reshape(2, 128, 64).transpose(1, 0, 2)
    m["w2k"] = np.ascontiguousarray(np.concatenate([w2k, w2k], axis=2)).reshape(128, 256)
    m["w2v"] = np.ascontiguousarray(g("cmp_v_w2")[0].reshape(2, 128, 64).transpose(1, 0, 2)).reshape(128, 128)
    b1k = g("cmp_k_b1")[0].reshape(2, 128).T
    b1v = g("cmp_v_b1")[0].reshape(2, 128).T
    m["b1"] = np.ascontiguousarray(np.concatenate([b1k, b1v], axis=1))
    pek = g("cmp_pe_k")[0].transpose(2, 1, 0)
    pev = g("cmp_pe_v")[0].transpose(2, 1, 0)
    m["pe"] = np.ascontiguousarray(np.concatenate([pek, pev], axis=0)).reshape(128, 64)
    m["retw"] = np.ascontiguousarray(np.broadcast_to(g("ret_norm_w")[0][None, :], (128, 512)))
    m.update(const_tables())
    return m


def host_inputs(inp, b):
    g = lambda k: np.asarray(inp[k], dtype=np.float32)
    m = {}
    m["x"] = np.ascontiguousarray(g("x")[b])
    nwa = np.stack([lay_normw(g("ffn1_norm_w")[0]), lay_normw(g("mix_norm_w")[0]),
                    lay_normw(g("ffn2_norm_w")[0]), lay_normw(g("final_norm_w"))], axis=1)
    m["nw"] = np.ascontiguousarray(nwa.reshape(128, 32))
    m["ident"] = np.eye(128, dtype=np.float32).astype(ml_dtypes.bfloat16)
    for i, k in ((1, "ffn1"), (2, "ffn2")):
        m["wg%d" % i] = lay_gateup(g(k + "_w_gate")[0])
        m["wu%d" % i] = lay_gateup(g(k + "_w_up")[0])
        m["wd%d" % i] = lay_down(g(k + "_w_down")[0])
    m["wfin"] = np.ascontiguousarray(np.broadcast_to(g("final_norm_w")[None, :], (128, D)))
    host_mixer_inputs(inp, m)
    return m


def kernel(**inputs):
    nc = build()
    in_maps = [host_inputs(inputs, b) for b in range(8)]
    res = run_bass_kernel_spmd(nc, in_maps, core_ids=list(range(8)))
    out = np.stack([np.asarray(r["out"]).reshape(S, D) for r in res.results], axis=0)
    return out.astype(np.float32)
```

```python
import os
import numpy as np
import ml_dtypes
import concourse.bass as bass
import concourse.mybir as mybir
from concourse.bass_utils import run_bass_kernel_spmd

F32 = mybir.dt.float32
BF16 = mybir.dt.bfloat16
AF = mybir.ActivationFunctionType
ALU = mybir.AluOpType
AX = mybir.AxisListType

S = 4096
D = 1024
DFF = 2816
T = 512
NT = S // T
NSEM = 12
RMS_EPS = 1e-6
GN_EPS = 1e-5


class Dep:
    __slots__ = ("name", "writer", "readers")

    def __init__(self, name):
        self.name = name
        self.writer = None
        self.readers = []


class Prog:
    ENG = ["pe", "act", "dve", "pool", "sp"]

    def __init__(self, nc):
        self.nc = nc
        self.ops = {e: [] for e in self.ENG}
        self.seen = {e: {f: -1 for f in self.ENG} for e in self.ENG}
        self.seen_dma = {e: set() for e in self.ENG}
        self.dma_ring = {q: [None] * NSEM for q in ("sp", "pool", "act")}
        self.dma_count = {q: 0 for q in ("sp", "pool", "act")}
        self.final = []

    def _need(self, rec, eng, ev):
        if ev[0] == "c":
            _, f, i = ev
            if f == "pe" and eng == "pe":
                return
            if self.seen[eng][f] >= i:
                return
            self.seen[eng][f] = i
            rec["waits"].append((f, i))
            self.ops[f][i]["signal"] = True
        else:
            key = ev[1:]
            if key in self.seen_dma[eng]:
                return
            self.seen_dma[eng].add(key)
            rec["dma_waits"].append(key)

    def _deps(self, rec, eng, me, reads, writes):
        for r in reads:
            if r.writer is not None:
                self._need(rec, eng, r.writer)
        for w in writes:
            if w.writer is not None:
                self._need(rec, eng, w.writer)
            for ev in w.readers:
                self._need(rec, eng, ev)
        for r in reads:
            if me[0] == "c":
                r.readers = [ev for ev in r.readers if not (ev[0] == "c" and ev[1] == me[1])]
            r.readers.append(me)
        for w in writes:
            w.writer = me
            w.readers = []

    def op(self, eng, fn, reads=(), writes=()):
        idx = len(self.ops[eng])
        rec = dict(fn=fn, waits=[], dma_waits=[], signal=False, dma=None)
        self.ops[eng].append(rec)
        self._deps(rec, eng, ("c", eng, idx), reads, writes)

    def dma(self, q, out, in_, reads=(), writes=(), final=False):
        k = self.dma_count[q]
        self.dma_count[q] += 1
        slot = k % NSEM
        val = 16 * (k // NSEM + 1)
        rec = dict(fn=(lambda e, o=out, i=in_: e.dma_start(out=o, in_=i)),
                   waits=[], dma_waits=[], signal=False, dma=(q, slot, val))
        prev = self.dma_ring[q][slot]
        if prev is not None:
            self._need(rec, q, ("d",) + prev)
        self.dma_ring[q][slot] = (q, slot, val)
        self.ops[q].append(rec)
        self._deps(rec, q, ("d", q, slot, val), reads, writes)
        if final:
            self.final.append((q, slot, val))

    def emit(self):
        nc = self.nc
        sem = {e: nc.alloc_semaphore("s_" + e) for e in self.ENG}
        dsem = {q: [nc.alloc_semaphore("d_%s%d" % (q, i)) for i in range(NSEM)]
                for q in ("sp", "pool")}
        for e in self.ENG:
            c = 0
            for rec in self.ops[e]:
                if rec["signal"]:
                    c += 1
                rec["semval"] = c
        ops = self.ops
        final = self.final

        def run(name, e):
            for rec in ops[name]:
                best = {}
                for (f, i) in rec["waits"]:
                    v = ops[f][i]["semval"]
                    best[f] = max(best.get(f, 0), v)
                for f, v in best.items():
                    e.wait_ge(sem[f], v)
                for (q, slot, val) in rec["dma_waits"]:
                    e.wait_ge(dsem[q][slot], val)
                ins = rec["fn"](e)
                if rec["signal"]:
                    ins.then_inc(sem[name], 1)
                if rec["dma"] is not None:
                    q, slot, val = rec["dma"]
                    ins.then_inc(dsem[q][slot], 16)
            if name == "sp":
                for (q, slot, val) in final:
                    e.wait_ge(dsem[q][slot], val)
                for q in ("sp", "pool"):
                    for slot in range(NSEM):
                        last = self.dma_ring[q][slot]
                        if last is not None:
                            e.wait_ge(dsem[q][slot], last[2])

        with nc.Block() as block:
            @block.tensor
            def _(e):
                run("pe", e)

            @block.scalar
            def _(e):
                run("act", e)

            @block.vector
            def _(e):
                run("dve", e)

            @block.gpsimd
            def _(e):
                run("pool", e)

            @block.sync
            def _(e):
                run("sp", e)


class Tn:
    def __init__(self, h, F, name):
        self.h = h
        self.F = F
        self.dep = Dep(name)

    def ap(self, dims, off=0, p0=0, np_=128):
        return bass.AP(self.h, p0 * self.F + off, [[self.F, np_]] + [[s, n] for (s, n) in dims])


class Ctx:
    def __init__(self, nc):
        self.nc = nc
        self.P = Prog(nc)
        self.dram = {}
        self.n = 0

    def sb(self, name, F, dtype):
        h = self.nc.alloc_sbuf_tensor(name, [128, F], dtype)
        return Tn(h, F, name)

    def din(self, name, shape, dtype=F32):
        h = self.nc.dram_tensor(name, list(shape), dtype, kind="ExternalInput")
        self.dram[name] = h
        return h


def lay_gateup(w):
    return np.ascontiguousarray(w.reshape(8, 128, 22, 128).transpose(2, 1, 0, 3)).reshape(22, 128, 1024)


def lay_down(w):
    a = w.reshape(2, 11, 128, 2, 512).transpose(0, 3, 2, 1, 4)
    return np.ascontiguousarray(a).reshape(4, 128, 11 * 512)


def lay_cols(w, ncols_piece=512):
    C = w.shape[1]
    npc = C // ncols_piece
    a = w.reshape(8, 128, npc, ncols_piece).transpose(2, 1, 0, 3)
    return np.ascontiguousarray(a).reshape(npc, 128, 8 * ncols_piece)


def lay_normw(w):
    return np.ascontiguousarray(w.reshape(8, 128).T)


def build(ntiles=NT, stage="full"):
    nc = bass.Bass("TRN2", target_bir_lowering=False)
    C = Ctx(nc)
    P = C.P

    x_d = C.din("x", [S, D])
    out_d = nc.dram_tensor("out", [S, D], F32, kind="ExternalOutput")
    nw_d = C.din("nw", [128, 32])
    ident_d = C.din("ident", [128, 128], BF16)
    wg_d = [C.din("wg%d" % i, [22, 128, 1024]) for i in (1, 2)]
    wu_d = [C.din("wu%d" % i, [22, 128, 1024]) for i in (1, 2)]
    wd_d = [C.din("wd%d" % i, [4, 128, 11 * 512]) for i in (1, 2)]
    wfin_d = C.din("wfin", [128, D])
    dbg_d = nc.dram_tensor("dbg", [NT, 128, 8 * T], F32, kind="ExternalOutput") if stage in ("ret", "mix") else None

    xt = C.sb("xt", 4 * D, F32)
    xt.d = [Dep("xt%d" % j) for j in range(4)]
    hnT = C.sb("hnT", 8 * T, BF16)
    xs = [C.sb("xs0", D, BF16)] * 2
    actT = C.sb("actT", 11 * T, BF16)
    wgr = [C.sb("wgr%d" % i, 1024, BF16) for i in range(2)]
    wur = [C.sb("wur%d" % i, 1024, BF16) for i in range(2)]
    wdr = [C.sb("wdr%d" % i, 11 * 512, BF16) for i in range(2)]
    sg = [C.sb("sg%d" % i, T, BF16) for i in range(2)]
    nw = C.sb("nw_sb", 32, F32)
    ident = C.sb("ident_sb", 128, BF16)
    wfin = C.sb("wfin_sb", D, F32)
    st = C.sb("stats", 64, F32)

    ps = []
    for i in range(8):
        h = nc.alloc_psum_tensor("ps%d" % i, [128, 512], F32)
        t_ = Tn(h, 512, "ps%d" % i)
        t_.hb = h.bitcast(BF16)
        ps.append(t_)

    def psb(i, dims, off=0, p0=0, np_=128):
        return bass.AP(ps[i].hb, p0 * 1024 + off, [[1024, np_]] + [[s, n] for (s, n) in dims])

    P.dma("sp", nw.ap([(1, 32)]), nw_d.ap(), writes=[nw.dep])
    P.dma("sp", ident.ap([(1, 128)]), ident_d.ap(), writes=[ident.dep])
    P.dma("sp", wfin.ap([(1, D)]), wfin_d.ap(), writes=[wfin.dep])

    cnt = {"w": 0, "wd": 0, "sg": 0, "psT": 0, "wq": 0}
    scr = {}

    def wload(ring_t, n_el, src_ap, key, i):
        if key not in scr:
            h = nc.dram_tensor("scr_" + key, [128, n_el], BF16, kind="Internal")
            scr[key] = (h, Dep("scr_" + key))
        h, dep = scr[key]
        if i == 0:
            P.dma("pool", ring_t.ap([(1, n_el)]), src_ap, writes=[ring_t.dep])
            if ntiles > 1:
                P.dma("sp", h.ap(), ring_t.ap([(1, n_el)]), reads=[ring_t.dep], writes=[dep])
        else:
            q_ = "sp"
            cnt["wq"] += 1
            P.dma(q_, ring_t.ap([(1, n_el)]), h.ap(), reads=[dep], writes=[ring_t.dep])

    st.d = [Dep("st%d" % j) for j in range(4)]

    def rms_stats(j):
        P.op("act", lambda e, j=j: e.activation(out=junk.ap([(1, D)]), in_=xt.ap([(1, D)], off=j * D),
                                                func=AF.Square, accum_out=st.ap([(1, 1)], off=j)),
             reads=[xt.d[j]], writes=[junk.dep, st.d[j]])
        P.op("act", lambda e, j=j: e.activation(out=st.ap([(1, 1)], off=8 + j), in_=st.ap([(1, 1)], off=j),
                                                func=AF.Sqrt, scale=1.0 / D, bias=RMS_EPS),
             reads=[st.d[j]], writes=[st.d[j]])
        P.op("dve", lambda e, j=j: e.reciprocal(out=st.ap([(1, 1)], off=16 + j), in_=st.ap([(1, 1)], off=8 + j)),
             reads=[st.d[j]], writes=[st.d[j]])

    def norm_T(norm_idx):
        for j in range(4):
            rms_stats(j)
            x2 = xs[cnt["psT"] % 2]
            P.op("dve", lambda e, j=j, x2=x2: e.tensor_scalar(out=x2.ap([(1, D)]), in0=xt.ap([(1, D)], off=j * D),
                                                       scalar1=st.ap([(1, 1)], off=16 + j), scalar2=None,
                                                       op0=ALU.mult),
                 reads=[xt.d[j], st.d[j]], writes=[x2.dep])
            b = 6 + (cnt["psT"] % 2)
            cnt["psT"] += 1
            for kc in range(8):
                P.op("pe", lambda e, kc=kc, b=b, x2=x2: e.transpose(out=psb(b, [(1, 128)], off=kc * 128),
                                                             in_=x2.ap([(1, 128)], off=kc * 128),
                                                             identity=ident.ap([(1, 128)])),
                     reads=[x2.dep, ident.dep], writes=[ps[b].dep])
            P.op("dve", lambda e, j=j, b=b: e.tensor_tensor(
                out=hnT.ap([(T, 8), (1, 128)], off=j * 128),
                in0=psb(b, [(128, 8), (1, 128)]),
                in1=nw.ap([(1, 8), (0, 128)], off=norm_idx * 8),
                op=ALU.mult),
                reads=[ps[b].dep, nw.dep], writes=[hnT.dep])

    def ffn(fi, i):
        for dh in range(2):
            for cc in range(11):
                c = dh * 11 + cc
                r = cnt["w"] % 2
                cnt["w"] += 1
                wload(wgr[r], 1024, wg_d[fi].ap()[c], "wg%d_%d" % (fi, c), i)
                wload(wur[r], 1024, wu_d[fi].ap()[c], "wu%d_%d" % (fi, c), i)
                bg, bu = (0, 2) if c % 2 == 0 else (1, 3)
                for kc in range(8):
                    P.op("pe", lambda e, kc=kc, r=r, bg=bg: e.matmul(
                        out=ps[bg].ap([(1, 512)]), lhsT=wgr[r].ap([(1, 128)], off=kc * 128),
                        rhs=hnT.ap([(1, T)], off=kc * T), start=(kc == 0), stop=(kc == 7)),
                        reads=[wgr[r].dep, hnT.dep], writes=[ps[bg].dep])
                for kc in range(8):
                    P.op("pe", lambda e, kc=kc, r=r, bu=bu: e.matmul(
                        out=ps[bu].ap([(1, 512)]), lhsT=wur[r].ap([(1, 128)], off=kc * 128),
                        rhs=hnT.ap([(1, T)], off=kc * T), start=(kc == 0), stop=(kc == 7)),
                        reads=[wur[r].dep, hnT.dep], writes=[ps[bu].dep])
                s = cnt["sg"] % 2
                cnt["sg"] += 1
                P.op("act", lambda e, s=s, bg=bg: e.activation(out=sg[s].ap([(1, T)]), in_=ps[bg].ap([(1, 512)]),
                                                               func=AF.Silu),
                     reads=[ps[bg].dep], writes=[sg[s].dep])
                P.op("dve", lambda e, s=s, bu=bu, cc=cc: e.tensor_tensor(
                    out=actT.ap([(1, T)], off=cc * T), in0=ps[bu].ap([(1, 512)]), in1=sg[s].ap([(1, T)]),
                    op=ALU.mult),
                    reads=[ps[bu].dep, sg[s].dep], writes=[actT.dep])
            rs = []
            for ch in range(2):
                r = cnt["wd"] % 2
                cnt["wd"] += 1
                wload(wdr[r], 11 * 512, wd_d[fi].ap()[dh * 2 + ch], "wd%d_%d" % (fi, dh * 2 + ch), i)
                rs.append(r)
            order = ([(ch, j) for ch in range(2) for j in range(4)] if dh == 0
                     else [(ch, j) for j in range(4) for ch in range(2)])
            for n_, (ch, j) in enumerate(order):
                r = rs[ch]
                b = 4 + (n_ % 2)
                for f in range(11):
                    P.op("pe", lambda e, f=f, j=j, r=r, b=b: e.matmul(
                        out=ps[b].ap([(1, 512)]), lhsT=actT.ap([(1, 128)], off=f * T + j * 128),
                        rhs=wdr[r].ap([(1, 512)], off=f * 512), start=(f == 0), stop=(f == 10)),
                        reads=[actT.dep, wdr[r].dep], writes=[ps[b].dep])
                P.op("dve", lambda e, j=j, ch=ch, b=b: e.scalar_tensor_tensor(
                    out=xt.ap([(1, 512)], off=j * D + ch * 512), in0=ps[b].ap([(1, 512)]), scalar=0.5,
                    in1=xt.ap([(1, 512)], off=j * D + ch * 512), op0=ALU.mult, op1=ALU.add),
                    reads=[ps[b].dep, xt.d[j]], writes=[xt.d[j]])

    def final_norm_store(i):
        for j in range(4):
            rms_stats(j)
            P.op("dve", lambda e, j=j: e.scalar_tensor_tensor(
                out=xt.ap([(1, D)], off=j * D), in0=xt.ap([(1, D)], off=j * D),
                scalar=st.ap([(1, 1)], off=16 + j), in1=wfin.ap([(1, D)]), op0=ALU.mult, op1=ALU.mult),
                reads=[xt.d[j], st.d[j], wfin.dep], writes=[xt.d[j]])
            P.dma("sp", out_d.ap()[i * T + j * 128:i * T + (j + 1) * 128, :], xt.ap([(1, D)], off=j * D),
                  reads=[xt.d[j]], final=True)

    win_d = C.din("win", [17, 128, 8 * 256])
    wng_d = C.din("wng", [128, 8 * 24])
    wout_d = C.din("wout", [4, 128, 8 * 256])
    w1s_d = C.din("w1s", [4, 128, 8 * 256])
    w2k_d = C.din("w2k", [128, 256])
    w2v_d = C.din("w2v", [128, 128])
    b1_d = C.din("b1", [128, 4])
    pe_d = C.din("pe", [128, 64])
    retw_d = C.din("retw", [128, 512])
    ropeR_d = C.din("ropeR", [NT, 128, 2 * T], BF16)
    ropeN_d = C.din("ropeN", [NT, 128, 2 * T], BF16)
    zt_d = C.din("zt", [NT, 128, 1024])
    dmat_d = C.din("dmat", [128, 512])
    xi_d = C.din("xi", [128, 256])
    eall_d = C.din("eall", [128, S], BF16)
    masks_d = C.din("masks", [128, 384], BF16)
    cmask_d = C.din("cmask", [NT, 128, 512], BF16)
    addc_d = C.din("addc", [NT, 128, 256])
    mcs_d = C.din("mcs", [128, 128], BF16)

    wp = [C.sb("wp%d" % i, 8 * 256, BF16) for i in range(3)]
    wng = C.sb("wng_sb", 8 * 24, BF16)
    w2k = C.sb("w2k_sb", 256, BF16)
    w2v = C.sb("w2v_sb", 128, BF16)
    b1 = C.sb("b1_sb", 4, F32)
    pe = C.sb("pe_sb", 64, F32)
    retw = C.sb("retw_sb", 512, F32)
    ropeR = C.sb("ropeR_sb", 2 * T, BF16)
    ropeN = C.sb("ropeN_sb", 2 * T, BF16)
    zt = C.sb("zt_sb", 1024, F32)
    dmat = C.sb("dmat_sb", 512, F32)
    xi = C.sb("xi_sb", 256, F32)
    masks = C.sb("masks_sb", 384, BF16)
    cmask = C.sb("cmask_sb", 512, BF16)
    addc = C.sb("addc_sb", 256, F32)
    mcs = C.sb("mcs_sb", 128, BF16)
    rqT = C.sb("rqT", 2 * T, BF16)
    rkTz = C.sb("rkTz", 4 * T, BF16)
    rqx = C.sb("rqx", 2 * T, BF16)
    qraw = C.sb("qraw", 4 * T, BF16)
    qrope = C.sb("qrope", 4 * T, BF16)
    kvc = C.sb("kvc", 2 * 528, BF16)
    ksTz = C.sb("ksTz", 2 * S, BF16)
    kwTz = C.sb("kwTz", 4 * T, BF16)
    vs_all = C.sb("vs_all", 32 * 130, BF16)
    vw_r = C.sb("vw_r", 8 * 130, BF16)
    rv = C.sb("rv", 4 * 512, BF16)
    g2 = C.sb("g2", 4 * 512, BF16)
    rkzp = C.sb("rkzp", 4 * 512, BF16)
    gates = C.sb("gates", 4 * 24, F32)
    catT = C.sb("catT", 8 * T, BF16)
    tA = C.sb("tA", 512, F32)
    tB = C.sb("tB", 512, F32)
    tC = C.sb("tC", 512, F32)
    junk = Tn(tC.h.bitcast(BF16), 1024, "junk")
    junk.dep = tC.dep
    state = C.sb("state", 512, F32)
    state_bf = C.sb("state_bf", 512, BF16)
    inner_bf = C.sb("inner_bf", 512, BF16)
    retg = C.sb("retg", 512, BF16)
    sm = C.sb("sm", 128, F32)
    blk = C.sb("blk", 4 * 1024, BF16)
    xh = C.sb("xh", 256, F32)
    gt = C.sb("gt", 256, F32)
    hT = C.sb("hT", 256, BF16)
    kcmpT = C.sb("kcmpT", 512, BF16)
    vcmp = C.sb("vcmp", 2 * 130, BF16)
    vstage = C.sb("vstage", 128, BF16)
    pT = [C.sb("pT%d" % i, 512, BF16) for i in range(2)]
    pT.append(xs[0])
    imp = C.sb("imp", 256, F32)
    selb = C.sb("selb", 128, BF16)
    qaug = [C.sb("qaug%d" % i_, 512, BF16) for i_ in range(2)]
    nsa_tok = Tn(gt.h.bitcast(BF16), 512, "nsa_tok")
    nsa_tok.dep = gt.dep
    acc = xh
    sm2 = C.sb("sm2", 64, F32)

    for (t_, d_, n_) in ((wng, wng_d, 192), (w2k, w2k_d, 256), (w2v, w2v_d, 128)):
        P.dma("pool", t_.ap([(1, n_)]), d_.ap(), writes=[t_.dep])
    for (t_, d_, n_) in ((b1, b1_d, 4), (pe, pe_d, 64), (retw, retw_d, 512), (dmat, dmat_d, 512),
                         (xi, xi_d, 256), (masks, masks_d, 384), (mcs, mcs_d, 128)):
        P.dma("sp", t_.ap([(1, n_)]), d_.ap(), writes=[t_.dep])
    P.op("dve", lambda e: e.memset(kcmpT.ap([(1, 512)]), 0.0), writes=[kcmpT.dep])
    for t_, n_ in ((rkTz, 4 * T), (kwTz, 4 * T), (rkzp, 2048), (blk, 4096), (selb, 128)):
        P.op("dve", lambda e, t_=t_, n_=n_: e.memset(t_.ap([(1, n_)]), 0.0), writes=[t_.dep])
    P.op("dve", lambda e: e.memset(vcmp.ap([(1, 260)]), 0.0), writes=[vcmp.dep])
    P.op("dve", lambda e: e.memset(vcmp.ap([(65, 4), (1, 1)], off=64), 1.0), writes=[vcmp.dep])
    P.op("dve", lambda e: e.memset(vs_all.ap([(65, 64), (1, 1)], off=64), 1.0), writes=[vs_all.dep])
    P.op("dve", lambda e: e.memset(vw_r.ap([(65, 16), (1, 1)], off=64), 1.0), writes=[vw_r.dep])
    P.op("dve", lambda e: e.memset(kvc.ap([(1, 2 * 528)]), 0.0), writes=[kvc.dep])
    P.op("dve", lambda e: e.memset(state.ap([(1, 512)]), 0.0), writes=[state.dep])
    P.op("dve", lambda e: e.memset(state_bf.ap([(1, 512)]), 0.0), writes=[state_bf.dep])
    if stage in ("ret", "mix"):
        P.op("dve", lambda e: e.memset(catT.ap([(1, 8 * T)]), 0.0), writes=[catT.dep])
    GAM = [1.0 - 2.0 ** (-5.0 - h) for h in range(4)]
    P.dma("sp", ksTz.ap([(1, S)], off=0, p0=64, np_=64), eall_d.ap()[0:64, :], writes=[ksTz.dep])
    P.dma("sp", ksTz.ap([(1, S)], off=S, p0=0, np_=64), eall_d.ap()[0:64, :], writes=[ksTz.dep])
    cnt["wp"] = 0
    cnt["pT"] = 0
    cnt["sc"] = 0
    cnt["qa"] = 0
    cnt["wq"] = 0

    def mm(out, lhsT, rhs, start, stop, reads, writes):
        P.op("pe", lambda e: e.matmul(out=out, lhsT=lhsT, rhs=rhs, start=start, stop=stop), reads, writes)

    def dve(name, reads, writes, **kw):
        P.op("dve", lambda e: getattr(e, name)(**kw), reads, writes)

    def act(reads, writes, **kw):
        P.op("act", lambda e: e.activation(**kw), reads, writes)

    def load_wp(src_ap, key, i):
        r = wp[cnt["wp"] % 3]
        cnt["wp"] += 1
        wload(r, 2048, src_ap, key, i)
        return r

    def transposes_to(dst_fn, src, nblk, dst_dep):
        b = 6 + (cnt["psT"] % 2)
        cnt["psT"] += 1
        for k in range(nblk):
            P.op("pe", lambda e, k=k: e.transpose(out=psb(b, [(1, 128)], off=k * 128),
                                                  in_=src.ap([(1, 128)], off=k * 128),
                                                  identity=ident.ap([(1, 128)])),
                 reads=[src.dep, ident.dep], writes=[ps[b].dep])
        dst_fn(psb(b, [(128, nblk), (1, 128)]), ps[b].dep)

    def actcopy(reads, writes, out, in_):
        P.op("dve", lambda e: e.tensor_copy(out=out, in_=in_), reads, writes)

    def mixer(i):
        P.dma("sp", ropeR.ap([(1, 2 * T)]), ropeR_d.ap()[i], writes=[ropeR.dep])
        P.dma("sp", ropeN.ap([(1, 2 * T)]), ropeN_d.ap()[i], writes=[ropeN.dep])
        P.dma("sp", zt.ap([(1, 1024)]), zt_d.ap()[i], writes=[zt.dep])
        P.dma("sp", cmask.ap([(1, 512)]), cmask_d.ap()[i], writes=[cmask.dep])
        P.dma("sp", addc.ap([(1, 256)]), addc_d.ap()[i], writes=[addc.dep])
        norm_T(1)
        CUT = int(os.environ.get('K_CUT', '99'))
        if CUT <= 1:
            return
        if i > 0:
            dve("tensor_copy", [kvc.dep], [kvc.dep], out=kvc.ap([(528, 2), (1, 16)]),
                in_=kvc.ap([(528, 2), (1, 16)], off=512))

        def cmp_gen():
            for g in range(2):
                for kv in range(2):
                    dve("tensor_tensor", [kvc.dep, pe.dep], [blk.dep],
                        out=blk.ap([(32, 32), (1, 32)], off=(kv * 2 + g) * 1024, p0=kv * 64, np_=64),
                        in0=kvc.ap([(1, 32), (16, 32)], off=g * 528, p0=kv * 64, np_=64),
                        in1=pe.ap([(1, 32), (0, 32)], off=g * 32, p0=kv * 64, np_=64), op=ALU.add)
            yield
            first = True
            for pcw in range(4):
                r = load_wp(w1s_d.ap()[pcw], "w1s%d" % pcw, i)
                for kv in range(2):
                    for g in range(2):
                        for hc in range(2):
                            idx = (kv * 2 + g) * 2 + hc
                            for l8 in range(8):
                                l = pcw * 8 + l8
                                mm(ps[6].ap([(1, 32)], off=idx * 32),
                                   r.ap([(1, 128)], off=l8 * 256 + hc * 128),
                                   blk.ap([(1, 32)], off=(kv * 2 + g) * 1024 + l * 32),
                                   first, (pcw == 3 and idx == 7 and l8 == 7), [r.dep, blk.dep], [ps[6].dep])
                                first = False
                yield
            for kv in range(2):
                for hc in range(2):
                    o_ = kv * 128 + hc * 32
                    dve("tensor_scalar", [ps[6].dep, b1.dep], [xh.dep], out=xh.ap([(64, 2), (1, 32)], off=o_),
                        in0=ps[6].ap([(64, 2), (1, 32)], off=o_), scalar1=b1.ap([(1, 1)], off=kv * 2 + hc),
                        scalar2=None, op0=ALU.add)
            dve("tensor_tensor", [xh.dep], [gt.dep], out=gt.ap([(1, 256)]), in0=xh.ap([(1, 256)]),
                in1=xh.ap([(1, 256)]), op=ALU.mult)
            dve("tensor_scalar", [gt.dep], [gt.dep], out=gt.ap([(1, 256)]), in0=gt.ap([(1, 256)]),
                scalar1=0.044715, scalar2=1.0, op0=ALU.mult, op1=ALU.add)
            dve("tensor_tensor", [gt.dep, xh.dep], [gt.dep], out=gt.ap([(1, 256)]), in0=gt.ap([(1, 256)]),
                in1=xh.ap([(1, 256)]), op=ALU.mult)
            act([gt.dep], [gt.dep], out=gt.ap([(1, 256)]), in_=gt.ap([(1, 256)]), func=AF.Sigmoid,
                scale=1.5957691216057308)
            dve("tensor_tensor", [gt.dep, xh.dep], [hT.dep], out=hT.ap([(1, 256)]), in0=gt.ap([(1, 256)]),
                in1=xh.ap([(1, 256)]), op=ALU.mult)
            yield
            for g in range(2):
                for hc in range(2):
                    mm(ps[7].ap([(1, 32)], off=g * 32), w2k.ap([(1, 128)], off=hc * 128),
                       hT.ap([(1, 32)], off=g * 64 + hc * 32), (g == 0 and hc == 0), (g == 1 and hc == 1),
                       [w2k.dep, hT.dep], [ps[7].dep])
            for g in range(2):
                for hc in range(2):
                    mm(ps[7].ap([(1, 64)], off=64 + g * 64, np_=32), hT.ap([(1, 32)], off=128 + g * 64 + hc * 32),
                       w2v.ap([(1, 64)], off=hc * 64), (g == 0 and hc == 0), (g == 1 and hc == 1),
                       [w2v.dep, hT.dep], [ps[7].dep])
            for g in range(2):
                actcopy([ps[7].dep], [kcmpT.dep], kcmpT.ap([(1, 32)], off=g * 256 + 32 * i, p0=g * 64, np_=64),
                        ps[7].ap([(1, 32)], off=g * 32, p0=g * 64, np_=64))
            actcopy([ps[7].dep], [vstage.dep], vstage.ap([(1, 128)], np_=32), ps[7].ap([(1, 128)], off=64, np_=32))
            P.dma("sp", vcmp.ap([(65, 2), (1, 64)], off=(i // 4) * 130, p0=32 * (i % 4), np_=32),
                  vstage.ap([(64, 2), (1, 64)], np_=32), reads=[vstage.dep], writes=[vcmp.dep])


        cg = [None]

        def cadv():
            if cg[0] is not None:
                next(cg[0], None)

        def rope_evac(bx, by, tab, dst_ap, dst_dep, split=None):
            dve("tensor_tensor", [ps[bx].dep, tab.dep], [tA.dep], out=tA.ap([(1, 512)]),
                in0=ps[bx].ap([(1, 512)]), in1=tab.ap([(1, 512)]), op=ALU.mult)
            dve("tensor_tensor", [ps[by].dep, tab.dep], [tB.dep], out=tB.ap([(1, 512)]),
                in0=ps[by].ap([(1, 512)]), in1=tab.ap([(1, 512)], off=512), op=ALU.mult)
            if split is None:
                dve("tensor_tensor", [tA.dep, tB.dep], [dst_dep], out=dst_ap,
                    in0=tA.ap([(1, 512)]), in1=tB.ap([(1, 512)]), op=ALU.add)
            else:
                tz, o_lo, o_hi = split
                for p0_, o__ in ((0, o_lo), (64, o_hi)):
                    dve("tensor_tensor", [tA.dep, tB.dep], [tz.dep], out=tz.ap([(1, 512)], off=o__, p0=p0_, np_=64),
                        in0=tA.ap([(1, 512)], p0=p0_, np_=64), in1=tB.ap([(1, 512)], p0=p0_, np_=64), op=ALU.add)

        for pc in (8, 0, 1, 2, 3, 4, 5, 6, 7, 9, 10):
            r = load_wp(win_d.ap()[pc], "win%d" % pc, i)
            bx, by = (0, 1) if pc % 2 == 0 else (2, 3)
            for gi, b in ((0, bx), (1, by)):
                for kc in range(8):
                    mm(ps[b].ap([(1, 512)]), r.ap([(1, 128)], off=kc * 256 + gi * 128),
                       hnT.ap([(1, T)], off=kc * T), kc == 0, kc == 7, [r.dep, hnT.dep], [ps[b].dep])
            if pc < 2:
                rope_evac(bx, by, ropeR, rqT.ap([(1, 512)], off=pc * 512), rqT.dep)
            elif pc < 4:
                rope_evac(bx, by, ropeR, None, None, split=(rkTz, (2 * (pc - 2)) * 512, (2 * (pc - 2) + 1) * 512))
            elif pc < 8:
                k = pc - 4
                actcopy([ps[bx].dep], [qraw.dep], qraw.ap([(1, 512)], off=k * 512), ps[bx].ap([(1, 512)]))
                rope_evac(bx, by, ropeN, qrope.ap([(1, 512)], off=k * 512), qrope.dep)
            elif pc == 8:
                for gi, b in ((0, bx), (1, by)):
                    actcopy([ps[b].dep], [kvc.dep], kvc.ap([(1, 512)], off=gi * 528 + 16), ps[b].ap([(1, 512)]))
                cg[0] = cmp_gen()
            elif pc == 9:
                rope_evac(bx, by, ropeN, None, None, split=(ksTz, i * 512, S + i * 512))
            else:
                rope_evac(bx, by, ropeN, None, None, split=(kwTz, (i % 2) * 512, 1024 + (i % 2) * 512))
            if pc != 8:
                cadv()

        if CUT <= 2:
            return
        for tp in range(6):
            cadv()
            r = load_wp(win_d.ap()[11 + tp], "win%d" % (11 + tp), i)
            for j in range(4):
                b = 4 + (j % 2)
                for kc in range(8):
                    mm(ps[b].ap([(1, 256)]), hnT.ap([(1, 128)], off=kc * T + j * 128),
                       r.ap([(1, 256)], off=kc * 256), kc == 0, kc == 7, [hnT.dep, r.dep], [ps[b].dep])
                if tp < 2:
                    actcopy([ps[b].dep], [rv.dep], rv.ap([(1, 256)], off=j * 512 + tp * 256), ps[b].ap([(1, 256)]))
                elif tp < 4:
                    act([ps[b].dep], [tC.dep], out=tC.ap([(1, 256)]), in_=ps[b].ap([(1, 256)]), func=AF.Silu)
                    dve("tensor_tensor", [tC.dep, retw.dep], [g2.dep],
                        out=g2.ap([(1, 256)], off=j * 512 + (tp - 2) * 256), in0=tC.ap([(1, 256)]),
                        in1=retw.ap([(1, 256)], off=(tp - 2) * 256), op=ALU.mult)
                elif tp == 4:
                    actcopy([ps[b].dep], [vs_all.dep], vs_all.ap([(65, 2), (1, 64)], off=(4 * i + j) * 130),
                            ps[b].ap([(64, 2), (1, 64)]))
                    actcopy([ps[b].dep], [vw_r.dep], vw_r.ap([(65, 2), (1, 64)], off=((i % 2) * 4 + j) * 130),
                            ps[b].ap([(64, 2), (1, 64)], off=128))
                else:
                    x1 = ps[b].ap([(128, 2), (64, 2), (1, 32)])
                    x2 = ps[b].ap([(128, 2), (64, 2), (1, 32)], off=32)
                    cz = zt.ap([(64, 2), (32, 2), (1, 32)], off=j * 128)
                    sz = zt.ap([(64, 2), (32, 2), (1, 32)], off=512 + j * 128)
                    t1 = tA.ap([(64, 2), (32, 2), (1, 32)])
                    t2 = tB.ap([(64, 2), (32, 2), (1, 32)])
                    t3 = tA.ap([(64, 2), (32, 2), (1, 32)], off=128)
                    t4 = tB.ap([(64, 2), (32, 2), (1, 32)], off=128)
                    dve("tensor_tensor", [ps[b].dep, zt.dep], [tA.dep], out=t1, in0=x1, in1=cz, op=ALU.mult)
                    dve("tensor_tensor", [ps[b].dep, zt.dep], [tB.dep], out=t2, in0=x2, in1=sz, op=ALU.mult)
                    dve("tensor_tensor", [ps[b].dep, zt.dep], [tA.dep], out=t3, in0=x2, in1=cz, op=ALU.mult)
                    dve("tensor_tensor", [ps[b].dep, zt.dep], [tB.dep], out=t4, in0=x1, in1=sz, op=ALU.mult)
                    dve("tensor_tensor", [tA.dep, tB.dep], [rkzp.dep],
                        out=rkzp.ap([(256, 2), (192, 2), (1, 32)], off=j * 512), in0=t1, in1=t2, op=ALU.subtract)
                    dve("tensor_tensor", [tA.dep, tB.dep], [rkzp.dep],
                        out=rkzp.ap([(256, 2), (192, 2), (1, 32)], off=j * 512 + 32), in0=t3, in1=t4, op=ALU.add)
        if cg[0] is not None:
            for _ in cg[0]:
                pass
        for j in range(4):
            b = 4 + (j % 2)
            for kc in range(8):
                mm(ps[b].ap([(1, 24)]), hnT.ap([(1, 128)], off=kc * T + j * 128),
                   wng.ap([(1, 24)], off=kc * 24), kc == 0, kc == 7, [hnT.dep, wng.dep], [ps[b].dep])
            act([ps[b].dep], [gates.dep], out=gates.ap([(1, 24)], off=j * 24), in_=ps[b].ap([(1, 24)]),
                func=AF.Sigmoid)

        if CUT <= 3:
            return
        def ret_gen():
            dve("tensor_tensor", [rqT.dep, xi.dep], [rqx.dep], out=rqx.ap([(512, 2), (128, 4), (1, 128)]),
                in0=rqT.ap([(512, 2), (128, 4), (1, 128)]), in1=xi.ap([(128, 2), (0, 4), (1, 128)]), op=ALU.mult)
            for j in range(4):
                for h in range(4):
                    o_ = (h // 2) * 512 + j * 128
                    mm(ps[6].ap([(1, 128)], off=h * 128), rkTz.ap([(1, 128)], off=h * 512 + j * 128),
                       rqT.ap([(1, 128)], off=o_), h == 0, h == 3, [rkTz.dep, rqT.dep], [ps[6].dep])
                dve("tensor_tensor", [ps[6].dep, dmat.dep], [inner_bf.dep], out=inner_bf.ap([(1, 512)]),
                    in0=ps[6].ap([(1, 512)]), in1=dmat.ap([(1, 512)]), op=ALU.mult)
                yield
                for h in range(4):
                    o_ = (h // 2) * 512 + j * 128
                    mm(ps[7].ap([(1, 128)], off=h * 128), inner_bf.ap([(1, 128)], off=h * 128),
                       rv.ap([(1, 128)], off=j * 512 + h * 128), h == 0, False, [inner_bf.dep, rv.dep], [ps[7].dep])
                    mm(ps[7].ap([(1, 128)], off=h * 128), rqx.ap([(1, 128)], off=o_),
                       state_bf.ap([(1, 128)], off=h * 128), False, h == 3,
                       [rqx.dep, state_bf.dep], [ps[7].dep])
                for h in range(4):
                    mm(ps[6].ap([(1, 128)], off=h * 128), rkzp.ap([(1, 128)], off=j * 512 + h * 128),
                       rv.ap([(1, 128)], off=j * 512 + h * 128), h == 0, h == 3, [rkzp.dep, rv.dep], [ps[6].dep])
                for h in range(4):
                    sa = state.ap([(1, 128)], off=h * 128)
                    dve("scalar_tensor_tensor", [state.dep, ps[6].dep], [state.dep], out=sa, in0=sa,
                        scalar=float(GAM[h] ** 128), in1=ps[6].ap([(1, 128)], off=h * 128),
                        op0=ALU.mult, op1=ALU.add)
                dve("tensor_copy", [state.dep], [state_bf.dep], out=state_bf.ap([(1, 512)]), in_=state.ap([(1, 512)]))
                for h in range(4):
                    dve("bn_stats", [ps[7].dep], [sm2.dep], out=sm2.ap([(1, 6)], off=h * 6),
                        in_=ps[7].ap([(1, 128)], off=h * 128))
                for h in range(4):
                    dve("bn_aggr", [sm2.dep], [sm2.dep], out=sm2.ap([(1, 2)], off=32 + h * 2), in_=sm2.ap([(1, 6)], off=h * 6))
                act([sm2.dep], [sm2.dep], out=sm2.ap([(1, 4)], off=48), in_=sm2.ap([(2, 4)], off=33), func=AF.Sqrt,
                    bias=GN_EPS, scale=1.0)
                dve("reciprocal", [sm2.dep], [sm2.dep], out=sm2.ap([(1, 4)], off=52), in_=sm2.ap([(1, 4)], off=48))
                for h in range(4):
                    dve("tensor_scalar", [ps[7].dep, sm2.dep], [tC.dep], out=tC.ap([(1, 128)], off=h * 128),
                        in0=ps[7].ap([(1, 128)], off=h * 128), scalar1=sm2.ap([(1, 1)], off=32 + 2 * h),
                        scalar2=sm2.ap([(1, 1)], off=52 + h), op0=ALU.subtract, op1=ALU.mult)
                dve("tensor_tensor", [tC.dep, g2.dep], [retg.dep], out=retg.ap([(1, 512)]), in0=tC.ap([(1, 512)]),
                    in1=g2.ap([(1, 512)], off=j * 512), op=ALU.mult)
                yield
                transposes_to(lambda pap, pdep, j=j: actcopy([pdep], [catT.dep],
                                                             catT.ap([(T, 4), (1, 128)], off=j * 128), pap),
                              retg, 4, catT.dep)


        rgen = ret_gen()

        n_iter = 0
        for qs_ in range(4):
            Q_ = 4 * i + qs_
            n_iter += 2 * ((i // 4 + 1) + (Q_ + 1) + (Q_ - max(0, Q_ - 4) + 1))
        ret_every = max(1, n_iter // 20)
        adv_cnt = [0]

        def adv():
            next(rgen, None)

        if stage == "ret":
            for _ in rgen:
                pass
            return

        def group_gen(qs, g, accb, acc_off):
            Q = 4 * i + qs

            def acc_ap():
                return accb.ap([(64, 4), (1, 64)], off=acc_off)

            p0 = g * 64

            def score_tile(pre, kT_ap, kdep, qsrc):
                b = cnt["sc"] % 2
                cnt["sc"] += 1
                first = True
                for (l_ap, r_ap, deps) in pre:
                    mm(ps[b].ap([(1, 512)]), l_ap, r_ap, first, False, deps, [ps[b].dep])
                    first = False
                if qsrc is qraw or qsrc is qrope:
                    q_ap = qsrc.ap([(512, 4), (1, 128)], off=qs * 128)
                else:
                    q_ap = qsrc.ap([(1, 512)])
                mm(ps[b].ap([(1, 512)]), kT_ap, q_ap, first, True, [kdep, qsrc.dep], [ps[b].dep])
                pt = pT[cnt["pT"] % 3]
                cnt["pT"] += 1
                act([ps[b].dep], [pt.dep], out=pt.ap([(1, 512)]), in_=ps[b].ap([(1, 512)]), func=AF.Exp,
                    scale=0.125)
                return pt

            def finish_branch(bank, gi, dst_ap, dst_dep, add_to):
                dve("tensor_scalar", [ps[bank].dep], [sm.dep], out=sm.ap([(1, 4)], off=64),
                    in0=ps[bank].ap([(65, 4)], off=64), scalar1=1e-20, scalar2=None, op0=ALU.max)
                dve("reciprocal", [sm.dep], [sm.dep], out=sm.ap([(1, 4)], off=68), in_=sm.ap([(1, 4)], off=64))
                dve("tensor_tensor", [sm.dep, gates.dep], [sm.dep], out=sm.ap([(1, 4)], off=72),
                    in0=sm.ap([(1, 4)], off=68), in1=gates.ap([(3, 4)], off=qs * 24 + g * 12 + gi), op=ALU.mult)
                if not add_to:
                    dve("tensor_tensor", [ps[bank].dep, sm.dep], [dst_dep], out=dst_ap,
                        in0=ps[bank].ap([(65, 4), (1, 64)]), in1=sm.ap([(1, 4), (0, 64)], off=72), op=ALU.mult)
                else:
                    dve("tensor_tensor", [ps[bank].dep, sm.dep], [tB.dep], out=tB.ap([(64, 4), (1, 64)]),
                        in0=ps[bank].ap([(65, 4), (1, 64)]), in1=sm.ap([(1, 4), (0, 64)], off=72), op=ALU.mult)
                    dve("tensor_tensor", [tB.dep, accb.dep], [dst_dep], out=dst_ap,
                        in0=tB.ap([(64, 4), (1, 64)]), in1=acc_ap(), op=ALU.add)

            def attn_loop(tiles, pre_fn, kT_fn, kdep, qsrc, pv_fn):
                pend = []
                for jk in tiles:
                    pt = score_tile(pre_fn(jk), kT_fn(jk), kdep, qsrc)
                    pend.append((jk, pt))
                    if len(pend) > 2:
                        pv_fn(*pend.pop(0))
                    adv()
                while pend:
                    pv_fn(*pend.pop(0))

            ncts = i // 4 + 1

            def cmp_pre(ct):
                if ct == i // 4:
                    return [(ident.ap([(1, 128)]), cmask.ap([(0, 4), (1, 128)], off=qs * 128),
                             [ident.dep, cmask.dep])]
                return [(ident.ap([(1, 128)]), masks.ap([(0, 4), (1, 128)], off=256), [ident.dep, masks.dep])]

            def cmp_pv(ct, pt):
                for k in range(4):
                    mm(ps[2].ap([(1, 65)], off=k * 65), pt.ap([(1, 128)], off=k * 128),
                       vcmp.ap([(1, 65)], off=ct * 130 + g * 65), (ct == 0 and k == 0),
                       (ct == ncts - 1 and k == 3), [pt.dep, vcmp.dep], [ps[2].dep])
                for k in range(4):
                    mm(ps[3].ap([(1, 64)], off=k * 64), pt.ap([(1, 128)], off=k * 128),
                       mcs.ap([(1, 64)], off=ct * 64), (ct == 0 and k == 0),
                       (ct == ncts - 1 and k == 3), [pt.dep, mcs.dep], [ps[3].dep])

            attn_loop(range(ncts), cmp_pre, lambda ct: kcmpT.ap([(1, 128)], off=g * 256 + ct * 128),
                      kcmpT.dep, qraw, cmp_pv)
            finish_branch(2, 0, acc_ap(), accb.dep, False)
            dve("tensor_tensor", [ps[3].dep, sm.dep], [tA.dep], out=tA.ap([(64, 4), (1, 64)]),
                in0=ps[3].ap([(64, 4), (1, 64)]), in1=sm.ap([(1, 4), (0, 64)], off=68), op=ALU.mult)
            dve("tensor_reduce", [tA.dep], [imp.dep], out=imp.ap([(1, 64)]), in_=tA.ap([(1, 64), (64, 4)]),
                axis=AX.X, op=ALU.add)
            dve("tensor_tensor", [imp.dep, addc.dep], [imp.dep], out=imp.ap([(1, 64)], off=64),
                in0=imp.ap([(1, 64)]), in1=addc.ap([(1, 64)], off=qs * 64), op=ALU.add)
            dve("max", [imp.dep], [sm.dep], out=sm.ap([(1, 8)], off=80), in_=imp.ap([(1, 64)], off=64))
            dve("match_replace", [imp.dep, sm.dep], [imp.dep], out=imp.ap([(1, 64)], off=128),
                in_to_replace=sm.ap([(1, 8)], off=80), in_values=imp.ap([(1, 64)], off=64), imm_value=-3e38)
            dve("max", [imp.dep], [sm.dep], out=sm.ap([(1, 8)], off=88), in_=imp.ap([(1, 64)], off=128))
            dve("tensor_scalar", [imp.dep, sm.dep], [selb.dep], out=selb.ap([(1, 64)], off=64 * (1 - g)),
                in0=imp.ap([(1, 64)], off=64), scalar1=sm.ap([(1, 1)], off=95), scalar2=-30000.0,
                op0=ALU.is_lt, op1=ALU.mult)
            yield
            j0 = max(0, Q - 4)

            def win_pre(jk):
                if jk == Q:
                    return [(ident.ap([(1, 128)]), masks.ap([(0, 4), (1, 128)], off=0), [ident.dep, masks.dep])]
                if jk == Q - 4:
                    return [(ident.ap([(1, 128)]), masks.ap([(0, 4), (1, 128)], off=128),
                             [ident.dep, masks.dep])]
                return []

            def win_pv(jk, pt):
                slot = (jk // 4) % 2
                for k in range(4):
                    mm(ps[5].ap([(1, 65)], off=k * 65), pt.ap([(1, 128)], off=k * 128),
                       vw_r.ap([(1, 65)], off=(slot * 4 + jk % 4) * 130 + g * 65), (jk == j0 and k == 0),
                       (jk == Q and k == 3), [pt.dep, vw_r.dep], [ps[5].dep])

            attn_loop(range(j0, Q + 1), win_pre,
                      lambda jk: kwTz.ap([(1, 128)], off=g * 1024 + ((jk // 4) % 2) * 512 + (jk % 4) * 128),
                      kwTz.dep, qrope, win_pv)
            bT = 6 + (cnt["psT"] % 2)
            cnt["psT"] += 1
            P.op("pe", lambda e, bT=bT: e.transpose(out=psb(bT, [(1, 128)]), in_=selb.ap([(1, 128)]),
                                                    identity=ident.ap([(1, 128)])),
                 reads=[selb.dep, ident.dep], writes=[ps[bT].dep])
            qa = qaug[cnt["qa"] % 2]
            cnt["qa"] += 1
            r0 = 64 * (1 - g)
            dve("tensor_copy", [ps[bT].dep], [qa.dep], out=qa.ap([(128, 4), (1, 128)], p0=r0, np_=64),
                in_=psb(bT, [(0, 4), (1, 128)], p0=r0, np_=64))
            dve("tensor_copy", [qrope.dep], [qa.dep], out=qa.ap([(128, 4), (1, 128)], p0=p0, np_=64),
                in_=qrope.ap([(512, 4), (1, 128)], off=qs * 128, p0=p0, np_=64))

            yield
            def sel_pre(jk):
                pre = []
                if jk == Q:
                    pre.append((ident.ap([(1, 128)]), masks.ap([(0, 4), (1, 128)], off=0),
                                [ident.dep, masks.dep]))
                return pre

            def sel_pv(jk, pt):
                for k in range(4):
                    mm(ps[4].ap([(1, 65)], off=k * 65), pt.ap([(1, 128)], off=k * 128),
                       vs_all.ap([(1, 65)], off=jk * 130 + g * 65), (jk == 0 and k == 0),
                       (jk == Q and k == 3), [pt.dep, vs_all.dep], [ps[4].dep])

            attn_loop(range(Q + 1), sel_pre, lambda jk: ksTz.ap([(1, 128)], off=g * S + jk * 128),
                      ksTz.dep, qa, sel_pv)
            finish_branch(5, 2, acc_ap(), accb.dep, True)
            finish_branch(4, 1, nsa_tok.ap([(64, 4), (1, 64)], off=g * 256), nsa_tok.dep, True)


        groups = [(qs, g) for qs in range(4) for g in range(2)]
        gens = [group_gen(qs, g, (xh, tA)[n_ % 2], (0, 256)[n_ % 2]) for n_, (qs, g) in enumerate(groups)]
        next(gens[0])
        for n_, (qs, g) in enumerate(groups):
            next(gens[n_])
            if n_ + 1 < len(groups):
                next(gens[n_ + 1])
            for _ in gens[n_]:
                pass
            if g == 1:
                transposes_to(lambda pap, pdep, qs=qs: actcopy([pdep], [catT.dep],
                                                               catT.ap([(T, 4), (1, 128)], off=4 * T + qs * 128), pap),
                              nsa_tok, 4, catT.dep)
        for _ in rgen:
            pass

    def outproj(i):
        for pc in range(4):
            r = load_wp(wout_d.ap()[pc], "wout%d" % pc, i)
            for j in range(4):
                b = 4 + (j % 2)
                for kc in range(8):
                    mm(ps[b].ap([(1, 256)]), catT.ap([(1, 128)], off=kc * T + j * 128),
                       r.ap([(1, 256)], off=kc * 256), kc == 0, kc == 7, [catT.dep, r.dep], [ps[b].dep])
                xa = xt.ap([(1, 256)], off=j * D + pc * 256)
                dve("tensor_tensor", [ps[b].dep, xt.d[j]], [xt.d[j]], out=xa, in0=ps[b].ap([(1, 256)]), in1=xa,
                    op=ALU.add)

    for i in range(ntiles):
        for j in range(4):
            P.dma("sp", xt.ap([(1, D)], off=j * D), x_d.ap()[i * T + j * 128:i * T + (j + 1) * 128, :],
                  writes=[xt.d[j]])
        norm_T(0)
        ffn(0, i)
        if stage == "ffn1":
            P.dma("sp", out_d.ap()[i * T:(i + 1) * T, :].rearrange("(j p) d -> p j d", p=128),
                  xt.ap([(D, 4), (1, D)]), reads=xt.d, final=True)
            continue
        mixer(i)
        if stage in ("ret", "mix") and os.environ.get("K_DBG", "1") == "1":
            P.dma("pool", dbg_d.ap()[i], catT.ap([(1, 8 * T)]), reads=[catT.dep], final=True)
        if stage == "ret":
            continue
        outproj(i)
        if stage == "mix":
            P.dma("sp", out_d.ap()[i * T:(i + 1) * T, :].rearrange("(j p) d -> p j d", p=128),
                  xt.ap([(D, 4), (1, D)]), reads=xt.d, final=True)
            continue
        norm_T(2)
        ffn(1, i)
        final_norm_store(i)

    P.emit()
    return nc


IN_W = (256, 256, 512, 512, 512, 128, 128, 128, 128, 128, 128, 24)


def win_cols():
    off = np.concatenate([[0], np.cumsum(IN_W)])
    o_rq, o_rk, o_rv, o_rg, o_nq, o_kc, o_vc, o_ks, o_vs, o_kw, o_vw, o_ng = [int(v) for v in off[:12]]

    def pl(base, h):
        return list(range(base + h * 64, base + h * 64 + 64))

    def ret_sw(base, h):
        b = base + h * 64
        return list(range(b + 32, b + 64)) + list(range(b, b + 32))

    def nsa_sw(base, h):
        b = base + h * 64
        return list(range(b + 8, b + 16)) + list(range(b, b + 8)) + list(range(b + 16, b + 64))

    F = []
    for base in (o_rq, o_rk):
        for pr in ((0, 1), (2, 3)):
            F += pl(base, pr[0]) + pl(base, pr[1])
            F += ret_sw(base, pr[0]) + ret_sw(base, pr[1])
    for k in range(4):
        F += pl(o_nq, k) + pl(o_nq, 4 + k)
        F += nsa_sw(o_nq, k) + nsa_sw(o_nq, 4 + k)
    F += pl(o_kc, 0) + pl(o_vc, 0)
    F += pl(o_kc, 1) + pl(o_vc, 1)
    F += pl(o_ks, 0) + pl(o_ks, 1)
    F += nsa_sw(o_ks, 0) + nsa_sw(o_ks, 1)
    F += pl(o_kw, 0) + pl(o_kw, 1)
    F += nsa_sw(o_kw, 0) + nsa_sw(o_kw, 1)
    Tm = (list(range(o_rv, o_rv + 512)) + list(range(o_rg, o_rg + 512)) + list(range(o_vs, o_vs + 128))
          + list(range(o_vw, o_vw + 128)) + list(range(o_rk, o_rk + 256)))
    ng = list(range(o_ng, o_ng + 24))
    return np.array(F + Tm), np.array(ng)


_CONST_CACHE = {}


def const_tables():
    if _CONST_CACHE:
        return _CONST_CACHE
    bf = ml_dtypes.bfloat16
    c = {}
    t = np.arange(S, dtype=np.float32)
    invR = (np.float32(10000.0) ** (-2.0 * np.arange(32, dtype=np.float32) / 64)).astype(np.float32)
    angR = (t[:, None] * invR[None, :]).astype(np.float32)
    cosR, sinR = np.cos(angR), np.sin(angR)
    d = np.arange(128) % 64
    cR = cosR[:, d % 32].T
    sR = np.where((d < 32)[:, None], -sinR[:, d % 32].T, sinR[:, d % 32].T)
    invN = (np.float32(500000.0) ** (-2.0 * np.arange(8, dtype=np.float32) / 16)).astype(np.float32)
    angN = (t[:, None] * invN[None, :]).astype(np.float32)
    cosN, sinN = np.cos(angN), np.sin(angN)
    cN = np.where((d < 16)[:, None], cosN[:, d % 8].T, 1.0)
    sN = np.where((d < 8)[:, None], -sinN[:, d % 8].T, np.where((d < 16)[:, None], sinN[:, d % 8].T, 0.0))

    def tiles2(a, b):
        a = a.reshape(128, NT, T).transpose(1, 0, 2)
        b = b.reshape(128, NT, T).transpose(1, 0, 2)
        return np.ascontiguousarray(np.concatenate([a, b], axis=2))
    c["ropeR"] = tiles2(cR, sR).astype(bf)
    c["ropeN"] = tiles2(cN, sN).astype(bf)
    gam = 1.0 - 2.0 ** (-5.0 - np.arange(4, dtype=np.float64))
    m = np.arange(S) % 128
    zeta = gam[None, :] ** (127.0 - m[:, None]) / 8.0
    cz = cosR[:, None, :] * zeta[:, :, None]
    sz = sinR[:, None, :] * zeta[:, :, None]

    def ztile(a):
        return a.reshape(NT, 4, 128, 4, 32).transpose(0, 2, 1, 3, 4).reshape(NT, 128, 512)
    c["zt"] = np.ascontiguousarray(np.concatenate([ztile(cz), ztile(sz)], axis=2)).astype(np.float32)
    mm_, nn_ = np.arange(128)[:, None], np.arange(128)[None, :]
    dm = np.stack([np.where(nn_ >= mm_, gam[h] ** np.maximum(nn_ - mm_, 0), 0.0) / 8.0 for h in range(4)], axis=1)
    c["dmat"] = np.ascontiguousarray(dm.reshape(128, 512)).astype(np.float32)
    xi = np.zeros((128, 2, 128))
    for p in range(128):
        for c2 in range(2):
            xi[p, c2] = gam[2 * c2 + p // 64] ** (np.arange(128) + 1.0)
    c["xi"] = xi.reshape(128, 256).astype(np.float32)
    e = np.zeros((128, S), np.float32)
    e[np.arange(S) // 64, np.arange(S)] = 1.0
    c["eall"] = e.astype(bf)
    NEG = -30000.0
    kk, qq = np.arange(128)[:, None], np.arange(128)[None, :]
    causal = np.where(kk <= qq, 0.0, NEG)
    anti = np.where(kk > qq, 0.0, NEG)
    cc0 = np.where(kk == 0, NEG, 0.0) + 0.0 * qq
    c["masks"] = np.concatenate([causal, anti, cc0], axis=1).astype(bf)
    cm = np.zeros((NT, 128, 4, 128), np.float32)
    for i in range(NT):
        ct = i // 4
        cidx = ct * 128 + np.arange(128) - 1
        for qs in range(4):
            tq = 512 * i + 128 * qs + np.arange(128)
            valid = (cidx[:, None] >= 0) & (16 * cidx[:, None] + 31 <= tq[None, :])
            cm[i, :, qs, :] = np.where(valid, 0.0, NEG)
    c["cmask"] = cm.reshape(NT, 128, 512).astype(bf)
    ad = np.zeros((NT, 128, 4, 64), np.float32)
    sblk = np.arange(64)[None, :]
    for i in range(NT):
        for qs in range(4):
            tq = 512 * i + 128 * qs + np.arange(128)
            cur = (tq // 64)[:, None]
            forced = (sblk == 0) | (sblk == cur) | (sblk == cur - 1)
            ad[i, :, qs, :] = np.where(sblk > cur, -1e30, np.where(forced, 1e4, 0.0))
    c["addc"] = ad.reshape(NT, 128, 256)
    n_cmp, n_sel = 255, 64
    c_start = np.arange(n_cmp) * 16
    s_start = np.arange(n_sel) * 64
    ov = (np.minimum(c_start[:, None] + 32, s_start[None, :] + 64) - np.maximum(c_start[:, None], s_start[None, :]))
    mcs = np.clip(ov, 0, None) / 32.0
    mc = np.zeros((256, 64), np.float32)
    mc[1:] = mcs
    c["mcs"] = np.ascontiguousarray(mc.reshape(2, 128, 64).transpose(1, 0, 2).reshape(128, 128)).astype(bf)
    _CONST_CACHE.update(c)
    return c


def host_mixer_inputs(inp, m):
    g = lambda k: np.asarray(inp[k], dtype=np.float32)
    fcols, ngcols = win_cols()
    w_in = g("w_in")[0]
    m["win"] = lay_cols(np.ascontiguousarray(w_in[:, fcols]), 256)
    wn = np.ascontiguousarray(w_in[:, ngcols])
    m["wng"] = np.ascontiguousarray(wn.reshape(8, 128, 24).transpose(1, 0, 2)).reshape(128, 192)
    m["wout"] = lay_cols(g("w_out")[0], 256)
    w1k = g("cmp_k_w1")[0].reshape(32, 64, 256).transpose(1, 0, 2)
    w1v = g("cmp_v_w1")[0].reshape(32, 64, 256).transpose(1, 0, 2)
    w1 = np.concatenate([w1k, w1v], axis=0)
    m["w1s"] = np.ascontiguousarray(w1.reshape(128, 4, 8 * 256).transpose(1, 0, 2))
    w2k = g("cmp_k_w2")[0].reshape(2, 128, 64).transpose(1, 0, 2)
    m["w2k"] = np.ascontiguousarray(np.concatenate([w2k, w2k], axis=2)).reshape(128, 256)
    m["w2v"] = np.ascontiguousarray(g("cmp_v_w2")[0].reshape(2, 128, 64).transpose(1, 0, 2)).reshape(128, 128)
    b1k = g("cmp_k_b1")[0].reshape(2, 128).T
    b1v = g("cmp_v_b1")[0].reshape(2, 128).T
    m["b1"] = np.ascontiguousarray(np.concatenate([b1k, b1v], axis=1))
    pek = g("cmp_pe_k")[0].transpose(2, 1, 0)
    pev = g("cmp_pe_v")[0].transpose(2, 1, 0)
    m["pe"] = np.ascontiguousarray(np.concatenate([pek, pev], axis=0)).reshape(128, 64)
    m["retw"] = np.ascontiguousarray(np.broadcast_to(g("ret_norm_w")[0][None, :], (128, 512)))
    m.update(const_tables())
    return m


def host_inputs(inp, b):
    g = lambda k: np.asarray(inp[k], dtype=np.float32)
    m = {}
    m["x"] = np.ascontiguousarray(g("x")[b])
    nwa = np.stack([lay_normw(g("ffn1_norm_w")[0]), lay_normw(g("mix_norm_w")[0]),
                    lay_normw(g("ffn2_norm_w")[0]), lay_normw(g("final_norm_w"))], axis=1)
    m["nw"] = np.ascontiguousarray(nwa.reshape(128, 32))
    m["ident"] = np.eye(128, dtype=np.float32).astype(ml_dtypes.bfloat16)
    for i, k in ((1, "ffn1"), (2, "ffn2")):
        m["wg%d" % i] = lay_gateup(g(k + "_w_gate")[0])
        m["wu%d" % i] = lay_gateup(g(k + "_w_up")[0])
        m["wd%d" % i] = lay_down(g(k + "_w_down")[0])
    m["wfin"] = np.ascontiguousarray(np.broadcast_to(g("final_norm_w")[None, :], (128, D)))
    host_mixer_inputs(inp, m)
    return m


def kernel(**inputs):
    nc = build()
    in_maps = [host_inputs(inputs, b) for b in range(8)]
    res = run_bass_kernel_spmd(nc, in_maps, core_ids=list(range(8)))
    out = np.stack([np.asarray(r["out"]).reshape(S, D) for r in res.results], axis=0)
    return out.astype(np.float32)
```

```python
import os
import numpy as np
import ml_dtypes
import concourse.bass as bass
import concourse.mybir as mybir
from concourse.bass_utils import run_bass_kernel_spmd

F32 = mybir.dt.float32
BF16 = mybir.dt.bfloat16
AF = mybir.ActivationFunctionType
ALU = mybir.AluOpType
AX = mybir.AxisListType

S = 4096
D = 1024
DFF = 2816
T = 512
NT = S // T
NSEM = 12
RMS_EPS = 1e-6
GN_EPS = 1e-5


class Dep:
    __slots__ = ("name", "writer", "readers")

    def __init__(self, name):
        self.name = name
        self.writer = None
        self.readers = []


class Prog:
    ENG = ["pe", "act", "dve", "pool", "sp"]

    def __init__(self, nc):
        self.nc = nc
        self.ops = {e: [] for e in self.ENG}
        self.seen = {e: {f: -1 for f in self.ENG} for e in self.ENG}
        self.seen_dma = {e: set() for e in self.ENG}
        self.dma_ring = {q: [None] * NSEM for q in ("sp", "pool", "act")}
        self.dma_count = {q: 0 for q in ("sp", "pool", "act")}
        self.final = []

    def _need(self, rec, eng, ev):
        if ev[0] == "c":
            _, f, i = ev
            if f == "pe" and eng == "pe":
                return
            if self.seen[eng][f] >= i:
                return
            self.seen[eng][f] = i
            rec["waits"].append((f, i))
            self.ops[f][i]["signal"] = True
        else:
            key = ev[1:]
            if key in self.seen_dma[eng]:
                return
            self.seen_dma[eng].add(key)
            rec["dma_waits"].append(key)

    def _deps(self, rec, eng, me, reads, writes):
        for r in reads:
            if r.writer is not None:
                self._need(rec, eng, r.writer)
        for w in writes:
            if w.writer is not None:
                self._need(rec, eng, w.writer)
            for ev in w.readers:
                self._need(rec, eng, ev)
        for r in reads:
            if me[0] == "c":
                r.readers = [ev for ev in r.readers if not (ev[0] == "c" and ev[1] == me[1])]
            r.readers.append(me)
        for w in writes:
            w.writer = me
            w.readers = []

    def op(self, eng, fn, reads=(), writes=()):
        idx = len(self.ops[eng])
        rec = dict(fn=fn, waits=[], dma_waits=[], signal=False, dma=None)
        self.ops[eng].append(rec)
        self._deps(rec, eng, ("c", eng, idx), reads, writes)

    def dma(self, q, out, in_, reads=(), writes=(), final=False):
        k = self.dma_count[q]
        self.dma_count[q] += 1
        slot = k % NSEM
        val = 16 * (k // NSEM + 1)
        rec = dict(fn=(lambda e, o=out, i=in_: e.dma_start(out=o, in_=i)),
                   waits=[], dma_waits=[], signal=False, dma=(q, slot, val))
        prev = self.dma_ring[q][slot]
        if prev is not None:
            self._need(rec, q, ("d",) + prev)
        self.dma_ring[q][slot] = (q, slot, val)
        self.ops[q].append(rec)
        self._deps(rec, q, ("d", q, slot, val), reads, writes)
        if final:
            self.final.append((q, slot, val))

    def emit(self):
        nc = self.nc
        sem = {e: nc.alloc_semaphore("s_" + e) for e in self.ENG}
        dsem = {q: [nc.alloc_semaphore("d_%s%d" % (q, i)) for i in range(NSEM)]
                for q in ("sp", "pool")}
        for e in self.ENG:
            c = 0
            for rec in self.ops[e]:
                if rec["signal"]:
                    c += 1
                rec["semval"] = c
        ops = self.ops
        final = self.final

        def run(name, e):
            for rec in ops[name]:
                best = {}
                for (f, i) in rec["waits"]:
                    v = ops[f][i]["semval"]
                    best[f] = max(best.get(f, 0), v)
                for f, v in best.items():
                    e.wait_ge(sem[f], v)
                for (q, slot, val) in rec["dma_waits"]:
                    e.wait_ge(dsem[q][slot], val)
                ins = rec["fn"](e)
                if rec["signal"]:
                    ins.then_inc(sem[name], 1)
                if rec["dma"] is not None:
                    q, slot, val = rec["dma"]
                    ins.then_inc(dsem[q][slot], 16)
            if name == "sp":
                for (q, slot, val) in final:
                    e.wait_ge(dsem[q][slot], val)
                for q in ("sp", "pool"):
                    for slot in range(NSEM):
                        last = self.dma_ring[q][slot]
                        if last is not None:
                            e.wait_ge(dsem[q][slot], last[2])

        with nc.Block() as block:
            @block.tensor
            def _(e):
                run("pe", e)

            @block.scalar
            def _(e):
                run("act", e)

            @block.vector
            def _(e):
                run("dve", e)

            @block.gpsimd
            def _(e):
                run("pool", e)

            @block.sync
            def _(e):
                run("sp", e)


class Tn:
    def __init__(self, h, F, name):
        self.h = h
        self.F = F
        self.dep = Dep(name)

    def ap(self, dims, off=0, p0=0, np_=128):
        return bass.AP(self.h, p0 * self.F + off, [[self.F, np_]] + [[s, n] for (s, n) in dims])


class Ctx:
    def __init__(self, nc):
        self.nc = nc
        self.P = Prog(nc)
        self.dram = {}
        self.n = 0

    def sb(self, name, F, dtype):
        h = self.nc.alloc_sbuf_tensor(name, [128, F], dtype)
        return Tn(h, F, name)

    def din(self, name, shape, dtype=F32):
        h = self.nc.dram_tensor(name, list(shape), dtype, kind="ExternalInput")
        self.dram[name] = h
        return h


def lay_gateup(w):
    return np.ascontiguousarray(w.reshape(8, 128, 22, 128).transpose(2, 1, 0, 3)).reshape(22, 128, 1024)


def lay_down(w):
    a = w.reshape(2, 11, 128, 2, 512).transpose(0, 3, 2, 1, 4)
    return np.ascontiguousarray(a).reshape(4, 128, 11 * 512)


def lay_cols(w, ncols_piece=512):
    C = w.shape[1]
    npc = C // ncols_piece
    a = w.reshape(8, 128, npc, ncols_piece).transpose(2, 1, 0, 3)
    return np.ascontiguousarray(a).reshape(npc, 128, 8 * ncols_piece)


def lay_normw(w):
    return np.ascontiguousarray(w.reshape(8, 128).T)


def build(ntiles=NT, stage="full"):
    nc = bass.Bass("TRN2", target_bir_lowering=False)
    C = Ctx(nc)
    P = C.P

    x_d = C.din("x", [S, D])
    out_d = nc.dram_tensor("out", [S, D], F32, kind="ExternalOutput")
    nw_d = C.din("nw", [128, 32])
    ident_d = C.din("ident", [128, 128], BF16)
    wg_d = [C.din("wg%d" % i, [22, 128, 1024]) for i in (1, 2)]
    wu_d = [C.din("wu%d" % i, [22, 128, 1024]) for i in (1, 2)]
    wd_d = [C.din("wd%d" % i, [4, 128, 11 * 512]) for i in (1, 2)]
    wfin_d = C.din("wfin", [128, D])
    dbg_d = nc.dram_tensor("dbg", [NT, 128, 8 * T], F32, kind="ExternalOutput") if stage in ("ret", "mix") else None

    xt = C.sb("xt", 4 * D, F32)
    xt.d = [Dep("xt%d" % j) for j in range(4)]
    hnT = C.sb("hnT", 8 * T, BF16)
    xs = [C.sb("xs0", D, BF16)] * 2
    actT = C.sb("actT", 11 * T, BF16)
    wgr = [C.sb("wgr%d" % i, 1024, BF16) for i in range(2)]
    wur = [C.sb("wur%d" % i, 1024, BF16) for i in range(2)]
    wdr = [C.sb("wdr%d" % i, 11 * 512, BF16) for i in range(2)]
    sg = [C.sb("sg%d" % i, T, BF16) for i in range(2)]
    nw = C.sb("nw_sb", 32, F32)
    ident = C.sb("ident_sb", 128, BF16)
    wfin = C.sb("wfin_sb", D, F32)
    st = C.sb("stats", 64, F32)

    ps = []
    for i in range(8):
        h = nc.alloc_psum_tensor("ps%d" % i, [128, 512], F32)
        t_ = Tn(h, 512, "ps%d" % i)
        t_.hb = h.bitcast(BF16)
        ps.append(t_)

    def psb(i, dims, off=0, p0=0, np_=128):
        return bass.AP(ps[i].hb, p0 * 1024 + off, [[1024, np_]] + [[s, n] for (s, n) in dims])

    P.dma("sp", nw.ap([(1, 32)]), nw_d.ap(), writes=[nw.dep])
    P.dma("sp", ident.ap([(1, 128)]), ident_d.ap(), writes=[ident.dep])
    P.dma("sp", wfin.ap([(1, D)]), wfin_d.ap(), writes=[wfin.dep])

    cnt = {"w": 0, "wd": 0, "sg": 0, "psT": 0, "wq": 0}
    scr = {}

    def wload(ring_t, n_el, src_ap, key, i):
        if key not in scr:
            h = nc.dram_tensor("scr_" + key, [128, n_el], BF16, kind="Internal")
            scr[key] = (h, Dep("scr_" + key))
        h, dep = scr[key]
        if i == 0:
            P.dma("pool", ring_t.ap([(1, n_el)]), src_ap, writes=[ring_t.dep])
            if ntiles > 1:
                P.dma("sp", h.ap(), ring_t.ap([(1, n_el)]), reads=[ring_t.dep], writes=[dep])
        else:
            q_ = "sp"
            cnt["wq"] += 1
            P.dma(q_, ring_t.ap([(1, n_el)]), h.ap(), reads=[dep], writes=[ring_t.dep])

    st.d = [Dep("st%d" % j) for j in range(4)]

    def rms_stats(j):
        P.op("act", lambda e, j=j: e.activation(out=junk.ap([(1, D)]), in_=xt.ap([(1, D)], off=j * D),
                                                func=AF.Square, accum_out=st.ap([(1, 1)], off=j)),
             reads=[xt.d[j]], writes=[junk.dep, st.d[j]])
        P.op("act", lambda e, j=j: e.activation(out=st.ap([(1, 1)], off=8 + j), in_=st.ap([(1, 1)], off=j),
                                                func=AF.Sqrt, scale=1.0 / D, bias=RMS_EPS),
             reads=[st.d[j]], writes=[st.d[j]])
        P.op("dve", lambda e, j=j: e.reciprocal(out=st.ap([(1, 1)], off=16 + j), in_=st.ap([(1, 1)], off=8 + j)),
             reads=[st.d[j]], writes=[st.d[j]])

    def norm_T(norm_idx):
        for j in range(4):
            rms_stats(j)
            x2 = xs[cnt["psT"] % 2]
            P.op("dve", lambda e, j=j, x2=x2: e.tensor_scalar(out=x2.ap([(1, D)]), in0=xt.ap([(1, D)], off=j * D),
                                                       scalar1=st.ap([(1, 1)], off=16 + j), scalar2=None,
                                                       op0=ALU.mult),
                 reads=[xt.d[j], st.d[j]], writes=[x2.dep])
            b = 6 + (cnt["psT"] % 2)
            cnt["psT"] += 1
            for kc in range(8):
                P.op("pe", lambda e, kc=kc, b=b, x2=x2: e.transpose(out=psb(b, [(1, 128)], off=kc * 128),
                                                             in_=x2.ap([(1, 128)], off=kc * 128),
                                                             identity=ident.ap([(1, 128)])),
                     reads=[x2.dep, ident.dep], writes=[ps[b].dep])
            P.op("dve", lambda e, j=j, b=b: e.tensor_tensor(
                out=hnT.ap([(T, 8), (1, 128)], off=j * 128),
                in0=psb(b, [(128, 8), (1, 128)]),
                in1=nw.ap([(1, 8), (0, 128)], off=norm_idx * 8),
                op=ALU.mult),
                reads=[ps[b].dep, nw.dep], writes=[hnT.dep])

    def ffn(fi, i):
        for dh in range(2):
            for cc in range(11):
                c = dh * 11 + cc
                r = cnt["w"] % 2
                cnt["w"] += 1
                wload(wgr[r], 1024, wg_d[fi].ap()[c], "wg%d_%d" % (fi, c), i)
                wload(wur[r], 1024, wu_d[fi].ap()[c], "wu%d_%d" % (fi, c), i)
                bg, bu = (0, 2) if c % 2 == 0 else (1, 3)
                for kc in range(8):
                    P.op("pe", lambda e, kc=kc, r=r, bg=bg: e.matmul(
                        out=ps[bg].ap([(1, 512)]), lhsT=wgr[r].ap([(1, 128)], off=kc * 128),
                        rhs=hnT.ap([(1, T)], off=kc * T), start=(kc == 0), stop=(kc == 7)),
                        reads=[wgr[r].dep, hnT.dep], writes=[ps[bg].dep])
                for kc in range(8):
                    P.op("pe", lambda e, kc=kc, r=r, bu=bu: e.matmul(
                        out=ps[bu].ap([(1, 512)]), lhsT=wur[r].ap([(1, 128)], off=kc * 128),
                        rhs=hnT.ap([(1, T)], off=kc * T), start=(kc == 0), stop=(kc == 7)),
                        reads=[wur[r].dep, hnT.dep], writes=[ps[bu].dep])
                s = cnt["sg"] % 2
                cnt["sg"] += 1
                P.op("act", lambda e, s=s, bg=bg: e.activation(out=sg[s].ap([(1, T)]), in_=ps[bg].ap([(1, 512)]),
                                                               func=AF.Silu),
                     reads=[ps[bg].dep], writes=[sg[s].dep])
                P.op("dve", lambda e, s=s, bu=bu, cc=cc: e.tensor_tensor(
                    out=actT.ap([(1, T)], off=cc * T), in0=ps[bu].ap([(1, 512)]), in1=sg[s].ap([(1, T)]),
                    op=ALU.mult),
                    reads=[ps[bu].dep, sg[s].dep], writes=[actT.dep])
            rs = []
            for ch in range(2):
                r = cnt["wd"] % 2
                cnt["wd"] += 1
                wload(wdr[r], 11 * 512, wd_d[fi].ap()[dh * 2 + ch], "wd%d_%d" % (fi, dh * 2 + ch), i)
                rs.append(r)
            order = ([(ch, j) for ch in range(2) for j in range(4)] if dh == 0
                     else [(ch, j) for j in range(4) for ch in range(2)])
            for n_, (ch, j) in enumerate(order):
                r = rs[ch]
                b = 4 + (n_ % 2)
                for f in range(11):
                    P.op("pe", lambda e, f=f, j=j, r=r, b=b: e.matmul(
                        out=ps[b].ap([(1, 512)]), lhsT=actT.ap([(1, 128)], off=f * T + j * 128),
                        rhs=wdr[r].ap([(1, 512)], off=f * 512), start=(f == 0), stop=(f == 10)),
                        reads=[actT.dep, wdr[r].dep], writes=[ps[b].dep])
                P.op("dve", lambda e, j=j, ch=ch, b=b: e.scalar_tensor_tensor(
                    out=xt.ap([(1, 512)], off=j * D + ch * 512), in0=ps[b].ap([(1, 512)]), scalar=0.5,
                    in1=xt.ap([(1, 512)], off=j * D + ch * 512), op0=ALU.mult, op1=ALU.add),
                    reads=[ps[b].dep, xt.d[j]], writes=[xt.d[j]])

    def final_norm_store(i):
        for j in range(4):
            rms_stats(j)
            P.op("dve", lambda e, j=j: e.scalar_tensor_tensor(
                out=xt.ap([(1, D)], off=j * D), in0=xt.ap([(1, D)], off=j * D),
                scalar=st.ap([(1, 1)], off=16 + j), in1=wfin.ap([(1, D)]), op0=ALU.mult, op1=ALU.mult),
                reads=[xt.d[j], st.d[j], wfin.dep], writes=[xt.d[j]])
            P.dma("sp", out_d.ap()[i * T + j * 128:i * T + (j + 1) * 128, :], xt.ap([(1, D)], off=j * D),
                  reads=[xt.d[j]], final=True)

    win_d = C.din("win", [17, 128, 8 * 256])
    wng_d = C.din("wng", [128, 8 * 24])
    wout_d = C.din("wout", [4, 128, 8 * 256])
    w1s_d = C.din("w1s", [4, 128, 8 * 256])
    w2k_d = C.din("w2k", [128, 256])
    w2v_d = C.din("w2v", [128, 128])
    b1_d = C.din("b1", [128, 4])
    pe_d = C.din("pe", [128, 64])
    retw_d = C.din("retw", [128, 512])
    ropeR_d = C.din("ropeR", [NT, 128, 2 * T], BF16)
    ropeN_d = C.din("ropeN", [NT, 128, 2 * T], BF16)
    zt_d = C.din("zt", [NT, 128, 1024])
    dmat_d = C.din("dmat", [128, 512])
    xi_d = C.din("xi", [128, 256])
    eall_d = C.din("eall", [128, S], BF16)
    masks_d = C.din("masks", [128, 384], BF16)
    cmask_d = C.din("cmask", [NT, 128, 512], BF16)
    addc_d = C.din("addc", [NT, 128, 256])
    mcs_d = C.din("mcs", [128, 128], BF16)

    wp = [C.sb("wp%d" % i, 8 * 256, BF16) for i in range(3)]
    wng = C.sb("wng_sb", 8 * 24, BF16)
    w2k = C.sb("w2k_sb", 256, BF16)
    w2v = C.sb("w2v_sb", 128, BF16)
    b1 = C.sb("b1_sb", 4, F32)
    pe = C.sb("pe_sb", 64, F32)
    retw = C.sb("retw_sb", 512, F32)
    ropeR = C.sb("ropeR_sb", 2 * T, BF16)
    ropeN = C.sb("ropeN_sb", 2 * T, BF16)
    zt = C.sb("zt_sb", 1024, F32)
    dmat = C.sb("dmat_sb", 512, F32)
    xi = C.sb("xi_sb", 256, F32)
    masks = C.sb("masks_sb", 384, BF16)
    cmask = C.sb("cmask_sb", 512, BF16)
    addc = C.sb("addc_sb", 256, F32)
    mcs = C.sb("mcs_sb", 128, BF16)
    rqT = C.sb("rqT", 2 * T, BF16)
    rkTz = C.sb("rkTz", 4 * T, BF16)
    rqx = C.sb("rqx", 2 * T, BF16)
    qraw = C.sb("qraw", 4 * T, BF16)
    qrope = C.sb("qrope", 4 * T, BF16)
    kvc = C.sb("kvc", 2 * 528, BF16)
    ksTz = C.sb("ksTz", 2 * S, BF16)
    kwTz = C.sb("kwTz", 4 * T, BF16)
    vs_all = C.sb("vs_all", 32 * 130, BF16)
    vw_r = C.sb("vw_r", 8 * 130, BF16)
    rv = C.sb("rv", 4 * 512, BF16)
    g2 = C.sb("g2", 4 * 512, BF16)
    rkzp = C.sb("rkzp", 4 * 512, BF16)
    gates = C.sb("gates", 4 * 24, F32)
    catT = C.sb("catT", 8 * T, BF16)
    tA = C.sb("tA", 512, F32)
    tB = C.sb("tB", 512, F32)
    tC = C.sb("tC", 512, F32)
    junk = Tn(tC.h.bitcast(BF16), 1024, "junk")
    junk.dep = tC.dep
    state = C.sb("state", 512, F32)
    state_bf = C.sb("state_bf", 512, BF16)
    inner_bf = C.sb("inner_bf", 512, BF16)
    retg = C.sb("retg", 512, BF16)
    sm = C.sb("sm", 128, F32)
    blk = C.sb("blk", 4 * 1024, BF16)
    xh = C.sb("xh", 256, F32)
    gt = C.sb("gt", 256, F32)
    hT = C.sb("hT", 256, BF16)
    kcmpT = C.sb("kcmpT", 512, BF16)
    vcmp = C.sb("vcmp", 2 * 130, BF16)
    vstage = C.sb("vstage", 128, BF16)
    pT = [C.sb("pT%d" % i, 512, BF16) for i in range(2)]
    pT.append(xs[0])
    imp = C.sb("imp", 256, F32)
    selb = C.sb("selb", 128, BF16)
    qaug = [C.sb("qaug%d" % i_, 512, BF16) for i_ in range(2)]
    nsa_tok = Tn(gt.h.bitcast(BF16), 512, "nsa_tok")
    nsa_tok.dep = gt.dep
    acc = xh
    sm2 = C.sb("sm2", 64, F32)

    for (t_, d_, n_) in ((wng, wng_d, 192), (w2k, w2k_d, 256), (w2v, w2v_d, 128)):
        P.dma("pool", t_.ap([(1, n_)]), d_.ap(), writes=[t_.dep])
    for (t_, d_, n_) in ((b1, b1_d, 4), (pe, pe_d, 64), (retw, retw_d, 512), (dmat, dmat_d, 512),
                         (xi, xi_d, 256), (masks, masks_d, 384), (mcs, mcs_d, 128)):
        P.dma("sp", t_.ap([(1, n_)]), d_.ap(), writes=[t_.dep])
    P.op("dve", lambda e: e.memset(kcmpT.ap([(1, 512)]), 0.0), writes=[kcmpT.dep])
    for t_, n_ in ((rkTz, 4 * T), (kwTz, 4 * T), (rkzp, 2048), (blk, 4096), (selb, 128)):
        P.op("dve", lambda e, t_=t_, n_=n_: e.memset(t_.ap([(1, n_)]), 0.0), writes=[t_.dep])
    P.op("dve", lambda e: e.memset(vcmp.ap([(1, 260)]), 0.0), writes=[vcmp.dep])
    P.op("dve", lambda e: e.memset(vcmp.ap([(65, 4), (1, 1)], off=64), 1.0), writes=[vcmp.dep])
    P.op("dve", lambda e: e.memset(vs_all.ap([(65, 64), (1, 1)], off=64), 1.0), writes=[vs_all.dep])
    P.op("dve", lambda e: e.memset(vw_r.ap([(65, 16), (1, 1)], off=64), 1.0), writes=[vw_r.dep])
    P.op("dve", lambda e: e.memset(kvc.ap([(1, 2 * 528)]), 0.0), writes=[kvc.dep])
    P.op("dve", lambda e: e.memset(state.ap([(1, 512)]), 0.0), writes=[state.dep])
    P.op("dve", lambda e: e.memset(state_bf.ap([(1, 512)]), 0.0), writes=[state_bf.dep])
    if stage in ("ret", "mix"):
        P.op("dve", lambda e: e.memset(catT.ap([(1, 8 * T)]), 0.0), writes=[catT.dep])
    GAM = [1.0 - 2.0 ** (-5.0 - h) for h in range(4)]
    P.dma("sp", ksTz.ap([(1, S)], off=0, p0=64, np_=64), eall_d.ap()[0:64, :], writes=[ksTz.dep])
    P.dma("sp", ksTz.ap([(1, S)], off=S, p0=0, np_=64), eall_d.ap()[0:64, :], writes=[ksTz.dep])
    cnt["wp"] = 0
    cnt["pT"] = 0
    cnt["sc"] = 0
    cnt["qa"] = 0
    cnt["wq"] = 0

    def mm(out, lhsT, rhs, start, stop, reads, writes):
        P.op("pe", lambda e: e.matmul(out=out, lhsT=lhsT, rhs=rhs, start=start, stop=stop), reads, writes)

    def dve(name, reads, writes, **kw):
        P.op("dve", lambda e: getattr(e, name)(**kw), reads, writes)

    def act(reads, writes, **kw):
        P.op("act", lambda e: e.activation(**kw), reads, writes)

    def load_wp(src_ap, key, i):
        r = wp[cnt["wp"] % 3]
        cnt["wp"] += 1
        wload(r, 2048, src_ap, key, i)
        return r

    def transposes_to(dst_fn, src, nblk, dst_dep):
        b = 6 + (cnt["psT"] % 2)
        cnt["psT"] += 1
        for k in range(nblk):
            P.op("pe", lambda e, k=k: e.transpose(out=psb(b, [(1, 128)], off=k * 128),
                                                  in_=src.ap([(1, 128)], off=k * 128),
                                                  identity=ident.ap([(1, 128)])),
                 reads=[src.dep, ident.dep], writes=[ps[b].dep])
        dst_fn(psb(b, [(128, nblk), (1, 128)]), ps[b].dep)

    def actcopy(reads, writes, out, in_):
        P.op("dve", lambda e: e.tensor_copy(out=out, in_=in_), reads, writes)

    def mixer(i):
        P.dma("sp", ropeR.ap([(1, 2 * T)]), ropeR_d.ap()[i], writes=[ropeR.dep])
        P.dma("sp", ropeN.ap([(1, 2 * T)]), ropeN_d.ap()[i], writes=[ropeN.dep])
        P.dma("sp", zt.ap([(1, 1024)]), zt_d.ap()[i], writes=[zt.dep])
        P.dma("sp", cmask.ap([(1, 512)]), cmask_d.ap()[i], writes=[cmask.dep])
        P.dma("sp", addc.ap([(1, 256)]), addc_d.ap()[i], writes=[addc.dep])
        norm_T(1)
        CUT = int(os.environ.get('K_CUT', '99'))
        if CUT <= 1:
            return
        if i > 0:
            dve("tensor_copy", [kvc.dep], [kvc.dep], out=kvc.ap([(528, 2), (1, 16)]),
                in_=kvc.ap([(528, 2), (1, 16)], off=512))

        def cmp_gen():
            for g in range(2):
                for kv in range(2):
                    dve("tensor_tensor", [kvc.dep, pe.dep], [blk.dep],
                        out=blk.ap([(32, 32), (1, 32)], off=(kv * 2 + g) * 1024, p0=kv * 64, np_=64),
                        in0=kvc.ap([(1, 32), (16, 32)], off=g * 528, p0=kv * 64, np_=64),
                        in1=pe.ap([(1, 32), (0, 32)], off=g * 32, p0=kv * 64, np_=64), op=ALU.add)
            yield
            first = True
            for pcw in range(4):
                r = load_wp(w1s_d.ap()[pcw], "w1s%d" % pcw, i)
                for kv in range(2):
                    for g in range(2):
                        for hc in range(2):
                            idx = (kv * 2 + g) * 2 + hc
                            for l8 in range(8):
                                l = pcw * 8 + l8
                                mm(ps[6].ap([(1, 32)], off=idx * 32),
                                   r.ap([(1, 128)], off=l8 * 256 + hc * 128),
                                   blk.ap([(1, 32)], off=(kv * 2 + g) * 1024 + l * 32),
                                   first, (pcw == 3 and idx == 7 and l8 == 7), [r.dep, blk.dep], [ps[6].dep])
                                first = False
                yield
            for kv in range(2):
                for hc in range(2):
                    o_ = kv * 128 + hc * 32
                    dve("tensor_scalar", [ps[6].dep, b1.dep], [xh.dep], out=xh.ap([(64, 2), (1, 32)], off=o_),
                        in0=ps[6].ap([(64, 2), (1, 32)], off=o_), scalar1=b1.ap([(1, 1)], off=kv * 2 + hc),
                        scalar2=None, op0=ALU.add)
            dve("tensor_tensor", [xh.dep], [gt.dep], out=gt.ap([(1, 256)]), in0=xh.ap([(1, 256)]),
                in1=xh.ap([(1, 256)]), op=ALU.mult)
            dve("tensor_scalar", [gt.dep], [gt.dep], out=gt.ap([(1, 256)]), in0=gt.ap([(1, 256)]),
                scalar1=0.044715, scalar2=1.0, op0=ALU.mult, op1=ALU.add)
            dve("tensor_tensor", [gt.dep, xh.dep], [gt.dep], out=gt.ap([(1, 256)]), in0=gt.ap([(1, 256)]),
                in1=xh.ap([(1, 256)]), op=ALU.mult)
            act([gt.dep], [gt.dep], out=gt.ap([(1, 256)]), in_=gt.ap([(1, 256)]), func=AF.Sigmoid,
                scale=1.5957691216057308)
            dve("tensor_tensor", [gt.dep, xh.dep], [hT.dep], out=hT.ap([(1, 256)]), in0=gt.ap([(1, 256)]),
                in1=xh.ap([(1, 256)]), op=ALU.mult)
            yield
            for g in range(2):
                for hc in range(2):
                    mm(ps[7].ap([(1, 32)], off=g * 32), w2k.ap([(1, 128)], off=hc * 128),
                       hT.ap([(1, 32)], off=g * 64 + hc * 32), (g == 0 and hc == 0), (g == 1 and hc == 1),
                       [w2k.dep, hT.dep], [ps[7].dep])
            for g in range(2):
                for hc in range(2):
                    mm(ps[7].ap([(1, 64)], off=64 + g * 64, np_=32), hT.ap([(1, 32)], off=128 + g * 64 + hc * 32),
                       w2v.ap([(1, 64)], off=hc * 64), (g == 0 and hc == 0), (g == 1 and hc == 1),
                       [w2v.dep, hT.dep], [ps[7].dep])
            for g in range(2):
                actcopy([ps[7].dep], [kcmpT.dep], kcmpT.ap([(1, 32)], off=g * 256 + 32 * i, p0=g * 64, np_=64),
                        ps[7].ap([(1, 32)], off=g * 32, p0=g * 64, np_=64))
            actcopy([ps[7].dep], [vstage.dep], vstage.ap([(1, 128)], np_=32), ps[7].ap([(1, 128)], off=64, np_=32))
            P.dma("sp", vcmp.ap([(65, 2), (1, 64)], off=(i // 4) * 130, p0=32 * (i % 4), np_=32),
                  vstage.ap([(64, 2), (1, 64)], np_=32), reads=[vstage.dep], writes=[vcmp.dep])


        cg = [None]

        def cadv():
            if cg[0] is not None:
                next(cg[0], None)

        def rope_evac(bx, by, tab, dst_ap, dst_dep, split=None):
            dve("tensor_tensor", [ps[bx].dep, tab.dep], [tA.dep], out=tA.ap([(1, 512)]),
                in0=ps[bx].ap([(1, 512)]), in1=tab.ap([(1, 512)]), op=ALU.mult)
            dve("tensor_tensor", [ps[by].dep, tab.dep], [tB.dep], out=tB.ap([(1, 512)]),
                in0=ps[by].ap([(1, 512)]), in1=tab.ap([(1, 512)], off=512), op=ALU.mult)
            if split is None:
                dve("tensor_tensor", [tA.dep, tB.dep], [dst_dep], out=dst_ap,
                    in0=tA.ap([(1, 512)]), in1=tB.ap([(1, 512)]), op=ALU.add)
            else:
                tz, o_lo, o_hi = split
                for p0_, o__ in ((0, o_lo), (64, o_hi)):
                    dve("tensor_tensor", [tA.dep, tB.dep], [tz.dep], out=tz.ap([(1, 512)], off=o__, p0=p0_, np_=64),
                        in0=tA.ap([(1, 512)], p0=p0_, np_=64), in1=tB.ap([(1, 512)], p0=p0_, np_=64), op=ALU.add)

        for pc in (8, 0, 1, 2, 3, 4, 5, 6, 7, 9, 10):
            r = load_wp(win_d.ap()[pc], "win%d" % pc, i)
            bx, by = (0, 1) if pc % 2 == 0 else (2, 3)
            for gi, b in ((0, bx), (1, by)):
                for kc in range(8):
                    mm(ps[b].ap([(1, 512)]), r.ap([(1, 128)], off=kc * 256 + gi * 128),
                       hnT.ap([(1, T)], off=kc * T), kc == 0, kc == 7, [r.dep, hnT.dep], [ps[b].dep])
            if pc < 2:
                rope_evac(bx, by, ropeR, rqT.ap([(1, 512)], off=pc * 512), rqT.dep)
            elif pc < 4:
                rope_evac(bx, by, ropeR, None, None, split=(rkTz, (2 * (pc - 2)) * 512, (2 * (pc - 2) + 1) * 512))
            elif pc < 8:
                k = pc - 4
                actcopy([ps[bx].dep], [qraw.dep], qraw.ap([(1, 512)], off=k * 512), ps[bx].ap([(1, 512)]))
                rope_evac(bx, by, ropeN, qrope.ap([(1, 512)], off=k * 512), qrope.dep)
            elif pc == 8:
                for gi, b in ((0, bx), (1, by)):
                    actcopy([ps[b].dep], [kvc.dep], kvc.ap([(1, 512)], off=gi * 528 + 16), ps[b].ap([(1, 512)]))
                cg[0] = cmp_gen()
            elif pc == 9:
                rope_evac(bx, by, ropeN, None, None, split=(ksTz, i * 512, S + i * 512))
            else:
                rope_evac(bx, by, ropeN, None, None, split=(kwTz, (i % 2) * 512, 1024 + (i % 2) * 512))
            if pc != 8:
                cadv()

        if CUT <= 2:
            return
        for tp in range(6):
            cadv()
            r = load_wp(win_d.ap()[11 + tp], "win%d" % (11 + tp), i)
            for j in range(4):
                b = 4 + (j % 2)
                for kc in range(8):
                    mm(ps[b].ap([(1, 256)]), hnT.ap([(1, 128)], off=kc * T + j * 128),
                       r.ap([(1, 256)], off=kc * 256), kc == 0, kc == 7, [hnT.dep, r.dep], [ps[b].dep])
                if tp < 2:
                    actcopy([ps[b].dep], [rv.dep], rv.ap([(1, 256)], off=j * 512 + tp * 256), ps[b].ap([(1, 256)]))
                elif tp < 4:
                    act([ps[b].dep], [tC.dep], out=tC.ap([(1, 256)]), in_=ps[b].ap([(1, 256)]), func=AF.Silu)
                    dve("tensor_tensor", [tC.dep, retw.dep], [g2.dep],
                        out=g2.ap([(1, 256)], off=j * 512 + (tp - 2) * 256), in0=tC.ap([(1, 256)]),
                        in1=retw.ap([(1, 256)], off=(tp - 2) * 256), op=ALU.mult)
                elif tp == 4:
                    actcopy([ps[b].dep], [vs_all.dep], vs_all.ap([(65, 2), (1, 64)], off=(4 * i + j) * 130),
                            ps[b].ap([(64, 2), (1, 64)]))
                    actcopy([ps[b].dep], [vw_r.dep], vw_r.ap([(65, 2), (1, 64)], off=((i % 2) * 4 + j) * 130),
                            ps[b].ap([(64, 2), (1, 64)], off=128))
                else:
                    x1 = ps[b].ap([(128, 2), (64, 2), (1, 32)])
                    x2 = ps[b].ap([(128, 2), (64, 2), (1, 32)], off=32)
                    cz = zt.ap([(64, 2), (32, 2), (1, 32)], off=j * 128)
                    sz = zt.ap([(64, 2), (32, 2), (1, 32)], off=512 + j * 128)
                    t1 = tA.ap([(64, 2), (32, 2), (1, 32)])
                    t2 = tB.ap([(64, 2), (32, 2), (1, 32)])
                    t3 = tA.ap([(64, 2), (32, 2), (1, 32)], off=128)
                    t4 = tB.ap([(64, 2), (32, 2), (1, 32)], off=128)
                    dve("tensor_tensor", [ps[b].dep, zt.dep], [tA.dep], out=t1, in0=x1, in1=cz, op=ALU.mult)
                    dve("tensor_tensor", [ps[b].dep, zt.dep], [tB.dep], out=t2, in0=x2, in1=sz, op=ALU.mult)
                    dve("tensor_tensor", [ps[b].dep, zt.dep], [tA.dep], out=t3, in0=x2, in1=cz, op=ALU.mult)
                    dve("tensor_tensor", [ps[b].dep, zt.dep], [tB.dep], out=t4, in0=x1, in1=sz, op=ALU.mult)
                    dve("tensor_tensor", [tA.dep, tB.dep], [rkzp.dep],
                        out=rkzp.ap([(256, 2), (192, 2), (1, 32)], off=j * 512), in0=t1, in1=t2, op=ALU.subtract)
                    dve("tensor_tensor", [tA.dep, tB.dep], [rkzp.dep],
                        out=rkzp.ap([(256, 2), (192, 2), (1, 32)], off=j * 512 + 32), in0=t3, in1=t4, op=ALU.add)
        if cg[0] is not None:
            for _ in cg[0]:
                pass
        for j in range(4):
            b = 4 + (j % 2)
            for kc in range(8):
                mm(ps[b].ap([(1, 24)]), hnT.ap([(1, 128)], off=kc * T + j * 128),
                   wng.ap([(1, 24)], off=kc * 24), kc == 0, kc == 7, [hnT.dep, wng.dep], [ps[b].dep])
            act([ps[b].dep], [gates.dep], out=gates.ap([(1, 24)], off=j * 24), in_=ps[b].ap([(1, 24)]),
                func=AF.Sigmoid)

        if CUT <= 3:
            return
        def ret_gen():
            dve("tensor_tensor", [rqT.dep, xi.dep], [rqx.dep], out=rqx.ap([(512, 2), (128, 4), (1, 128)]),
                in0=rqT.ap([(512, 2), (128, 4), (1, 128)]), in1=xi.ap([(128, 2), (0, 4), (1, 128)]), op=ALU.mult)
            for j in range(4):
                for h in range(4):
                    o_ = (h // 2) * 512 + j * 128
                    mm(ps[6].ap([(1, 128)], off=h * 128), rkTz.ap([(1, 128)], off=h * 512 + j * 128),
                       rqT.ap([(1, 128)], off=o_), h == 0, h == 3, [rkTz.dep, rqT.dep], [ps[6].dep])
                dve("tensor_tensor", [ps[6].dep, dmat.dep], [inner_bf.dep], out=inner_bf.ap([(1, 512)]),
                    in0=ps[6].ap([(1, 512)]), in1=dmat.ap([(1, 512)]), op=ALU.mult)
                yield
                for h in range(4):
                    o_ = (h // 2) * 512 + j * 128
                    mm(ps[7].ap([(1, 128)], off=h * 128), inner_bf.ap([(1, 128)], off=h * 128),
                       rv.ap([(1, 128)], off=j * 512 + h * 128), h == 0, False, [inner_bf.dep, rv.dep], [ps[7].dep])
                    mm(ps[7].ap([(1, 128)], off=h * 128), rqx.ap([(1, 128)], off=o_),
                       state_bf.ap([(1, 128)], off=h * 128), False, h == 3,
                       [rqx.dep, state_bf.dep], [ps[7].dep])
                for h in range(4):
                    mm(ps[6].ap([(1, 128)], off=h * 128), rkzp.ap([(1, 128)], off=j * 512 + h * 128),
                       rv.ap([(1, 128)], off=j * 512 + h * 128), h == 0, h == 3, [rkzp.dep, rv.dep], [ps[6].dep])
                for h in range(4):
                    sa = state.ap([(1, 128)], off=h * 128)
                    dve("scalar_tensor_tensor", [state.dep, ps[6].dep], [state.dep], out=sa, in0=sa,
                        scalar=float(GAM[h] ** 128), in1=ps[6].ap([(1, 128)], off=h * 128),
                        op0=ALU.mult, op1=ALU.add)
                dve("tensor_copy", [state.dep], [state_bf.dep], out=state_bf.ap([(1, 512)]), in_=state.ap([(1, 512)]))
                for h in range(4):
                    dve("bn_stats", [ps[7].dep], [sm2.dep], out=sm2.ap([(1, 6)], off=h * 6),
                        in_=ps[7].ap([(1, 128)], off=h * 128))
                for h in range(4):
                    dve("bn_aggr", [sm2.dep], [sm2.dep], out=sm2.ap([(1, 2)], off=32 + h * 2), in_=sm2.ap([(1, 6)], off=h * 6))
                act([sm2.dep], [sm2.dep], out=sm2.ap([(1, 4)], off=48), in_=sm2.ap([(2, 4)], off=33), func=AF.Sqrt,
                    bias=GN_EPS, scale=1.0)
                dve("reciprocal", [sm2.dep], [sm2.dep], out=sm2.ap([(1, 4)], off=52), in_=sm2.ap([(1, 4)], off=48))
                for h in range(4):
                    dve("tensor_scalar", [ps[7].dep, sm2.dep], [tC.dep], out=tC.ap([(1, 128)], off=h * 128),
                        in0=ps[7].ap([(1, 128)], off=h * 128), scalar1=sm2.ap([(1, 1)], off=32 + 2 * h),
                        scalar2=sm2.ap([(1, 1)], off=52 + h), op0=ALU.subtract, op1=ALU.mult)
                dve("tensor_tensor", [tC.dep, g2.dep], [retg.dep], out=retg.ap([(1, 512)]), in0=tC.ap([(1, 512)]),
                    in1=g2.ap([(1, 512)], off=j * 512), op=ALU.mult)
                yield
                transposes_to(lambda pap, pdep, j=j: actcopy([pdep], [catT.dep],
                                                             catT.ap([(T, 4), (1, 128)], off=j * 128), pap),
                              retg, 4, catT.dep)


        rgen = ret_gen()

        n_iter = 0
        for qs_ in range(4):
            Q_ = 4 * i + qs_
            n_iter += 2 * ((i // 4 + 1) + (Q_ + 1) + (Q_ - max(0, Q_ - 4) + 1))
        ret_every = max(1, n_iter // 20)
        adv_cnt = [0]

        def adv():
            adv_cnt[0] += 1
            if adv_cnt[0] % ret_every == 0:
                next(rgen, None)

        if stage == "ret":
            for _ in rgen:
                pass
            return

        def group_gen(qs, g, accb, acc_off):
            Q = 4 * i + qs

            def acc_ap():
                return accb.ap([(64, 4), (1, 64)], off=acc_off)

            p0 = g * 64

            def score_tile(pre, kT_ap, kdep, qsrc):
                b = cnt["sc"] % 2
                cnt["sc"] += 1
                first = True
                for (l_ap, r_ap, deps) in pre:
                    mm(ps[b].ap([(1, 512)]), l_ap, r_ap, first, False, deps, [ps[b].dep])
                    first = False
                if qsrc is qraw or qsrc is qrope:
                    q_ap = qsrc.ap([(512, 4), (1, 128)], off=qs * 128)
                else:
                    q_ap = qsrc.ap([(1, 512)])
                mm(ps[b].ap([(1, 512)]), kT_ap, q_ap, first, True, [kdep, qsrc.dep], [ps[b].dep])
                pt = pT[cnt["pT"] % 3]
                cnt["pT"] += 1
                act([ps[b].dep], [pt.dep], out=pt.ap([(1, 512)]), in_=ps[b].ap([(1, 512)]), func=AF.Exp,
                    scale=0.125)
                return pt

            def finish_branch(bank, gi, dst_ap, dst_dep, add_to):
                dve("tensor_scalar", [ps[bank].dep], [sm.dep], out=sm.ap([(1, 4)], off=64),
                    in0=ps[bank].ap([(65, 4)], off=64), scalar1=1e-20, scalar2=None, op0=ALU.max)
                dve("reciprocal", [sm.dep], [sm.dep], out=sm.ap([(1, 4)], off=68), in_=sm.ap([(1, 4)], off=64))
                dve("tensor_tensor", [sm.dep, gates.dep], [sm.dep], out=sm.ap([(1, 4)], off=72),
                    in0=sm.ap([(1, 4)], off=68), in1=gates.ap([(3, 4)], off=qs * 24 + g * 12 + gi), op=ALU.mult)
                if not add_to:
                    dve("tensor_tensor", [ps[bank].dep, sm.dep], [dst_dep], out=dst_ap,
                        in0=ps[bank].ap([(65, 4), (1, 64)]), in1=sm.ap([(1, 4), (0, 64)], off=72), op=ALU.mult)
                else:
                    dve("tensor_tensor", [ps[bank].dep, sm.dep], [tB.dep], out=tB.ap([(64, 4), (1, 64)]),
                        in0=ps[bank].ap([(65, 4), (1, 64)]), in1=sm.ap([(1, 4), (0, 64)], off=72), op=ALU.mult)
                    dve("tensor_tensor", [tB.dep, accb.dep], [dst_dep], out=dst_ap,
                        in0=tB.ap([(64, 4), (1, 64)]), in1=acc_ap(), op=ALU.add)

            def attn_loop(tiles, pre_fn, kT_fn, kdep, qsrc, pv_fn):
                pend = []
                for jk in tiles:
                    pt = score_tile(pre_fn(jk), kT_fn(jk), kdep, qsrc)
                    pend.append((jk, pt))
                    if len(pend) > 2:
                        pv_fn(*pend.pop(0))
                    adv()
                while pend:
                    pv_fn(*pend.pop(0))

            ncts = i // 4 + 1

            def cmp_pre(ct):
                if ct == i // 4:
                    return [(ident.ap([(1, 128)]), cmask.ap([(0, 4), (1, 128)], off=qs * 128),
                             [ident.dep, cmask.dep])]
                return [(ident.ap([(1, 128)]), masks.ap([(0, 4), (1, 128)], off=256), [ident.dep, masks.dep])]

            def cmp_pv(ct, pt):
                for k in range(4):
                    mm(ps[2].ap([(1, 65)], off=k * 65), pt.ap([(1, 128)], off=k * 128),
                       vcmp.ap([(1, 65)], off=ct * 130 + g * 65), (ct == 0 and k == 0),
                       (ct == ncts - 1 and k == 3), [pt.dep, vcmp.dep], [ps[2].dep])
                for k in range(4):
                    mm(ps[3].ap([(1, 64)], off=k * 64), pt.ap([(1, 128)], off=k * 128),
                       mcs.ap([(1, 64)], off=ct * 64), (ct == 0 and k == 0),
                       (ct == ncts - 1 and k == 3), [pt.dep, mcs.dep], [ps[3].dep])

            attn_loop(range(ncts), cmp_pre, lambda ct: kcmpT.ap([(1, 128)], off=g * 256 + ct * 128),
                      kcmpT.dep, qraw, cmp_pv)
            finish_branch(2, 0, acc_ap(), accb.dep, False)
            dve("tensor_tensor", [ps[3].dep, sm.dep], [tA.dep], out=tA.ap([(64, 4), (1, 64)]),
                in0=ps[3].ap([(64, 4), (1, 64)]), in1=sm.ap([(1, 4), (0, 64)], off=68), op=ALU.mult)
            dve("tensor_reduce", [tA.dep], [imp.dep], out=imp.ap([(1, 64)]), in_=tA.ap([(1, 64), (64, 4)]),
                axis=AX.X, op=ALU.add)
            dve("tensor_tensor", [imp.dep, addc.dep], [imp.dep], out=imp.ap([(1, 64)], off=64),
                in0=imp.ap([(1, 64)]), in1=addc.ap([(1, 64)], off=qs * 64), op=ALU.add)
            dve("max", [imp.dep], [sm.dep], out=sm.ap([(1, 8)], off=80), in_=imp.ap([(1, 64)], off=64))
            dve("match_replace", [imp.dep, sm.dep], [imp.dep], out=imp.ap([(1, 64)], off=128),
                in_to_replace=sm.ap([(1, 8)], off=80), in_values=imp.ap([(1, 64)], off=64), imm_value=-3e38)
            dve("max", [imp.dep], [sm.dep], out=sm.ap([(1, 8)], off=88), in_=imp.ap([(1, 64)], off=128))
            dve("tensor_scalar", [imp.dep, sm.dep], [selb.dep], out=selb.ap([(1, 64)], off=64 * (1 - g)),
                in0=imp.ap([(1, 64)], off=64), scalar1=sm.ap([(1, 1)], off=95), scalar2=-30000.0,
                op0=ALU.is_lt, op1=ALU.mult)
            yield
            j0 = max(0, Q - 4)

            def win_pre(jk):
                if jk == Q:
                    return [(ident.ap([(1, 128)]), masks.ap([(0, 4), (1, 128)], off=0), [ident.dep, masks.dep])]
                if jk == Q - 4:
                    return [(ident.ap([(1, 128)]), masks.ap([(0, 4), (1, 128)], off=128),
                             [ident.dep, masks.dep])]
                return []

            def win_pv(jk, pt):
                slot = (jk // 4) % 2
                for k in range(4):
                    mm(ps[5].ap([(1, 65)], off=k * 65), pt.ap([(1, 128)], off=k * 128),
                       vw_r.ap([(1, 65)], off=(slot * 4 + jk % 4) * 130 + g * 65), (jk == j0 and k == 0),
                       (jk == Q and k == 3), [pt.dep, vw_r.dep], [ps[5].dep])

            attn_loop(range(j0, Q + 1), win_pre,
                      lambda jk: kwTz.ap([(1, 128)], off=g * 1024 + ((jk // 4) % 2) * 512 + (jk % 4) * 128),
                      kwTz.dep, qrope, win_pv)
            bT = 6 + (cnt["psT"] % 2)
            cnt["psT"] += 1
            P.op("pe", lambda e, bT=bT: e.transpose(out=psb(bT, [(1, 128)]), in_=selb.ap([(1, 128)]),
                                                    identity=ident.ap([(1, 128)])),
                 reads=[selb.dep, ident.dep], writes=[ps[bT].dep])
            qa = qaug[cnt["qa"] % 2]
            cnt["qa"] += 1
            r0 = 64 * (1 - g)
            dve("tensor_copy", [ps[bT].dep], [qa.dep], out=qa.ap([(128, 4), (1, 128)], p0=r0, np_=64),
                in_=psb(bT, [(0, 4), (1, 128)], p0=r0, np_=64))
            dve("tensor_copy", [qrope.dep], [qa.dep], out=qa.ap([(128, 4), (1, 128)], p0=p0, np_=64),
                in_=qrope.ap([(512, 4), (1, 128)], off=qs * 128, p0=p0, np_=64))

            yield
            def sel_pre(jk):
                pre = []
                if jk == Q:
                    pre.append((ident.ap([(1, 128)]), masks.ap([(0, 4), (1, 128)], off=0),
                                [ident.dep, masks.dep]))
                return pre

            def sel_pv(jk, pt):
                for k in range(4):
                    mm(ps[4].ap([(1, 65)], off=k * 65), pt.ap([(1, 128)], off=k * 128),
                       vs_all.ap([(1, 65)], off=jk * 130 + g * 65), (jk == 0 and k == 0),
                       (jk == Q and k == 3), [pt.dep, vs_all.dep], [ps[4].dep])

            attn_loop(range(Q + 1), sel_pre, lambda jk: ksTz.ap([(1, 128)], off=g * S + jk * 128),
                      ksTz.dep, qa, sel_pv)
            finish_branch(5, 2, acc_ap(), accb.dep, True)
            finish_branch(4, 1, nsa_tok.ap([(64, 4), (1, 64)], off=g * 256), nsa_tok.dep, True)


        groups = [(qs, g) for qs in range(4) for g in range(2)]
        gens = [group_gen(qs, g, (xh, tA)[n_ % 2], (0, 256)[n_ % 2]) for n_, (qs, g) in enumerate(groups)]
        next(gens[0])
        for n_, (qs, g) in enumerate(groups):
            next(gens[n_])
            if n_ + 1 < len(groups):
                next(gens[n_ + 1])
            for _ in gens[n_]:
                pass
            if g == 1:
                transposes_to(lambda pap, pdep, qs=qs: actcopy([pdep], [catT.dep],
                                                               catT.ap([(T, 4), (1, 128)], off=4 * T + qs * 128), pap),
                              nsa_tok, 4, catT.dep)
        for _ in rgen:
            pass

    def outproj(i):
        for pc in range(4):
            r = load_wp(wout_d.ap()[pc], "wout%d" % pc, i)
            for j in range(4):
                b = 4 + (j % 2)
                for kc in range(8):
                    mm(ps[b].ap([(1, 256)]), catT.ap([(1, 128)], off=kc * T + j * 128),
                       r.ap([(1, 256)], off=kc * 256), kc == 0, kc == 7, [catT.dep, r.dep], [ps[b].dep])
                xa = xt.ap([(1, 256)], off=j * D + pc * 256)
                dve("tensor_tensor", [ps[b].dep, xt.d[j]], [xt.d[j]], out=xa, in0=ps[b].ap([(1, 256)]), in1=xa,
                    op=ALU.add)

    for i in range(ntiles):
        for j in range(4):
            P.dma("sp", xt.ap([(1, D)], off=j * D), x_d.ap()[i * T + j * 128:i * T + (j + 1) * 128, :],
                  writes=[xt.d[j]])
        norm_T(0)
        ffn(0, i)
        if stage == "ffn1":
            P.dma("sp", out_d.ap()[i * T:(i + 1) * T, :].rearrange("(j p) d -> p j d", p=128),
                  xt.ap([(D, 4), (1, D)]), reads=xt.d, final=True)
            continue
        mixer(i)
        if stage in ("ret", "mix") and os.environ.get("K_DBG", "1") == "1":
            P.dma("pool", dbg_d.ap()[i], catT.ap([(1, 8 * T)]), reads=[catT.dep], final=True)
        if stage == "ret":
            continue
        outproj(i)
        if stage == "mix":
            P.dma("sp", out_d.ap()[i * T:(i + 1) * T, :].rearrange("(j p) d -> p j d", p=128),
                  xt.ap([(D, 4), (1, D)]), reads=xt.d, final=True)
            continue
        norm_T(2)
        ffn(1, i)
        final_norm_store(i)

    P.emit()
    return nc


IN_W = (256, 256, 512, 512, 512, 128, 128, 128, 128, 128, 128, 24)


def win_cols():
    off = np.concatenate([[0], np.cumsum(IN_W)])
    o_rq, o_rk, o_rv, o_rg, o_nq, o_kc, o_vc, o_ks, o_vs, o_kw, o_vw, o_ng = [int(v) for v in off[:12]]

    def pl(base, h):
        return list(range(base + h * 64, base + h * 64 + 64))

    def ret_sw(base, h):
        b = base + h * 64
        return list(range(b + 32, b + 64)) + list(range(b, b + 32))

    def nsa_sw(base, h):
        b = base + h * 64
        return list(range(b + 8, b + 16)) + list(range(b, b + 8)) + list(range(b + 16, b + 64))

    F = []
    for base in (o_rq, o_rk):
        for pr in ((0, 1), (2, 3)):
            F += pl(base, pr[0]) + pl(base, pr[1])
            F += ret_sw(base, pr[0]) + ret_sw(base, pr[1])
    for k in range(4):
        F += pl(o_nq, k) + pl(o_nq, 4 + k)
        F += nsa_sw(o_nq, k) + nsa_sw(o_nq, 4 + k)
    F += pl(o_kc, 0) + pl(o_vc, 0)
    F += pl(o_kc, 1) + pl(o_vc, 1)
    F += pl(o_ks, 0) + pl(o_ks, 1)
    F += nsa_sw(o_ks, 0) + nsa_sw(o_ks, 1)
    F += pl(o_kw, 0) + pl(o_kw, 1)
    F += nsa_sw(o_kw, 0) + nsa_sw(o_kw, 1)
    Tm = (list(range(o_rv, o_rv + 512)) + list(range(o_rg, o_rg + 512)) + list(range(o_vs, o_vs + 128))
          + list(range(o_vw, o_vw + 128)) + list(range(o_rk, o_rk + 256)))
    ng = list(range(o_ng, o_ng + 24))
    return np.array(F + Tm), np.array(ng)


_CONST_CACHE = {}


def const_tables():
    if _CONST_CACHE:
        return _CONST_CACHE
    bf = ml_dtypes.bfloat16
    c = {}
    t = np.arange(S, dtype=np.float32)
    invR = (np.float32(10000.0) ** (-2.0 * np.arange(32, dtype=np.float32) / 64)).astype(np.float32)
    angR = (t[:, None] * invR[None, :]).astype(np.float32)
    cosR, sinR = np.cos(angR), np.sin(angR)
    d = np.arange(128) % 64
    cR = cosR[:, d % 32].T
    sR = np.where((d < 32)[:, None], -sinR[:, d % 32].T, sinR[:, d % 32].T)
    invN = (np.float32(500000.0) ** (-2.0 * np.arange(8, dtype=np.float32) / 16)).astype(np.float32)
    angN = (t[:, None] * invN[None, :]).astype(np.float32)
    cosN, sinN = np.cos(angN), np.sin(angN)
    cN = np.where((d < 16)[:, None], cosN[:, d % 8].T, 1.0)
    sN = np.where((d < 8)[:, None], -sinN[:, d % 8].T, np.where((d < 16)[:, None], sinN[:, d % 8].T, 0.0))

    def tiles2(a, b):
        a = a.reshape(128, NT, T).transpose(1, 0, 2)
        b = b.reshape(128, NT, T).transpose(1, 0, 2)
        return np.ascontiguousarray(np.concatenate([a, b], axis=2))
    c["ropeR"] = tiles2(cR, sR).astype(bf)
    c["ropeN"] = tiles2(cN, sN).astype(bf)
    gam = 1.0 - 2.0 ** (-5.0 - np.arange(4, dtype=np.float64))
    m = np.arange(S) % 128
    zeta = gam[None, :] ** (127.0 - m[:, None]) / 8.0
    cz = cosR[:, None, :] * zeta[:, :, None]
    sz = sinR[:, None, :] * zeta[:, :, None]

    def ztile(a):
        return a.reshape(NT, 4, 128, 4, 32).transpose(0, 2, 1, 3, 4).reshape(NT, 128, 512)
    c["zt"] = np.ascontiguousarray(np.concatenate([ztile(cz), ztile(sz)], axis=2)).astype(np.float32)
    mm_, nn_ = np.arange(128)[:, None], np.arange(128)[None, :]
    dm = np.stack([np.where(nn_ >= mm_, gam[h] ** np.maximum(nn_ - mm_, 0), 0.0) / 8.0 for h in range(4)], axis=1)
    c["dmat"] = np.ascontiguousarray(dm.reshape(128, 512)).astype(np.float32)
    xi = np.zeros((128, 2, 128))
    for p in range(128):
        for c2 in range(2):
            xi[p, c2] = gam[2 * c2 + p // 64] ** (np.arange(128) + 1.0)
    c["xi"] = xi.reshape(128, 256).astype(np.float32)
    e = np.zeros((128, S), np.float32)
    e[np.arange(S) // 64, np.arange(S)] = 1.0
    c["eall"] = e.astype(bf)
    NEG = -30000.0
    kk, qq = np.arange(128)[:, None], np.arange(128)[None, :]
    causal = np.where(kk <= qq, 0.0, NEG)
    anti = np.where(kk > qq, 0.0, NEG)
    cc0 = np.where(kk == 0, NEG, 0.0) + 0.0 * qq
    c["masks"] = np.concatenate([causal, anti, cc0], axis=1).astype(bf)
    cm = np.zeros((NT, 128, 4, 128), np.float32)
    for i in range(NT):
        ct = i // 4
        cidx = ct * 128 + np.arange(128) - 1
        for qs in range(4):
            tq = 512 * i + 128 * qs + np.arange(128)
            valid = (cidx[:, None] >= 0) & (16 * cidx[:, None] + 31 <= tq[None, :])
            cm[i, :, qs, :] = np.where(valid, 0.0, NEG)
    c["cmask"] = cm.reshape(NT, 128, 512).astype(bf)
    ad = np.zeros((NT, 128, 4, 64), np.float32)
    sblk = np.arange(64)[None, :]
    for i in range(NT):
        for qs in range(4):
            tq = 512 * i + 128 * qs + np.arange(128)
            cur = (tq // 64)[:, None]
            forced = (sblk == 0) | (sblk == cur) | (sblk == cur - 1)
            ad[i, :, qs, :] = np.where(sblk > cur, -1e30, np.where(forced, 1e4, 0.0))
    c["addc"] = ad.reshape(NT, 128, 256)
    n_cmp, n_sel = 255, 64
    c_start = np.arange(n_cmp) * 16
    s_start = np.arange(n_sel) * 64
    ov = (np.minimum(c_start[:, None] + 32, s_start[None, :] + 64) - np.maximum(c_start[:, None], s_start[None, :]))
    mcs = np.clip(ov, 0, None) / 32.0
    mc = np.zeros((256, 64), np.float32)
    mc[1:] = mcs
    c["mcs"] = np.ascontiguousarray(mc.reshape(2, 128, 64).transpose(1, 0, 2).reshape(128, 128)).astype(bf)
    _CONST_CACHE.update(c)
    return c


def host_mixer_inputs(inp, m):
    g = lambda k: np.asarray(inp[k], dtype=np.float32)
    fcols, ngcols = win_cols()
    w_in = g("w_in")[0]
    m["win"] = lay_cols(np.ascontiguousarray(w_in[:, fcols]), 256)
    wn = np.ascontiguousarray(w_in[:, ngcols])
    m["wng"] = np.ascontiguousarray(wn.reshape(8, 128, 24).transpose(1, 0, 2)).reshape(128, 192)
    m["wout"] = lay_cols(g("w_out")[0], 256)
    w1k = g("cmp_k_w1")[0].reshape(32, 64, 256).transpose(1, 0, 2)
    w1v = g("cmp_v_w1")[0].reshape(32, 64, 256).transpose(1, 0, 2)
    w1 = np.concatenate([w1k, w1v], axis=0)
    m["w1s"] = np.ascontiguousarray(w1.reshape(128, 4, 8 * 256).transpose(1, 0, 2))
    w2k = g("cmp_k_w2")[0].reshape(2, 128, 64).transpose(1, 0, 2)
    m["w2k"] = np.ascontiguousarray(np.concatenate([w2k, w2k], axis=2)).reshape(128, 256)
    m["w2v"] = np.ascontiguousarray(g("cmp_v_w2")[0].reshape(2, 128, 64).transpose(1, 0, 2)).reshape(128, 128)
    b1k = g("cmp_k_b1")[0].reshape(2, 128).T
    b1v = g("cmp_v_b1")[0].reshape(2, 128).T
    m["b1"] = np.ascontiguousarray(np.concatenate([b1k, b1v], axis=1))
    pek = g("cmp_pe_k")[0].transpose(2, 1, 0)
    pev = g("cmp_pe_v")[0].transpose(2, 1, 0)
    m["pe"] = np.ascontiguousarray(np.concatenate([pek, pev], axis=0)).reshape(128, 64)
    m["retw"] = np.ascontiguousarray(np.broadcast_to(g("ret_norm_w")[0][None, :], (128, 512)))
    m.update(const_tables())
    return m


def host_inputs(inp, b):
    g = lambda k: np.asarray(inp[k], dtype=np.float32)
    m = {}
    m["x"] = np.ascontiguousarray(g("x")[b])
    nwa = np.stack([lay_normw(g("ffn1_norm_w")[0]), lay_normw(g("mix_norm_w")[0]),
                    lay_normw(g("ffn2_norm_w")[0]), lay_normw(g("final_norm_w"))], axis=1)
    m["nw"] = np.ascontiguousarray(nwa.reshape(128, 32))
    m["ident"] = np.eye(128, dtype=np.float32).astype(ml_dtypes.bfloat16)
    for i, k in ((1, "ffn1"), (2, "ffn2")):
        m["wg%d" % i] = lay_gateup(g(k + "_w_gate")[0])
        m["wu%d" % i] = lay_gateup(g(k + "_w_up")[0])
        m["wd%d" % i] = lay_down(g(k + "_w_down")[0])
    m["wfin"] = np.ascontiguousarray(np.broadcast_to(g("final_norm_w")[None, :], (128, D)))
    host_mixer_inputs(inp, m)
    return m


def kernel(**inputs):
    nc = build()
    in_maps = [host_inputs(inputs, b) for b in range(8)]
    res = run_bass_kernel_spmd(nc, in_maps, core_ids=list(range(8)))
    out = np.stack([np.asarray(r["out"]).reshape(S, D) for r in res.results], axis=0)
    return out.astype(np.float32)
```
